# Optimizing a Trainium2 kernel written in Bass

```python
import math
import jax, jax.numpy as jnp
from jax import lax
import numpy as np

D_MODEL = 1024
BATCH = 2
SEQ = 8192
DEPTH = 4

N_META = 16
BLOCK = 128
N_PAD = BLOCK - N_META
SB_HEADS = 8
SB_HEAD_DIM = 64
SB_WIDTH = SB_HEADS * SB_HEAD_DIM
SSM_HEADS = 16
SSM_HEAD_DIM = 64
SSM_WIDTH = SSM_HEADS * SSM_HEAD_DIM
SSM_GROUPS = 2
SSM_STATE = 128
CONV_K = 4
CONV_CH = SSM_WIDTH + 2 * SSM_GROUPS * SSM_STATE
MLA_HEADS = 8
Q_LORA = 384
KV_LORA = 256
QK_NOPE = 64
QK_ROPE = 32
V_HEAD = 64
MLA_WIDTH = MLA_HEADS * V_HEAD
ROPE_BASE = 10000.0
MIX_WIDTH = SB_WIDTH + SSM_WIDTH + MLA_WIDTH
D_FF = -(-8 * D_MODEL // (3 * 256)) * 256
IN_SPLITS = (SB_WIDTH, SB_WIDTH, SB_WIDTH, SSM_WIDTH, CONV_CH, SSM_HEADS, Q_LORA, KV_LORA, QK_ROPE)
IN_DIM = 3 * SB_WIDTH + SSM_WIDTH + CONV_CH + SSM_HEADS + Q_LORA + KV_LORA + QK_ROPE
EPS = 1e-6

kernel_name = "hymba_sb_ssd_mla_hybrid"

F32 = jnp.float32


def rmsnorm(x, g):
    xf = x.astype(F32)
    y = xf * lax.rsqrt(jnp.mean(xf * xf, axis=-1, keepdims=True) + EPS)
    return (y * g.astype(F32)).astype(x.dtype)


def rope_tables(length):
    pos = (jnp.arange(length) - N_PAD).astype(F32)
    inv = ROPE_BASE ** (-jnp.arange(0, QK_ROPE, 2, dtype=F32) / QK_ROPE)
    ang = pos[:, None] * inv[None, :]
    return jnp.cos(ang), jnp.sin(ang)


def apply_rope(x, cos, sin):
    xf = x.astype(F32)
    x1, x2 = jnp.split(xf, 2, axis=-1)
    return jnp.concatenate([x1 * cos - x2 * sin, x1 * sin + x2 * cos], axis=-1).astype(x.dtype)


def sweep_query_blocks(block_fn, q):
    b, L, h, d = q.shape
    nb = L // BLOCK
    qb = jnp.moveaxis(q.reshape(b, nb, BLOCK, h, d), 1, 0)
    starts = jnp.arange(nb) * BLOCK
    out = lax.map(lambda a: block_fn(a[0], a[1]), (qb, starts))
    return jnp.moveaxis(out, 0, 1).reshape(b, L, h, out.shape[-1])


def stick_breaking_attention(q, k, v, valid):
    L = k.shape[1]
    scale = 1.0 / math.sqrt(SB_HEAD_DIM)
    kidx = jnp.arange(L)
    kf = k.astype(F32)
    vf = v.astype(F32)

    def block(qb, start):
        qidx = start + jnp.arange(BLOCK)
        mask = (kidx[None, :] < qidx[:, None]) & valid[None, :]
        z = jnp.einsum('bqhd,bkhd->bhqk', qb.astype(F32), kf) * scale
        log_1m = jnp.where(mask, jax.nn.log_sigmoid(-z), 0.0)
        between = lax.cumsum(log_1m, axis=3, reverse=True) - log_1m
        w = jnp.exp(jnp.where(mask, jax.nn.log_sigmoid(z) + between, -jnp.inf))
        return jnp.einsum('bhqk,bkhd->bqhd', w, vf)

    return sweep_query_blocks(block, q).astype(q.dtype)


def mla_attention(q, k, v, valid):
    L = k.shape[1]
    scale = 1.0 / math.sqrt(QK_NOPE + QK_ROPE)
    kidx = jnp.arange(L)
    kf = k.astype(F32)
    vf = v.astype(F32)

    def block(qb, start):
        qidx = start + jnp.arange(BLOCK)
        mask = (kidx[None, :] <= qidx[:, None]) & valid[None, :]
        s = jnp.einsum('bqhd,bkhd->bhqk', qb.astype(F32), kf) * scale
        p = jax.nn.softmax(jnp.where(mask, s, -1e30), axis=-1)
        return jnp.einsum('bhqk,bkhd->bqhd', p, vf)

    return sweep_query_blocks(block, q).astype(q.dtype)


def ssd_chunked(xh, dt, a, bm, cm):
    b, L, H, P = xh.shape
    G, N = bm.shape[2], bm.shape[3]
    J = H // G
    nc = L // BLOCK
    X = (xh * dt[..., None]).reshape(b, nc, BLOCK, G, J, P)
    dA = (dt * a).reshape(b, nc, BLOCK, G, J)
    Acs = jnp.cumsum(dA, axis=2)
    Bc = bm.reshape(b, nc, BLOCK, G, N)
    Cc = cm.reshape(b, nc, BLOCK, G, N)
    tri = jnp.tril(jnp.ones((BLOCK, BLOCK), dtype=bool))
    seg = Acs[:, :, :, None] - Acs[:, :, None, :]
    Ldec = jnp.exp(jnp.where(tri[None, None, :, :, None, None], seg, -jnp.inf))
    CB = jnp.einsum('bclgn,bcsgn->bclsg', Cc, Bc)
    y_diag = jnp.einsum('bclsgj,bcsgjp->bclgjp', CB[..., None] * Ldec, X)
    decay = jnp.exp(Acs[:, :, -1:] - Acs)
    states = jnp.einsum('bclgn,bclgjp->bcgjpn', Bc, X * decay[..., None])
    chunk_decay = jnp.exp(Acs[:, :, -1])

    def step(carry, inp):
        cd, st = inp
        return carry * cd[..., None, None] + st, carry

    init = jnp.zeros((b, G, J, P, N), F32)
    _, prev = lax.scan(step, init, (jnp.moveaxis(chunk_decay, 1, 0), jnp.moveaxis(states, 1, 0)))
    prev = jnp.moveaxis(prev, 0, 1)
    y_off = jnp.einsum('bclgn,bcgjpn->bclgjp', Cc, prev) * jnp.exp(Acs)[..., None]
    return (y_diag + y_off).reshape(b, L, H, P)


def ssd_mixer(z, xbc, dt_raw, conv_w, conv_b, dt_bias, a_log, d_skip, norm_g, valid):
    b, L, _ = xbc.shape
    xbc = xbc * valid[None, :, None].astype(xbc.dtype)
    xbc = lax.conv_general_dilated(
        xbc, conv_w[:, None, :].astype(xbc.dtype), window_strides=(1,), padding=[(CONV_K - 1, 0)],
        dimension_numbers=('NWC', 'WIO', 'NWC'), feature_group_count=CONV_CH) + conv_b.astype(xbc.dtype)
    xbc = jax.nn.silu(xbc)
    xs, bm, cm = jnp.split(xbc, [SSM_WIDTH, SSM_WIDTH + SSM_GROUPS * SSM_STATE], axis=-1)
    xh = xs.reshape(b, L, SSM_HEADS, SSM_HEAD_DIM).astype(F32)
    bm = bm.reshape(b, L, SSM_GROUPS, SSM_STATE).astype(F32)
    cm = cm.reshape(b, L, SSM_GROUPS, SSM_STATE).astype(F32)
    dt = jax.nn.softplus(dt_raw.astype(F32) + dt_bias.astype(F32)) * valid[None, :, None].astype(F32)
    a = -jnp.exp(a_log.astype(F32))
    y = ssd_chunked(xh, dt, a, bm, cm) + xh * d_skip.astype(F32)[:, None]
    y = y.reshape(b, L, SSM_WIDTH)
    return rmsnorm(y * jax.nn.silu(z.astype(F32)), norm_g).astype(z.dtype)


def setup_inputs(seed: int = 0) -> dict:
    key = jax.random.key(seed)
    ks = jax.random.split(key, 24)

    def dense(k, shape, fan_in):
        return jax.random.normal(k, shape, F32) * fan_in ** -0.5

    def gain(k, shape):
        return 1.0 + 0.02 * jax.random.normal(k, shape, F32)

    dt0 = jnp.exp(jax.random.uniform(ks[6], (DEPTH, SSM_HEADS), F32, math.log(1e-3), math.log(1e-1)))
    return {
        "x": jax.random.normal(ks[0], (BATCH, SEQ, D_MODEL), F32),
        "meta_tokens": jax.random.normal(ks[1], (N_META, D_MODEL), F32),
        "mix_norm_g": gain(ks[2], (DEPTH, D_MODEL)),
        "w_in": dense(ks[3], (DEPTH, D_MODEL, IN_DIM), D_MODEL),
        "conv_w": dense(ks[4], (DEPTH, CONV_K, CONV_CH), CONV_K),
        "conv_b": 0.02 * jax.random.normal(ks[5], (DEPTH, CONV_CH), F32),
        "dt_bias": dt0 + jnp.log(-jnp.expm1(-dt0)),
        "a_log": jnp.log(jax.random.uniform(ks[7], (DEPTH, SSM_HEADS), F32, 1.0, 16.0)),
        "d_skip": gain(ks[8], (DEPTH, SSM_HEADS)),
        "ssm_norm_g": gain(ks[9], (DEPTH, SSM_WIDTH)),
        "q_norm_g": gain(ks[10], (DEPTH, Q_LORA)),
        "w_uq": dense(ks[11], (DEPTH, Q_LORA, MLA_HEADS * (QK_NOPE + QK_ROPE)), Q_LORA),
        "kv_norm_g": gain(ks[12], (DEPTH, KV_LORA)),
        "w_ukv": dense(ks[13], (DEPTH, KV_LORA, MLA_HEADS * (QK_NOPE + V_HEAD)), KV_LORA),
        "w_out": dense(ks[14], (DEPTH, MIX_WIDTH, D_MODEL), MIX_WIDTH),
        "ffn_norm_g": gain(ks[15], (DEPTH, D_MODEL)),
        "w_gate": dense(ks[16], (DEPTH, D_MODEL, D_FF), D_MODEL),
        "w_up": dense(ks[17], (DEPTH, D_MODEL, D_FF), D_MODEL),
        "w_down": dense(ks[18], (DEPTH, D_FF, D_MODEL), D_FF),
        "final_norm_g": gain(ks[19], (D_MODEL,)),
    }


def reference(x, meta_tokens, mix_norm_g, w_in, conv_w, conv_b, dt_bias, a_log, d_skip, ssm_norm_g,
              q_norm_g, w_uq, kv_norm_g, w_ukv, w_out, ffn_norm_g, w_gate, w_up, w_down, final_norm_g):
    b = x.shape[0]
    dt = x.dtype
    h = jnp.concatenate([
        jnp.zeros((b, N_PAD, D_MODEL), dt),
        jnp.broadcast_to(meta_tokens.astype(dt)[None], (b, N_META, D_MODEL)),
        x], axis=1)
    L = h.shape[1]
    valid = jnp.arange(L) >= N_PAD
    cos, sin = rope_tables(L)
    split_at = [int(o) for o in np.cumsum(IN_SPLITS)[:-1]]

    for i in range(DEPTH):
        hn = rmsnorm(h, mix_norm_g[i])
        proj = hn @ w_in[i].astype(dt)
        sb_q, sb_k, sb_v, ssm_z, ssm_xbc, ssm_dt, c_q, c_kv, k_r = jnp.split(proj, split_at, axis=-1)

        o_sb = stick_breaking_attention(
            sb_q.reshape(b, L, SB_HEADS, SB_HEAD_DIM),
            sb_k.reshape(b, L, SB_HEADS, SB_HEAD_DIM),
            sb_v.reshape(b, L, SB_HEADS, SB_HEAD_DIM), valid).reshape(b, L, SB_WIDTH)

        o_ssm = ssd_mixer(ssm_z, ssm_xbc, ssm_dt, conv_w[i], conv_b[i], dt_bias[i], a_log[i],
                          d_skip[i], ssm_norm_g[i], valid)

        q = (rmsnorm(c_q, q_norm_g[i]) @ w_uq[i].astype(dt)).reshape(b, L, MLA_HEADS, QK_NOPE + QK_ROPE)
        q_nope, q_rope = jnp.split(q, [QK_NOPE], axis=-1)
        q_rope = apply_rope(q_rope, cos[None, :, None, :], sin[None, :, None, :])
        kv = (rmsnorm(c_kv, kv_norm_g[i]) @ w_ukv[i].astype(dt)).reshape(b, L, MLA_HEADS, QK_NOPE + V_HEAD)
        k_nope, v_mla = jnp.split(kv, [QK_NOPE], axis=-1)
        k_rope = apply_rope(k_r, cos[None], sin[None])
        q_mla = jnp.concatenate([q_nope, q_rope], axis=-1)
        k_mla = jnp.concatenate(
            [k_nope, jnp.broadcast_to(k_rope[:, :, None, :], (b, L, MLA_HEADS, QK_ROPE))], axis=-1)
        o_mla = mla_attention(q_mla, k_mla, v_mla, valid).reshape(b, L, MLA_WIDTH)

        h = h + jnp.concatenate([o_sb, o_ssm, o_mla], axis=-1) @ w_out[i].astype(dt)

        fn = rmsnorm(h, ffn_norm_g[i])
        h = h + (jax.nn.silu(fn @ w_gate[i].astype(dt)) * (fn @ w_up[i].astype(dt))) @ w_down[i].astype(dt)

    return rmsnorm(h, final_norm_g)[:, BLOCK:, :]
```

```python
import contextlib
import numpy as np
import ml_dtypes
import concourse.bass as bass
import concourse.mybir as mybir
from concourse.bass_utils import run_bass_kernel_spmd

F32 = mybir.dt.float32
BF16 = mybir.dt.bfloat16
AF = mybir.ActivationFunctionType
ALU = mybir.AluOpType
NPBF = ml_dtypes.bfloat16

D = 1024
L = 8320
NB = 65
DEPTH = 4
DFF = 2816
NFF = 22
EPS = 1e-6
TB = 17
TT = TB * 128


class Buf:
    __slots__ = ("name", "w", "r", "dsem", "excl")

    def __init__(self, name, excl=False):
        self.name = name
        self.excl = excl
        self.w = {}
        self.r = {}
        self.dsem = None


def _merge(d, s, v):
    if d.get(s, 0) < v:
        d[s] = v


class Prog:
    ENG = ("pe", "act", "dve", "pool", "sp")

    def __init__(self, nc):
        self.nc = nc
        self.es = contextlib.ExitStack()
        self.root_es = contextlib.ExitStack()
        self.prog = {k: [] for k in self.ENG}
        self.sem = {}
        self.cnt = {}
        self.seen = {k: {} for k in self.ENG}
        self.nbuf = 0
        for k in ("pe", "act", "dve", "pool"):
            self.newsem("c_" + k)
        self.out_bufs = []
        self.free_d = []
        self.live = []

    def newsem(self, name):
        self.sem[name] = self.root_es.enter_context(self.nc.semaphore(name))
        self.cnt[name] = 0

    def buf(self, name=None, excl=False):
        self.nbuf += 1
        return Buf(f"{name or 'b'}_{self.nbuf}", excl)

    def sb(self, name, shape, dtype):
        self.nbuf += 1
        return self.es.enter_context(self.nc.sbuf_tensor(f"{name}_{self.nbuf}", list(shape), dtype))

    def ps(self, name, shape, dtype):
        self.nbuf += 1
        return self.es.enter_context(self.nc.psum_tensor(f"{name}_{self.nbuf}", list(shape), dtype))

    def dram(self, name, shape, dtype, kind):
        return self.nc.dram_tensor(name, list(shape), dtype, kind=kind).ap()

    def _deps(self, reads, writes, pwrites):
        deps = {}
        for b in reads:
            for s, v in b.w.items():
                _merge(deps, s, v)
        for b in writes:
            for s, v in b.w.items():
                _merge(deps, s, v)
            for s, v in b.r.items():
                _merge(deps, s, v)
        for b in pwrites:
            for s, v in b.r.items():
                _merge(deps, s, v)
        return deps

    def _waits(self, eng, deps, is_dma):
        own = "c_" + eng
        seen = self.seen[eng]
        for s, v in deps.items():
            if s == own and eng == "pe" and not is_dma:
                continue
            if s.startswith("d_"):
                v = self.cnt[s]
            if seen.get(s, 0) >= v:
                continue
            seen[s] = v
            h = self.sem[s]
            self.prog[eng].append(lambda e, h=h, v=v: e.wait_ge(h, v))

    def _mark(self, tok, reads, writes, pwrites):
        s, v = tok
        for b in reads:
            _merge(b.r, s, v)
        for b in writes:
            b.w = {s: v}
            b.r = {}
        for b in pwrites:
            _merge(b.w, s, v)

    def op(self, eng, fn, reads=(), writes=(), pwrites=()):
        if any(b.excl for b in reads):
            writes = tuple(writes) + tuple(b for b in reads if b.excl)
            reads = tuple(b for b in reads if not b.excl)
        self._waits(eng, self._deps(reads, writes, pwrites), False)
        s = "c_" + eng
        self.cnt[s] += 1
        v = self.cnt[s]
        h = self.sem[s]
        self.prog[eng].append(lambda e, fn=fn, h=h: fn(e).then_inc(h, 1))
        self._mark((s, v), reads, writes, pwrites)

    def dma(self, q, out, in_, reads=(), writes=(), pwrites=(), **kw):
        self._waits(q, self._deps(reads, writes, pwrites), True)
        b = (tuple(writes) + tuple(pwrites))[0]
        if b.dsem is None:
            if self.free_d:
                b.dsem = self.free_d.pop()
            else:
                b.dsem = f"d_{len(self.sem)}"
                self.newsem(b.dsem)
            self.live.append(b)
        s = b.dsem
        self.cnt[s] += 16
        v = self.cnt[s]
        h = self.sem[s]
        self.prog[q].append(lambda e, h=h: e.dma_start(out=out, in_=in_, **kw).then_inc(h, 16))
        self._mark((s, v), reads, writes, pwrites)

    def allgather(self, in_ap, out_ap, reads, writes):
        self._waits("pool", self._deps(reads, writes, ()), True)
        s = "d_cc"
        if s not in self.sem:
            self.newsem(s)
        self.cnt[s] += 1
        v = self.cnt[s]
        h = self.sem[s]
        self.prog["pool"].append(lambda e, h=h: e.collective_compute(
            "AllGather", ALU.bypass, replica_groups=[[0, 1, 2, 3], [4, 5, 6, 7]], ins=[in_ap.opt()], outs=[out_ap.opt()]).then_inc(h))
        self._mark((s, v), reads, writes, ())

    def barrier(self):
        allc = {("c_" + k): self.cnt["c_" + k] for k in ("pe", "act", "dve", "pool")}
        for s in self.cnt:
            if s.startswith("d_"):
                allc[s] = self.cnt[s]
        for eng in self.ENG:
            self._waits(eng, {s: v for s, v in allc.items() if v > 0}, eng == "sp")
        for b in self.live:
            if b.dsem not in self.free_d:
                self.free_d.append(b.dsem)
            b.dsem = None
        self.live = []

    def mm(self, out, lhsT, rhs, start, stop, reads, writes):
        self.op("pe", lambda e: e.matmul(out, lhsT=lhsT, rhs=rhs, start=start, stop=stop), reads, writes)

    def tr(self, out, in_, ident, reads, writes=(), pwrites=()):
        self.op("pe", lambda e: e.transpose(out, in_, ident), reads, writes, pwrites)

    def act(self, out, in_, func, reads, writes=(), pwrites=(), scale=1.0, bias=0.0, accum_out=None):
        if accum_out is None:
            self.op("act", lambda e: e.activation(out=out, in_=in_, func=func, scale=scale, bias=bias),
                    reads, writes, pwrites)
        else:
            self.op("act", lambda e: e.activation(out=out, in_=in_, func=func, scale=scale, bias=bias,
                                                  accum_out=accum_out), reads, writes, pwrites)

    def tt(self, eng, out, in0, in1, op, reads, writes=(), pwrites=()):
        self.op(eng, lambda e: e.tensor_tensor(out=out, in0=in0, in1=in1, op=op), reads, writes, pwrites)

    def ts(self, eng, out, in0, s1, op0, reads, writes=(), pwrites=(), s2=None, op1=None):
        if op1 is None:
            self.op(eng, lambda e: e.tensor_scalar(out=out, in0=in0, scalar1=s1, scalar2=None, op0=op0),
                    reads, writes, pwrites)
        else:
            self.op(eng, lambda e: e.tensor_scalar(out=out, in0=in0, scalar1=s1, scalar2=s2, op0=op0, op1=op1),
                    reads, writes, pwrites)

    def stt(self, out, in0, scalar, in1, op0, op1, reads, writes=(), pwrites=()):
        self.op("dve", lambda e: e.scalar_tensor_tensor(out=out, in0=in0, scalar=scalar, in1=in1, op0=op0, op1=op1),
                reads, writes, pwrites)

    def copy(self, eng, out, in_, reads, writes=(), pwrites=()):
        if eng == "act":
            self.act(out, in_, AF.Copy, reads, writes, pwrites)
        else:
            self.op(eng, lambda e: e.tensor_copy(out=out, in_=in_), reads, writes, pwrites)

    def memset(self, eng, ap, val, writes=(), pwrites=()):
        self.op(eng, lambda e: e.memset(ap, val), (), writes, pwrites)

    def recip(self, out, in_, reads, writes=(), pwrites=()):
        self.op("dve", lambda e: e.reciprocal(out=out, in_=in_), reads, writes, pwrites)

    def finish(self):
        deps = {}
        for b in self.out_bufs:
            for s, v in b.w.items():
                _merge(deps, s, v)
        self._waits("sp", deps, True)
        nc = self.nc
        with nc.Block() as block:
            @block.sync
            def _(e):
                for f in self.prog["sp"]:
                    f(e)

            @block.tensor
            def _(e):
                for f in self.prog["pe"]:
                    f(e)

            @block.scalar
            def _(e):
                for f in self.prog["act"]:
                    f(e)

            @block.vector
            def _(e):
                for f in self.prog["dve"]:
                    f(e)

            @block.gpsimd
            def _(e):
                for f in self.prog["pool"]:
                    f(e)
        self.es.close()
        self.root_es.close()


class Rot:
    def __init__(self, P, name, shape, dtype, n):
        self.t = [P.sb(f"{name}{i}", shape, dtype) for i in range(n)]
        self.b = [P.buf(f"{name}{i}") for i in range(n)]
        self.i = 0
        self.n = n

    def next(self):
        i = self.i
        self.i = (i + 1) % self.n
        return self.t[i], self.b[i]


def setup_consts(P, cdram):
    C = {}
    c32 = P.sb("c32", [128, 896], F32)
    b32 = P.buf("c32")
    P.dma("sp", c32[:], cdram, writes=[b32])
    C["c32"], C["b32"] = c32, b32
    cbf = P.sb("cbf", [128, 896], BF16)
    bbf = P.buf("cbf")
    P.copy("dve", cbf[:], c32[:], [b32], [bbf])
    C["cbf"], C["bbf"] = cbf, bbf
    C["ident32"] = c32[:, 0:128]
    C["tri32"] = c32[:, 128:256]
    C["negmask32"] = c32[:, 512:640]
    C["ones32"] = c32[:, 640:768]
    C["ident"] = cbf[:, 0:128]
    C["strict"] = cbf[:, 256:384]
    C["U"] = cbf[:, 384:512]
    C["ones"] = cbf[:, 640:768]
    C["strict32"] = c32[:, 256:384]
    C["validcol"] = c32[:, 768:769]
    C["le"] = cbf[:, 128:256]
    return C


def host_consts():
    i = np.arange(128)
    ident = (i[:, None] == i[None, :]).astype(np.float32)
    tri = (i[:, None] <= i[None, :]).astype(np.float32)
    strict = (i[:, None] < i[None, :]).astype(np.float32)
    U = (i[:, None] >= i[None, :]).astype(np.float32)
    negmask = np.where(i[None, :] < i[:, None], -30000.0, 0.0).astype(np.float32)
    ones = np.ones((128, 128), np.float32)
    le = (i[:, None] <= i[None, :]).astype(np.float32)
    valid = np.broadcast_to((i >= 112).astype(np.float32)[:, None], (128, 128))
    return np.ascontiguousarray(np.concatenate([ident, tri, strict, U, negmask, ones, valid], axis=1))


def rms_rows(P, C, h_ap, hb, g_bc, gb, out_bf, out_b, small):
    junk, junk_b, ssq, ssq_b, sd, sd_b, rs, rs_b = small
    P.act(junk[:], h_ap, AF.Square, [hb], [junk_b, ssq_b], accum_out=ssq[:])
    P.act(sd[:], ssq[:], AF.Sqrt, [ssq_b], [sd_b], scale=1.0 / D, bias=EPS)
    P.recip(rs[:], sd[:], [sd_b], [rs_b])
    P.stt(out_bf, h_ap, rs[:, 0:1], g_bc, ALU.mult, ALU.mult, [hb, rs_b, gb], [out_b])


SSM_POS = [4 * r + 1 + i for r in range(4) for i in range(2)]
FGROUPS = [(0, 4), (4, 4), (8, 4), (12, 4), (16, 1)]


def emit_F(P, C, banks, bankb, variant, io, tag=""):
    do_mix = variant != "pre"
    last = variant == "last"
    groups = FGROUPS
    with contextlib.ExitStack() as es1:
        save1 = P.es
        P.es = es1
        h = P.sb("h", [128, TB, D], F32)
        hb = [P.buf(f"h{i}{tag}") for i in range(TB)]
        for i in range(TB):
            P.dma("sp", h[:, i, :], io["h_src"][i], reads=io.get("h_src_b", ()), writes=[hb[i]])
        gn = P.sb("gn", [128, D], F32)
        gn_b = P.buf("gn" + tag)
        P.dma("sp", gn[:], io["g_next"], writes=[gn_b])
        junk = P.sb("junk", [128, D], BF16)
        small = (junk, P.buf("junk"), P.sb("ssq", [128, 1], F32), P.buf("ssq"), P.sb("sd", [128, 1], F32), P.buf("sd"),
                 P.sb("rs", [128, 1], F32), P.buf("rs"))
        XT = {}

        def alloc_xT():
            XT["xT"] = P.sb("xT", [128, 8, TT], BF16)
            XT["xT_b"] = [P.buf(f"xT{i}{tag}") for i in range(TB)]
            XT["nrm"] = Rot(P, "nrm", [128, D], BF16, 2)

        def norm_T(g_tile, g_b, blk, bank_i):
            xT, xT_b, nrm = XT["xT"], XT["xT_b"], XT["nrm"]
            t, b = nrm.next()
            rms_rows(P, C, h[:, blk, :], hb[blk], g_tile[:], g_b, t[:], b, small)
            pst = banks[bank_i][:].bitcast(BF16)
            for k in range(8):
                P.tr(pst[:, k * 128:(k + 1) * 128], t[:, k * 128:(k + 1) * 128], C["ident"],
                     [b, C["bbf"]], [bankb[bank_i]])
            P.copy("act" if blk % 2 else "dve", xT[:, :, blk * 128:(blk + 1) * 128],
                   pst.rearrange("p (k t) -> p k t", k=8), [bankb[bank_i]], [xT_b[blk]])

        if do_mix:
            with contextlib.ExitStack() as es2:
                P.es = es2
                gssm = P.sb("gssm", [128, 8], F32)
                gssm_b = P.buf("gssm" + tag)
                P.dma("sp", gssm[:], io["g_ssm"], writes=[gssm_b])
                wout = P.sb("wout", [128, 16, D], BF16)
                wout_b = [P.buf(f"wout{i}{tag}") for i in range(4)]
                stg = Rot(P, "stgo" + tag, [128, 2, D], F32, 2)
                wv = io["w_out"].rearrange("(c p) f -> p c f", p=128)
                for i in range(8):
                    t, b = stg.next()
                    P.dma("sp", t[:], wv[:, 2 * i:2 * i + 2, :], writes=[b])
                    P.copy("pool", wout[:, 2 * i:2 * i + 2, :], t[:], [b], pwrites=[wout_b[i // 2]])
                om = Rot(P, "om" + tag, [128, 16, 512], BF16, 2)
                sq = Rot(P, "sq" + tag, [128, 512], BF16, 3)
                sdt = P.sb("sdt", [128, 512], F32)
                sdt_b = P.buf("sdt")
                rst = P.sb("rst", [128, 512], F32)
                rst_b = P.buf("rst")
                omx = io.get("om_extra")
                if omx:
                    omx = omx()
                for (b0, nb) in groups:
                    n = nb * 128
                    o, ob = om.next()
                    io["om_load"](o, ob, b0, nb, omx)
                    for c in range(8):
                        s_, s_b = sq.next()
                        P.act(s_[:, 0:n], o[:, SSM_POS[c], 0:n], AF.Square, [ob], [s_b])
                        P.mm(banks[0][:, 0:n], C["ones"], s_[:, 0:n], c == 0, c == 7, [s_b, C["bbf"]], [bankb[0]])
                    P.act(sdt[:, 0:n], banks[0][:, 0:n], AF.Sqrt, [bankb[0]], [sdt_b], scale=1.0 / D, bias=EPS)
                    P.recip(rst[:, 0:n], sdt[:, 0:n], [sdt_b], [rst_b])
                    for c in range(8):
                        P.stt(o[:, SSM_POS[c], 0:n], o[:, SSM_POS[c], 0:n], gssm[:, c:c + 1], rst[:, 0:n], ALU.mult, ALU.mult,
                              [ob, gssm_b, rst_b], pwrites=[ob])
                    for bi in range(nb):
                        blk = b0 + bi
                        for oc in range(2):
                            bk = 1 + (2 * blk + oc) % 4
                            for c in range(16):
                                P.mm(banks[bk][:], o[:, c, bi * 128:(bi + 1) * 128], wout[:, c, oc * 512:(oc + 1) * 512],
                                     c == 0, c == 15, [ob, wout_b[c // 4]], [bankb[bk]])
                            P.tt("dve", h[:, blk, oc * 512:(oc + 1) * 512], h[:, blk, oc * 512:(oc + 1) * 512],
                                 banks[bk][:], ALU.add, [bankb[bk], hb[blk]], pwrites=[hb[blk]])
                P.barrier()
                P.es = es1
            alloc_xT()
            xT, xT_b = XT["xT"], XT["xT_b"]
            with contextlib.ExitStack() as es2:
                P.es = es2
                gffn = P.sb("gffn", [128, D], F32)
                gffn_b = P.buf("gffn" + tag)
                P.dma("sp", gffn[:], io["g_ffn"], writes=[gffn_b])
                for blk in range(TB):
                    norm_T(gffn, gffn_b, blk, blk % 2)
                stg_g = Rot(P, "stg_g" + tag, [128, 8, 128], F32, 2)
                stg_u = Rot(P, "stg_u" + tag, [128, 8, 128], F32, 2)
                stg_d = Rot(P, "stg_d" + tag, [128, D], F32, 2)
                wgp = Rot(P, "wgp" + tag, [128, 8, 128], BF16, 2)
                wup = Rot(P, "wup" + tag, [128, 8, 128], BF16, 2)
                wdp = Rot(P, "wdp" + tag, [128, D], BF16, 2)
                sg = Rot(P, "sg" + tag, [128, 512], F32, 2)
                actT = Rot(P, "actT" + tag, [128, 512], BF16, 2)
                wgv = io["w_gate"].rearrange("(k p) f -> p k f", p=128)
                wuv = io["w_up"].rearrange("(k p) f -> p k f", p=128)
                wd_d = io["w_down"]
                mmi = 0
                for f in range(NFF):
                    tg, tg_b = stg_g.next()
                    tu, tu_b = stg_u.next()
                    td, td_b = stg_d.next()
                    P.dma("sp", tg[:], wgv[:, :, f * 128:(f + 1) * 128], writes=[tg_b])
                    P.dma("sp", tu[:], wuv[:, :, f * 128:(f + 1) * 128], writes=[tu_b])
                    P.dma("sp", td[:], wd_d[f * 128:(f + 1) * 128, :], writes=[td_b])
                    wgt, wg_b = wgp.next()
                    wut, wu_b = wup.next()
                    wdt, wd_b = wdp.next()
                    P.copy("pool", wgt[:], tg[:], [tg_b], [wg_b])
                    P.copy("pool", wut[:], tu[:], [tu_b], [wu_b])
                    P.copy("pool", wdt[:], td[:], [td_b], [wd_b])
                    for (b0, nb) in groups:
                        n = nb * 128
                        t0 = b0 * 128
                        rb = [xT_b[b0 + i] for i in range(nb)]
                        for k in range(8):
                            P.mm(banks[0][:, 0:n], wgt[:, k, :], xT[:, k, t0:t0 + n], k == 0, k == 7, rb + [wg_b], [bankb[0]])
                        for k in range(8):
                            P.mm(banks[1][:, 0:n], wut[:, k, :], xT[:, k, t0:t0 + n], k == 0, k == 7, rb + [wu_b], [bankb[1]])
                        s_, s_b = sg.next()
                        P.act(s_[:, 0:n], banks[0][:, 0:n], AF.Silu, [bankb[0]], [s_b])
                        a, a_b = actT.next()
                        P.tt("dve", a[:, 0:n], s_[:, 0:n], banks[1][:, 0:n], ALU.mult, [s_b, bankb[1]], [a_b])
                        for bi in range(nb):
                            blk = b0 + bi
                            for oc in range(2):
                                bk = 2 + mmi % 6
                                mmi += 1
                                P.mm(banks[bk][:], a[:, bi * 128:(bi + 1) * 128], wdt[:, oc * 512:(oc + 1) * 512],
                                     True, True, [a_b, wd_b], [bankb[bk]])
                                P.tt("dve", h[:, blk, oc * 512:(oc + 1) * 512], h[:, blk, oc * 512:(oc + 1) * 512],
                                     banks[bk][:], ALU.add, [bankb[bk], hb[blk]], pwrites=[hb[blk]])
                P.barrier()
                P.es = es1

        if not do_mix:
            alloc_xT()
        xT, xT_b = XT["xT"], XT["xT_b"]
        if last:
            yo = Rot(P, "yo" + tag, [128, D], F32, 2)
            junk, junk_b, ssq, ssq_b, sd, sd_b, rs, rs_b = small
            for blk in range(TB):
                t, b = yo.next()
                P.act(junk[:], h[:, blk, :], AF.Square, [hb[blk]], [junk_b, ssq_b], accum_out=ssq[:])
                P.act(sd[:], ssq[:], AF.Sqrt, [ssq_b], [sd_b], scale=1.0 / D, bias=EPS)
                P.recip(rs[:], sd[:], [sd_b], [rs_b])
                P.stt(t[:], h[:, blk, :], rs[:, 0:1], gn[:], ALU.mult, ALU.mult, [hb[blk], rs_b, gn_b], [b])
                P.dma("sp", io["y_d"][blk], t[:], reads=[b], pwrites=[io["y_b"]])
        else:
            for blk in range(TB):
                norm_T(gn, gn_b, blk, blk % 2)
                if do_mix:
                    P.dma("sp", io["h_dst"][blk], h[:, blk, :], reads=[hb[blk]], pwrites=[io["h_dst_b"]])
            io["hn_store"](xT, xT_b)
        P.barrier()
        P.es = save1


def build_F(variant):
    nc = bass.Bass("TRN2", target_bir_lowering=False)
    P = Prog(nc)
    do_mix = variant != "pre"
    last = variant == "last"
    cdram = P.dram("consts", [128, 896], F32, "ExternalInput")
    io = {"h_src": P.dram("h_in", [TB, 128, D], F32, "ExternalInput"), "g_next": P.dram("g_next", [128, D], F32, "ExternalInput")}
    if do_mix:
        om_d = P.dram("omixT", [16, 128, TT], BF16, "ExternalInput")
        io.update({"w_out": P.dram("w_out", [2048, D], F32, "ExternalInput"), "w_gate": P.dram("w_gate", [D, DFF], F32, "ExternalInput"),
                   "w_up": P.dram("w_up", [D, DFF], F32, "ExternalInput"), "w_down": P.dram("w_down", [DFF, D], F32, "ExternalInput"),
                   "g_ssm": P.dram("g_ssm", [128, 8], F32, "ExternalInput"), "g_ffn": P.dram("g_ffn", [128, D], F32, "ExternalInput")})

        def om_load(o, ob, b0, nb, omx):
            P.dma("sp", o[:, :, 0:nb * 128], om_d[:, :, b0 * 128:(b0 + nb) * 128].rearrange("c p t -> p c t"), writes=[ob])
        io["om_load"] = om_load
    if last:
        io["y_d"] = P.dram("y_out", [TB, 128, D], F32, "ExternalOutput")
        io["y_b"] = P.buf("y_out")
        P.out_bufs.append(io["y_b"])
    else:
        hnT_d = P.dram("hnT_out", [8, 128, TT], BF16, "ExternalOutput")
        hnT_db = P.buf("hnT_out")
        P.out_bufs.append(hnT_db)

        def hn_store(xT, xT_b):
            P.dma("sp", hnT_d.rearrange("k p t -> p k t"), xT[:], reads=xT_b, pwrites=[hnT_db])
        io["hn_store"] = hn_store
        if do_mix:
            io["h_dst"] = P.dram("h_out", [TB, 128, D], F32, "ExternalOutput")
            io["h_dst_b"] = P.buf("h_out")
            P.out_bufs.append(io["h_dst_b"])
    C = setup_consts(P, cdram)
    banks = [P.ps(f"bank{i}", [128, 512], F32) for i in range(8)]
    bankb = [P.buf(f"bank{i}", excl=True) for i in range(8)]
    emit_F(P, C, banks, bankb, variant, io)
    P.finish()
    return nc


QG = [(4 * g, min(4, NB - 4 * g)) for g in range(17)]


def load_cast(P, dst_ap, dst_b, src_ap, shape, name, eng="pool", pw=False):
    t = P.sb(name + "_stg", shape, F32)
    b = P.buf(name + "_stg")
    P.dma("sp", t[:], src_ap, writes=[b])
    if pw:
        P.copy(eng, dst_ap, t[:], [b], pwrites=[dst_b])
    else:
        P.copy(eng, dst_ap, t[:], [b], [dst_b])
    return t, b


def emit_sb(P, C, banks, bankb, hn_load, wsb_d, out_store, hn_rot):
    import os
    with contextlib.ExitStack() as es2:
        save_es = P.es
        P.es = es2
        w = P.sb("sb_w", [128, 8, 384], BF16)
        w_b = P.buf("sb_w")
        load_cast(P, w[:], w_b, wsb_d.rearrange("(k p) f -> p k f", p=128), [128, 8, 384], "sb_w")
        zer = P.sb("sb_zero", [128, 512], BF16)
        zer_b = P.buf("sb_zero")
        P.memset("dve", zer[:], 0.0, [zer_b])
        qT = P.sb("sb_qT", [128, L], BF16)
        kT = P.sb("sb_kT", [128, L], BF16)
        vE = P.sb("sb_vE", [128, NB, 128], BF16)
        vO = P.sb("sb_vO", [128, NB, 128], BF16)
        qT_b = [P.buf(f"sb_qT{g}") for g in range(17)]
        kT_b = [P.buf(f"sb_kT{g}") for g in range(17)]
        v_b = [P.buf(f"sb_v{g}") for g in range(17)]
        P.memset("dve", vE[:].rearrange("p b f -> p (b f)"), 0.0, pwrites=v_b)
        P.memset("pool", vO[:].rearrange("p b f -> p (b f)"), 0.0, pwrites=v_b)
        for g, (b0, nb) in enumerate(QG[:int(os.environ.get("SB_NPROJ", "17"))]):
            n = nb * 128
            t0 = b0 * 128
            hn, hn_b = hn_rot.next()
            hn_load(hn, hn_b, g)
            for k in range(8):
                P.mm(banks[0][:, 0:n], w[:, k, 0:128], hn[:, k, 0:n], k == 0, k == 7, [w_b, hn_b], [bankb[0]])
            P.copy("act", qT[:, t0:t0 + n], banks[0][:, 0:n], [bankb[0]], [qT_b[g]])
            for k in range(8):
                P.mm(banks[1][:, 0:n], w[:, k, 128:256], hn[:, k, 0:n], k == 0, k == 7, [w_b, hn_b], [bankb[1]])
            P.copy("dve", kT[:, t0:t0 + n], banks[1][:, 0:n], [bankb[1]], [kT_b[g]])
            for bi in range(nb):
                for k in range(8):
                    P.mm(banks[2][:, bi * 128:(bi + 1) * 128], hn[:, k, bi * 128:(bi + 1) * 128], w[:, k, 256:384],
                         k == 0, k == 7, [w_b, hn_b], [bankb[2]])
            pv = banks[2][:, 0:n].rearrange("p (b f) -> p b f", f=128)
            P.copy("act", vE[:, b0:b0 + nb, 0:64], pv[:, :, 0:64], [bankb[2]], pwrites=[v_b[g]])
            P.copy("dve", vO[:, b0:b0 + nb, 64:128], pv[:, :, 64:128], [bankb[2]], pwrites=[v_b[g]])
        e_r = Rot(P, "sb_e", [128, 512], F32, 3)
        sp_r = Rot(P, "sb_sp", [128, 512], BF16, 4)
        x_r = Rot(P, "sb_x", [128, 512], F32, 2)
        a_r = Rot(P, "sb_a", [128, 512], BF16, 4)
        o_r = Rot(P, "sb_o", [128, 512], BF16, 2)
        steps = []
        for g, (b0, nb) in enumerate(QG):
            gend = b0 + nb
            first = True
            for kb in range(gend - 1, -1, -1):
                for hd in range(2):
                    qb0 = max(kb, b0)
                    steps.append(dict(g=g, hd=hd, kb=kb, qs=qb0 * 128, n=(gend - qb0) * 128, co=(qb0 - b0) * 128,
                                      diag=kb >= b0, first=(kb == gend - 1), firstg=(kb == gend - 1 and hd == 0),
                                      lastg=(kb == 0 and hd == 1), ncols=nb * 128, t0=b0 * 128))
        import os
        if os.environ.get("SB_MAXG"):
            steps = [st for st in steps if st["g"] < int(os.environ["SB_MAXG"])]
        S = len(steps)
        for st in steps:
            st["e"] = st["sp"] = st["x"] = st["a"] = None

        def stageA(i):
            st = steps[i]
            hs = slice(64 * st["hd"], 64 * st["hd"] + 64)
            n, qs, kb = st["n"], st["qs"], st["kb"]
            zb = i % 2
            P.mm(banks[zb][:, 0:n], kT[hs, kb * 128:(kb + 1) * 128], qT[hs, qs:qs + n], True, True,
                 [kT_b[kb // 4], qT_b[st["g"]]], [bankb[zb]])
            e, e_b = e_r.next()
            P.act(e[:, 0:n], banks[zb][:, 0:n], AF.Exp, [bankb[zb]], [e_b], scale=0.125)
            if st["diag"]:
                P.tt("dve", e[:, 0:128], e[:, 0:128], C["strict32"], ALU.mult, [e_b, C["b32"]], pwrites=[e_b])
            sp, sp_b = sp_r.next()
            P.act(sp[:, 0:n], e[:, 0:n], AF.Ln, [e_b], [sp_b], bias=1.0)
            st["e"], st["sp"] = (e, e_b), (sp, sp_b)

        def cbank(st):
            return 2 + 2 * (st["g"] % 2) + st["hd"]

        def stageB(i):
            st = steps[i]
            n, co = st["n"], st["co"]
            cb = cbank(st)
            if st["first"]:
                P.mm(banks[cb][:, 0:st["ncols"]], zer[:, 0:128], zer[:, 0:st["ncols"]], True, True, [zer_b], [bankb[cb]])
            sp, sp_b = st["sp"]
            P.mm(banks[cb][:, co:co + n], C["U"], sp[:, 0:n], False, True, [sp_b, C["bbf"]], [bankb[cb]])
            x, x_b = x_r.next()
            P.act(x[:, 0:n], banks[cb][:, co:co + n], AF.Exp, [bankb[cb]], [x_b], scale=-1.0)
            e, e_b = st["e"]
            a, a_b = a_r.next()
            P.tt("dve", a[:, 0:n], e[:, 0:n], x[:, 0:n], ALU.mult, [e_b, x_b], [a_b])
            st["a"] = (a, a_b)

        def stageC(i):
            st = steps[i]
            n, co = st["n"], st["co"]
            cb = cbank(st)
            sp, sp_b = st["sp"]
            P.mm(banks[cb][:, co:co + n], C["strict"], sp[:, 0:n], False, True, [sp_b, C["bbf"]], [bankb[cb]])

        def stageD(i):
            st = steps[i]
            n, co, kb, g = st["n"], st["co"], st["kb"], st["g"]
            ob = 6 + g % 2
            if st["firstg"]:
                P.mm(banks[ob][:, 0:st["ncols"]], zer[:, 0:128], zer[:, 0:st["ncols"]], True, True, [zer_b], [bankb[ob]])
            a, a_b = st["a"]
            vv = vE if st["hd"] == 0 else vO
            P.mm(banks[ob][:, co:co + n], vv[:, kb, :], a[:, 0:n], False, True, [a_b, v_b[kb // 4]], [bankb[ob]])
            if st["lastg"]:
                nc_ = st["ncols"]
                o, o_b = o_r.next()
                P.copy("act", o[:, 0:nc_], banks[ob][:, 0:nc_], [bankb[ob]], [o_b])
                out_store(0, o, st["t0"], nc_, o_b)

        if S:
            stageA(0)
        for i in range(S + 2):
            if i + 1 < S:
                stageA(i + 1)
            if i < S:
                stageB(i)
            if 0 <= i - 1 < S:
                stageC(i - 1)
            if 0 <= i - 2 < S:
                stageD(i - 2)
        P.barrier()
        P.es = save_es


def emit_mla(P, C, banks, bankb, hn_load, W, out_store, hn_rot):
    SC = 1.0 / np.sqrt(96.0)
    with contextlib.ExitStack() as es2:
        save_es = P.es
        P.es = es2
        wm = P.sb("ml_wm", [128, 8, 832], BF16)
        wm_b = P.buf("ml_wm")
        wA = P.sb("ml_wA", [128, 3, 192], BF16)
        wB = P.sb("ml_wB", [128, 3, 192], BF16)
        wK = P.sb("ml_wK", [128, 2, 128], BF16)
        wV = P.sb("ml_wV", [128, 2, 128], BF16)
        wu_b = P.buf("ml_wu")
        gq = P.sb("ml_gq", [128, 3], F32)
        gkv = P.sb("ml_gkv", [128, 2], F32)
        g_b = P.buf("ml_g")
        P.dma("sp", gq[:], W["g_q"], pwrites=[g_b])
        P.dma("sp", gkv[:], W["g_kv"], pwrites=[g_b])
        with contextlib.ExitStack() as es3:
            P.es = es3
            stg = Rot(P, "ml_stg", [128, 2, 832], F32, 2)
            wv_ = W["w_mla_in"].rearrange("(k p) f -> p k f", p=128)
            for i in range(4):
                t, b = stg.next()
                P.dma("sp", t[:], wv_[:, 2 * i:2 * i + 2, :], writes=[b])
                P.copy("pool", wm[:, 2 * i:2 * i + 2, :], t[:], [b], pwrites=[wm_b])
            sA = P.sb("ml_sA", [128, 3, 192], F32)
            sB = P.sb("ml_sB", [128, 3, 192], F32)
            sK = P.sb("ml_sK", [128, 2, 128], F32)
            sV = P.sb("ml_sV", [128, 2, 128], F32)
            s_b = P.buf("ml_s")
            P.dma("sp", sA[:], W["w_uqA"].rearrange("(k p) f -> p k f", p=128), pwrites=[s_b])
            P.dma("sp", sB[:], W["w_uqB"].rearrange("(k p) f -> p k f", p=128), pwrites=[s_b])
            P.dma("sp", sK[:], W["w_ukvk"].rearrange("(k p) f -> p k f", p=128), pwrites=[s_b])
            P.dma("sp", sV[:], W["w_ukvv"].rearrange("(k p) f -> p k f", p=128), pwrites=[s_b])
            for c in range(3):
                P.ts("dve", wA[:, c, :], sA[:, c, :], gq[:, c:c + 1], ALU.mult, [s_b, g_b], pwrites=[wu_b])
                P.ts("dve", wB[:, c, :], sB[:, c, :], gq[:, c:c + 1], ALU.mult, [s_b, g_b], pwrites=[wu_b])
            for c in range(2):
                P.ts("dve", wK[:, c, :], sK[:, c, :], gkv[:, c:c + 1], ALU.mult, [s_b, g_b], pwrites=[wu_b])
                P.ts("dve", wV[:, c, :], sV[:, c, :], gkv[:, c:c + 1], ALU.mult, [s_b, g_b], pwrites=[wu_b])
            P.barrier()
            P.es = es2
        qT = P.sb("ml_qT", [128, 2, L], BF16)
        kT = P.sb("ml_kT", [128, 2, L], BF16)
        va = P.sb("ml_va", [128, NB, 2, 128], BF16)
        qT_b = [P.buf(f"ml_qT{g}") for g in range(17)]
        kT_b = [P.buf(f"ml_kT{g}") for g in range(17)]
        v_b = [P.buf(f"ml_v{g}") for g in range(17)]
        P.memset("dve", va[:].rearrange("p b h f -> p (b h f)"), 1.0, pwrites=v_b)
        cq = P.sb("ml_cq", [128, 3, 512], BF16)
        sqq = P.sb("ml_sqq", [128, 3, 512], BF16)
        ckv = P.sb("ml_ckv", [128, 2, 512], BF16)
        sqkv = P.sb("ml_sqkv", [128, 2, 512], BF16)
        c_b = P.buf("ml_c")
        sdq = P.sb("ml_sdq", [128, 512], F32)
        rq = P.sb("ml_rq", [128, 512], F32)
        sdk = P.sb("ml_sdk", [128, 512], F32)
        rkv = P.sb("ml_rkv", [128, 512], F32)
        sdt = P.sb("ml_sdt", [128, 4], F32)
        rkt = P.sb("ml_rkt", [128, 4], F32)
        r_b = P.buf("ml_r")
        rp = P.sb("ml_rp", [128, 2, 512], F32)
        rp_b = P.buf("ml_rp")
        cs = P.sb("ml_cs", [128, 2, 512], F32)
        cs_b = P.buf("ml_cs")
        t1 = P.sb("ml_t1", [128, 512], F32)
        t2 = P.sb("ml_t2", [128, 512], F32)
        t_b = P.buf("ml_t")
        bc = [0]

        def nbk():
            bc[0] = (bc[0] + 1) % 8
            return bc[0]

        for g, (b0, nb) in enumerate(QG):
            n = nb * 128
            t0 = b0 * 128
            hn, hn_b = hn_rot.next()
            hn_load(hn, hn_b, g)
            P.dma("sp", rp[64:96, :, 0:n], W["rope"][:, :, t0:t0 + n].rearrange("a r t -> r a t"), writes=[rp_b])
            for c in range(3):
                bk = nbk()
                for k in range(8):
                    P.mm(banks[bk][:, 0:n], wm[:, k, c * 128:(c + 1) * 128], hn[:, k, 0:n], k == 0, k == 7,
                         [wm_b, hn_b], [bankb[bk]])
                P.act(sqq[:, c, 0:n], banks[bk][:, 0:n], AF.Square, [bankb[bk]], writes=[c_b] if c == 0 else (),
                      pwrites=() if c == 0 else [c_b])
                P.copy("dve", cq[:, c, 0:n], banks[bk][:, 0:n], [bankb[bk]], pwrites=[c_b])
            for c in range(2):
                bk = nbk()
                for k in range(8):
                    P.mm(banks[bk][:, 0:n], wm[:, k, 384 + c * 128:384 + (c + 1) * 128], hn[:, k, 0:n], k == 0, k == 7,
                         [wm_b, hn_b], [bankb[bk]])
                P.act(sqkv[:, c, 0:n], banks[bk][:, 0:n], AF.Square, [bankb[bk]], pwrites=[c_b])
                P.copy("dve", ckv[:, c, 0:n], banks[bk][:, 0:n], [bankb[bk]], pwrites=[c_b])
            bk = nbk()
            for c in range(3):
                P.mm(banks[bk][:, 0:n], C["ones"], sqq[:, c, 0:n], c == 0, c == 2, [c_b, C["bbf"]], [bankb[bk]])
            P.act(sdq[:, 0:n], banks[bk][:, 0:n], AF.Sqrt, [bankb[bk]], [r_b], scale=1.0 / 384, bias=EPS)
            P.recip(rq[:, 0:n], sdq[:, 0:n], [r_b], pwrites=[r_b])
            bk = nbk()
            for c in range(2):
                P.mm(banks[bk][:, 0:n], C["ones"], sqkv[:, c, 0:n], c == 0, c == 1, [c_b, C["bbf"]], [bankb[bk]])
            P.act(sdk[:, 0:n], banks[bk][:, 0:n], AF.Sqrt, [bankb[bk]], pwrites=[r_b], scale=1.0 / 256, bias=EPS)
            P.recip(rkv[:, 0:n], sdk[:, 0:n], [r_b], pwrites=[r_b])
            bk = nbk()
            for bi in range(nb):
                for c in range(2):
                    P.mm(banks[bk][:, bi:bi + 1], sqkv[:, c, bi * 128:(bi + 1) * 128], C["ones"][:, 0:1], c == 0, c == 1,
                         [c_b, C["bbf"]], [bankb[bk]])
            P.act(sdt[:, 0:nb], banks[bk][:, 0:nb], AF.Sqrt, [bankb[bk]], pwrites=[r_b], scale=1.0 / 256, bias=EPS)
            P.recip(rkt[:, 0:nb], sdt[:, 0:nb], [r_b], pwrites=[r_b])
            P.tt("dve", cs[64:96, 0, 0:n], rp[64:96, 0, 0:n], rq[64:96, 0:n], ALU.mult, [rp_b, r_b], [cs_b])
            P.tt("dve", cs[64:96, 1, 0:n], rp[64:96, 1, 0:n], rq[64:96, 0:n], ALU.mult, [rp_b, r_b], pwrites=[cs_b])
            for h in range(2):
                ba = nbk()
                for c in range(3):
                    P.mm(banks[ba][0:96, 0:n], wA[:, c, h * 96:(h + 1) * 96], cq[:, c, 0:n], c == 0, c == 2,
                         [wu_b, c_b], [bankb[ba]])
                P.tt("dve", qT[0:64, h, t0:t0 + n], banks[ba][0:64, 0:n], rq[0:64, 0:n], ALU.mult, [bankb[ba], r_b],
                     pwrites=[qT_b[g]])
                P.tt("dve", t1[64:96, 0:n], banks[ba][64:96, 0:n], cs[64:96, 0, 0:n], ALU.mult, [bankb[ba], cs_b], [t_b])
                bb = nbk()
                for c in range(3):
                    P.mm(banks[bb][0:96, 0:n], wB[:, c, h * 96:(h + 1) * 96], cq[:, c, 0:n], c == 0, c == 2,
                         [wu_b, c_b], [bankb[bb]])
                P.tt("dve", t2[64:96, 0:n], banks[bb][64:96, 0:n], cs[64:96, 1, 0:n], ALU.mult, [bankb[bb], cs_b],
                     pwrites=[t_b])
                P.tt("dve", qT[64:96, h, t0:t0 + n], t1[64:96, 0:n], t2[64:96, 0:n], ALU.add, [t_b], pwrites=[qT_b[g]])
            for h in range(2):
                bk = nbk()
                for c in range(2):
                    P.mm(banks[bk][0:64, 0:n], wK[:, c, h * 64:(h + 1) * 64], ckv[:, c, 0:n], c == 0, c == 1,
                         [wu_b, c_b], [bankb[bk]])
                P.tt("dve", kT[0:64, h, t0:t0 + n], banks[bk][0:64, 0:n], rkv[0:64, 0:n], ALU.mult, [bankb[bk], r_b],
                     pwrites=[kT_b[g]])
            b1 = nbk()
            for k in range(8):
                P.mm(banks[b1][0:96, 0:n], wm[:, k, 640:736], hn[:, k, 0:n], k == 0, k == 7, [wm_b, hn_b], [bankb[b1]])
            P.tt("dve", t1[64:96, 0:n], banks[b1][64:96, 0:n], rp[64:96, 0, 0:n], ALU.mult, [bankb[b1], rp_b], [t_b])
            b2 = nbk()
            for k in range(8):
                P.mm(banks[b2][0:96, 0:n], wm[:, k, 736:832], hn[:, k, 0:n], k == 0, k == 7, [wm_b, hn_b], [bankb[b2]])
            P.tt("dve", t2[64:96, 0:n], banks[b2][64:96, 0:n], rp[64:96, 1, 0:n], ALU.mult, [bankb[b2], rp_b],
                 pwrites=[t_b])
            P.tt("dve", kT[64:96, 0, t0:t0 + n], t1[64:96, 0:n], t2[64:96, 0:n], ALU.add, [t_b], pwrites=[kT_b[g]])
            P.tt("pool", kT[64:96, 1, t0:t0 + n], t1[64:96, 0:n], t2[64:96, 0:n], ALU.add, [t_b], pwrites=[kT_b[g]])
            bk = nbk()
            for bi in range(nb):
                for c in range(2):
                    P.mm(banks[bk][:, bi * 128:(bi + 1) * 128], ckv[:, c, bi * 128:(bi + 1) * 128], wV[:, c, :],
                         c == 0, c == 1, [wu_b, c_b], [bankb[bk]])
            for bi in range(nb):
                blk = b0 + bi
                P.ts("dve", va[:, blk, 0, 0:64], banks[bk][:, bi * 128:bi * 128 + 64], rkt[:, bi:bi + 1], ALU.mult,
                     [bankb[bk], r_b], pwrites=[v_b[g]])
                P.ts("dve", va[:, blk, 1, 64:128], banks[bk][:, bi * 128 + 64:bi * 128 + 128], rkt[:, bi:bi + 1], ALU.mult,
                     [bankb[bk], r_b], pwrites=[v_b[g]])
            if g == 0:
                v0 = va[:, 0, :, :].rearrange("p h f -> p (h f)")
                P.ts("dve", v0, v0, C["validcol"], ALU.mult, [C["b32"]], pwrites=[v_b[0]])
        p_r = Rot(P, "ml_p", [128, 512], BF16, 4)
        o_r = Rot(P, "ml_o", [128, 512], BF16, 2)
        rc = P.sb("ml_rc", [128, 512], F32)
        rc_b = P.buf("ml_rc")
        steps = []
        for g, (b0, nb) in enumerate(QG):
            gend = b0 + nb
            for hd in range(2):
                for kb in range(gend):
                    qb0 = max(kb, b0)
                    steps.append(dict(g=g, hd=hd, kb=kb, qs=qb0 * 128, n=(gend - qb0) * 128, co=(qb0 - b0) * 128,
                                      diag=kb >= b0, first=(kb == 0), last=(kb == gend - 1), ncols=nb * 128, t0=b0 * 128))
        S = len(steps)
        cur_o = {}

        def stA(i):
            st = steps[i]
            n, qs, kb, hd = st["n"], st["qs"], st["kb"], st["hd"]
            zb = i % 2
            P.mm(banks[zb][:, 0:n], kT[0:96, hd, kb * 128:(kb + 1) * 128], qT[0:96, hd, qs:qs + n], True, True,
                 [kT_b[kb // 4], qT_b[st["g"]]], [bankb[zb]])
            p, p_b = p_r.next()
            P.act(p[:, 0:n], banks[zb][:, 0:n], AF.Exp, [bankb[zb]], [p_b], scale=SC)
            if st["diag"]:
                P.tt("dve", p[:, 0:128], p[:, 0:128], C["le"], ALU.mult, [p_b, C["bbf"]], pwrites=[p_b])
            st["p"] = (p, p_b)

        def stB(i):
            st = steps[i]
            n, co, kb, hd, g = st["n"], st["co"], st["kb"], st["hd"], st["g"]
            ob = 2 + 2 * (g % 2) + hd
            p, p_b = st["p"]
            P.mm(banks[ob][:, co:co + n], va[:, kb, hd, :], p[:, 0:n], st["first"], st["last"], [p_b, v_b[kb // 4]],
                 [bankb[ob]])
            if st["last"]:
                nc_ = st["ncols"]
                if hd == 0:
                    cur_o[g] = o_r.next()
                o, o_b = cur_o[g]
                dn = slice(64, 128) if hd == 0 else slice(0, 64)
                nm = slice(0, 64) if hd == 0 else slice(64, 128)
                P.ts("dve", rc[dn, 0:nc_], banks[ob][dn, 0:nc_], 1e-30, ALU.add, [bankb[ob]], [rc_b])
                P.recip(rc[dn, 0:nc_], rc[dn, 0:nc_], [rc_b], pwrites=[rc_b])
                P.tt("dve", o[nm, 0:nc_], banks[ob][nm, 0:nc_], rc[dn, 0:nc_], ALU.mult, [bankb[ob], rc_b],
                     writes=[o_b] if hd == 0 else (), pwrites=() if hd == 0 else [o_b])
                if hd == 1:
                    out_store(3, o, st["t0"], nc_, o_b)

        if S:
            stA(0)
        for i in range(S):
            if i + 1 < S:
                stA(i + 1)
            stB(i)
        P.barrier()
        P.es = save_es


def emit_ssd(P, C, banks, bankb, hn_load, W, out_store, hn_rot):
    with contextlib.ExitStack() as es2:
        save_es = P.es
        P.es = es2
        ws = P.sb("sd_ws", [128, 8, 768], BF16)
        ws_b = P.buf("sd_ws")
        wdt = P.sb("sd_wdt", [128, 8, 4], BF16)
        wdt_b = P.buf("sd_wdt")
        cw = P.sb("sd_cw", [128, 4, 4], F32)
        cb = P.sb("sd_cb", [128, 4], F32)
        dtb = P.sb("sd_dtb", [128, 4], F32)
        alog = P.sb("sd_alog", [128, 4], F32)
        abc = P.sb("sd_abc", [128, 4], F32)
        dsk = P.sb("sd_dsk", [128, 4], F32)
        sm_b = P.buf("sd_small")
        P.dma("sp", cw[:], W["conv_w"], pwrites=[sm_b])
        P.dma("sp", cb[:], W["conv_b"], pwrites=[sm_b])
        P.dma("sp", dtb[:], W["dt_bias"], pwrites=[sm_b])
        P.dma("sp", alog[:], W["a_log"], pwrites=[sm_b])
        P.dma("sp", dsk[:], W["d_skip"], pwrites=[sm_b])
        a_b = P.buf("sd_a")
        P.act(abc[:], alog[:], AF.Exp, [sm_b], [a_b])
        P.ts("dve", abc[:], abc[:], -1.0, ALU.mult, [a_b], pwrites=[a_b])
        with contextlib.ExitStack() as es3:
            P.es = es3
            stg = Rot(P, "sd_stg", [128, 2, 768], F32, 2)
            wv_ = W["w_ssm"].rearrange("(k p) f -> p k f", p=128)
            for i in range(4):
                t, b = stg.next()
                P.dma("sp", t[:], wv_[:, 2 * i:2 * i + 2, :], writes=[b])
                P.copy("pool", ws[:, 2 * i:2 * i + 2, :], t[:], [b], pwrites=[ws_b])
            sdt_ = P.sb("sd_sdt", [128, 8, 4], F32)
            sdt_b = P.buf("sd_sdt")
            P.dma("sp", sdt_[:], W["w_dt"].rearrange("(k p) f -> p k f", p=128), writes=[sdt_b])
            P.copy("pool", wdt[:], sdt_[:], [sdt_b], [wdt_b])
            P.barrier()
            P.es = es2
        raw = P.sb("sd_raw", [128, 4, 515], F32)
        raw_b = P.buf("sd_raw")
        P.memset("dve", raw[:].rearrange("p c t -> p (c t)"), 0.0, [raw_b])
        nm4 = P.sb("sd_nm4", [128, 4, 128], F32)
        nm4_b = P.buf("sd_nm4")
        for h in range(4):
            P.copy("pool", nm4[:, h, :], C["negmask32"], [C["b32"]], pwrites=[nm4_b])
        acc = P.sb("sd_acc", [128, 512], F32)
        acc_b = P.buf("sd_acc")
        xcT = P.sb("sd_xcT", [128, 4, 512], BF16)
        xcT_b = P.buf("sd_xcT")
        sz = P.sb("sd_sz", [128, 2, 512], F32)
        sz_b = P.buf("sd_sz")
        xtok = P.sb("sd_xtok", [128, 4, 256], BF16)
        btok = P.sb("sd_btok", [128, 4, 128], BF16)
        tok_b = P.buf("sd_tok")
        dtt = P.sb("sd_dtt", [128, 4, 4], F32)
        dte = P.sb("sd_dte", [128, 4, 4], F32)
        dt = P.sb("sd_dt", [128, 4, 4], F32)
        dA = P.sb("sd_dA", [128, 4, 4], F32)
        dt_b = P.buf("sd_dt")
        acs = P.sb("sd_acs", [128, 4], F32)
        eA = P.sb("sd_eA", [128, 4], F32)
        dtmp = P.sb("sd_dtmp", [128, 4], F32)
        dend = P.sb("sd_dend", [128, 4], F32)
        cd = P.sb("sd_cd", [128, 4], F32)
        dec_b = P.buf("sd_dec")
        dAbc = P.sb("sd_dAbc", [128, 4, 128], F32)
        ndAbc = P.sb("sd_ndAbc", [128, 4, 128], F32)
        dAbc_b = P.buf("sd_dAbc")
        Ld = P.sb("sd_Ld", [128, 4, 128], F32)
        Ld_b = P.buf("sd_Ld")
        GT = P.sb("sd_GT", [128, 4, 128], BF16)
        GT_b = P.buf("sd_GT")
        Xd = P.sb("sd_Xd", [128, 256], BF16)
        Xd_b = P.buf("sd_Xd")
        Xdd = P.sb("sd_Xdd", [128, 256], BF16)
        Xdd_b = P.buf("sd_Xdd")
        yt = P.sb("sd_yt", [128, 256], F32)
        y = P.sb("sd_y", [128, 256], F32)
        y_b = P.buf("sd_y")
        xsk = P.sb("sd_xsk", [128, 256], F32)
        xsk_b = P.buf("sd_xsk")
        S = P.sb("sd_S", [128, 256], F32)
        Sbf = P.sb("sd_Sbf", [128, 256], BF16)
        S_b = P.buf("sd_S")
        Sbf_b = P.buf("sd_Sbf")
        P.memset("dve", S[:], 0.0, [S_b])
        P.memset("dve", Sbf[:], 0.0, [Sbf_b])
        og = Rot(P, "sd_og", [128, 2, 512], BF16, 2)
        bc = [0]

        def nbk():
            bc[0] = (bc[0] + 1) % 8
            return bc[0]

        def bc3(ap2, nlast):
            return ap2.unsqueeze(2).to_broadcast([128, 4, nlast])

        for g, (b0, nb) in enumerate(QG):
            n = nb * 128
            t0 = b0 * 128
            hn, hn_b = hn_rot.next()
            hn_load(hn, hn_b, g)
            for c in range(2):
                bk = nbk()
                for k in range(8):
                    P.mm(banks[bk][:, 0:n], ws[:, k, c * 128:(c + 1) * 128], hn[:, k, 0:n], k == 0, k == 7,
                         [ws_b, hn_b], [bankb[bk]])
                P.act(sz[:, c, 0:n], banks[bk][:, 0:n], AF.Silu, [bankb[bk]], writes=[sz_b] if c == 0 else (),
                      pwrites=() if c == 0 else [sz_b])
            for c in range(4):
                bk = nbk()
                for k in range(8):
                    P.mm(banks[bk][:, 0:n], ws[:, k, 256 + c * 128:256 + (c + 1) * 128], hn[:, k, 0:n], k == 0, k == 7,
                         [ws_b, hn_b], [bankb[bk]])
                P.copy("act", raw[:, c, 3:3 + n], banks[bk][:, 0:n], [bankb[bk]], pwrites=[raw_b])
                P.ts("dve", acc[:, 0:n], raw[:, c, 3:3 + n], cw[:, c, 3:4], ALU.mult, [raw_b, sm_b], [acc_b],
                     s2=cb[:, c:c + 1], op1=ALU.add)
                for tap in (2, 1, 0):
                    P.stt(acc[:, 0:n], raw[:, c, tap:tap + n], cw[:, c, tap:tap + 1], acc[:, 0:n], ALU.mult, ALU.add,
                          [raw_b, sm_b, acc_b], pwrites=[acc_b])
                P.act(xcT[:, c, 0:n], acc[:, 0:n], AF.Silu, [acc_b], writes=[xcT_b] if c == 0 else (),
                      pwrites=() if c == 0 else [xcT_b])
                P.copy("pool", raw[:, c, 0:3], raw[:, c, n:n + 3], [raw_b, acc_b], pwrites=[raw_b])
            bk = nbk()
            for bi in range(nb):
                for k in range(8):
                    P.mm(banks[bk][:, bi * 4:(bi + 1) * 4], hn[:, k, bi * 128:(bi + 1) * 128], wdt[:, k, :], k == 0, k == 7,
                         [wdt_b, hn_b], [bankb[bk]])
            pdt = banks[bk][:, 0:4 * nb].rearrange("p (b h) -> p b h", h=4)
            P.tt("dve", dtt[:, 0:nb, :], pdt, dtb[:].unsqueeze(1).to_broadcast([128, nb, 4]), ALU.add,
                 [bankb[bk], sm_b], [dt_b])
            P.act(dte[:, 0:nb, :], dtt[:, 0:nb, :], AF.Exp, [dt_b], pwrites=[dt_b])
            P.act(dt[:, 0:nb, :], dte[:, 0:nb, :], AF.Ln, [dt_b], pwrites=[dt_b], bias=1.0)
            if g == 0:
                P.ts("dve", dt[:, 0, :], dt[:, 0, :], C["validcol"], ALU.mult, [dt_b, C["b32"]], pwrites=[dt_b])
            P.tt("dve", dA[:, 0:nb, :], dt[:, 0:nb, :], abc[:].unsqueeze(1).to_broadcast([128, nb, 4]), ALU.mult,
                 [dt_b, a_b], pwrites=[dt_b])
            for bi in range(nb):
                bk = nbk()
                pst = banks[bk][:].bitcast(BF16)
                for c in range(3):
                    P.tr(pst[:, c * 128:(c + 1) * 128], xcT[:, c, bi * 128:(bi + 1) * 128], C["ident"],
                         [xcT_b, C["bbf"]], [bankb[bk]])
                P.copy("act", xtok[:, bi, :], pst[:, 0:256], [bankb[bk]], writes=[tok_b] if bi == 0 else (),
                       pwrites=() if bi == 0 else [tok_b])
                P.copy("act", btok[:, bi, :], pst[:, 256:384], [bankb[bk]], pwrites=[tok_b])
            o, o_b = og.next()
            for bi in range(nb):
                cols = slice(bi * 128, (bi + 1) * 128)
                bk = nbk()
                P.mm(banks[bk][:, 0:4], C["tri32"], dA[:, bi, :], True, True, [dt_b, C["b32"]], [bankb[bk]])
                P.mm(banks[bk][:, 4:8], C["ones32"], dA[:, bi, :], True, True, [dt_b, C["b32"]], [bankb[bk]])
                P.act(eA[:], banks[bk][:, 0:4], AF.Exp, [bankb[bk]], [dec_b])
                P.act(acs[:], banks[bk][:, 0:4], AF.Identity, [bankb[bk]], pwrites=[dec_b])
                P.act(cd[:], banks[bk][:, 4:8], AF.Exp, [bankb[bk]], pwrites=[dec_b])
                P.tt("dve", dtmp[:], banks[bk][:, 4:8], acs[:], ALU.subtract, [bankb[bk], dec_b], pwrites=[dec_b])
                P.act(dend[:], dtmp[:], AF.Exp, [dec_b], pwrites=[dec_b])
                P.copy("pool", dAbc[:], bc3(dA[:, bi, :], 128), [dt_b], [dAbc_b])
                P.ts("pool", ndAbc[:], dAbc[:], -1.0, ALU.mult, [dAbc_b], pwrites=[dAbc_b])
                bs = nbk()
                P.mm(banks[bs][:], C["tri32"], ndAbc[:].rearrange("p h l -> p (h l)"), True, False,
                     [dAbc_b, C["b32"]], [bankb[bs]])
                P.mm(banks[bs][:], C["ident32"], nm4[:].rearrange("p h l -> p (h l)"), False, False,
                     [nm4_b, C["b32"]], [bankb[bs]])
                for h in range(4):
                    P.mm(banks[bs][:, h * 128:(h + 1) * 128], dAbc[:, h, :], C["tri32"], False, h == 3,
                         [dAbc_b, C["b32"]], [bankb[bs]])
                P.act(Ld[:].rearrange("p h l -> p (h l)"), banks[bs][:], AF.Exp, [bankb[bs]], [Ld_b])
                bcb = nbk()
                P.mm(banks[bcb][:, 0:128], xcT[:, 2, cols], xcT[:, 3, cols], True, True, [xcT_b], [bankb[bcb]])
                P.tt("dve", GT[:], Ld[:], banks[bcb][:, 0:128].unsqueeze(1).to_broadcast([128, 4, 128]), ALU.mult,
                     [Ld_b, bankb[bcb]], [GT_b])
                P.tt("pool", Xd[:].rearrange("p (h q) -> p h q", h=4), xtok[:, bi, :].rearrange("p (h q) -> p h q", h=4),
                     bc3(dt[:, bi, :], 64), ALU.mult, [tok_b, dt_b], [Xd_b])
                P.tt("pool", Xdd[:].rearrange("p (h q) -> p h q", h=4), Xd[:].rearrange("p (h q) -> p h q", h=4),
                     bc3(dend[:], 64), ALU.mult, [Xd_b, dec_b], [Xdd_b])
                P.tt("pool", xsk[:].rearrange("p (h q) -> p h q", h=4), xtok[:, bi, :].rearrange("p (h q) -> p h q", h=4),
                     bc3(dsk[:], 64), ALU.mult, [tok_b, sm_b], [xsk_b])
                by = nbk()
                for h in range(4):
                    P.mm(banks[by][:, h * 64:(h + 1) * 64], GT[:, h, :], Xd[:, h * 64:(h + 1) * 64], True, True,
                         [GT_b, Xd_b], [bankb[by]])
                bo = nbk()
                P.mm(banks[bo][:, 0:256], xcT[:, 3, cols], Sbf[:], True, True, [xcT_b, Sbf_b], [bankb[bo]])
                P.tt("dve", yt[:].rearrange("p (h q) -> p h q", h=4), banks[bo][:, 0:256].rearrange("p (h q) -> p h q", h=4),
                     bc3(eA[:], 64), ALU.mult, [bankb[bo], dec_b], [y_b])
                P.tt("dve", y[:], banks[by][:, 0:256], yt[:], ALU.add, [bankb[by], y_b], pwrites=[y_b])
                P.tt("dve", y[:], y[:], xsk[:], ALU.add, [y_b, xsk_b], pwrites=[y_b])
                bst = nbk()
                P.mm(banks[bst][:, 0:256], btok[:, bi, :], Xdd[:], True, True, [tok_b, Xdd_b], [bankb[bst]])
                P.tt("dve", S[:].rearrange("p (h q) -> p h q", h=4), S[:].rearrange("p (h q) -> p h q", h=4),
                     bc3(cd[:], 64), ALU.mult, [S_b, dec_b], pwrites=[S_b])
                P.tt("dve", S[:], S[:], banks[bst][:, 0:256], ALU.add, [S_b, bankb[bst]], pwrites=[S_b])
                P.copy("act", Sbf[:], S[:], [S_b], [Sbf_b])
                bt = nbk()
                for c in range(2):
                    P.tr(banks[bt][:, c * 128:(c + 1) * 128], y[:, c * 128:(c + 1) * 128], C["ident32"],
                         [y_b, C["b32"]], [bankb[bt]])
                P.tt("dve", o[:, :, cols], banks[bt][:, 0:256].rearrange("p (c t) -> p c t", c=2), sz[:, :, cols], ALU.mult,
                     [bankb[bt], sz_b], writes=[o_b] if bi == 0 else (), pwrites=() if bi == 0 else [o_b])
            out_store(1, o[:, 0, :], t0, n, o_b)
            out_store(2, o[:, 1, :], t0, n, o_b)
        P.barrier()
        P.es = save_es


def m_weight_tensors(P, parts, lead=None):
    def T(name, shape):
        return P.dram(name, ([lead] if lead else []) + shape, F32, "ExternalInput")
    W = {}
    if "sb" in parts:
        W["w_sb"] = T("w_sb", [D, 384])
    if "mla" in parts:
        W.update({"w_mla_in": T("w_mla_in", [D, 832]), "w_uqA": T("w_uqA", [384, 192]), "w_uqB": T("w_uqB", [384, 192]),
                  "w_ukvk": T("w_ukvk", [256, 128]), "w_ukvv": T("w_ukvv", [256, 128]), "g_q": T("g_q", [128, 3]),
                  "g_kv": T("g_kv", [128, 2])})
        W["rope"] = P.dram("rope", [2, 32, L], F32, "ExternalInput")
    if "ssd" in parts:
        W.update({"w_ssm": T("w_ssm", [D, 768]), "w_dt": T("w_dt", [D, 4]), "conv_w": T("conv_w4", [128, 4, 4]),
                  "conv_b": T("conv_b4", [128, 4]), "dt_bias": T("dt_bias4", [128, 4]), "a_log": T("a_log4", [128, 4]),
                  "d_skip": T("d_skip4", [128, 4])})
    return W


def emit_M(P, C, banks, bankb, hn_rot, W, hn_load, out_store, parts=("sb", "mla", "ssd")):
    if "sb" in parts:
        emit_sb(P, C, banks, bankb, hn_load, W["w_sb"], out_store, hn_rot)
    if "mla" in parts:
        emit_mla(P, C, banks, bankb, hn_load, W, out_store, hn_rot)
    if "ssd" in parts:
        emit_ssd(P, C, banks, bankb, hn_load, W, out_store, hn_rot)


def build_M(parts=("sb", "mla", "ssd")):
    nc = bass.Bass("TRN2", target_bir_lowering=False)
    P = Prog(nc)
    cdram = P.dram("consts", [128, 896], F32, "ExternalInput")
    hnT_d = P.dram("hnT", [8, 128, L], BF16, "ExternalInput")
    out_d = P.dram("omixT_out", [4, 128, L], BF16, "ExternalOutput")
    out_b = [P.buf(f"omix_out{i}") for i in range(4)]
    P.out_bufs += out_b
    C = setup_consts(P, cdram)
    banks = [P.ps(f"bank{i}", [128, 512], F32) for i in range(8)]
    bankb = [P.buf(f"bank{i}", excl=True) for i in range(8)]
    hn_rot = Rot(P, "hn", [128, 8, 512], BF16, 2)
    W = m_weight_tensors(P, parts)

    def hn_load(hn, hn_b, g):
        b0, nb = QG[g]
        P.dma("sp", hn[:, :, 0:nb * 128], hnT_d[:, :, b0 * 128:(b0 + nb) * 128].rearrange("k p t -> p k t"), writes=[hn_b])

    def out_store(cc, o, t0, n, o_b):
        P.dma("sp", out_d[cc][:, t0:t0 + n], o[:, 0:n], reads=[o_b], pwrites=[out_b[cc]])

    emit_M(P, C, banks, bankb, hn_rot, W, hn_load, out_store, parts)
    P.finish()
    return nc


O_SBQ, O_SBK, O_SBV, O_Z, O_X, O_B, O_C, O_DT, O_CQ, O_CKV, O_KR = 0, 512, 1024, 1536, 2560, 3584, 3840, 4096, 4112, 4496, 4752


def m_inputs(d, i, j):
    w_in = d["w_in"][i]
    m = {"consts": host_consts()}
    m["w_sb"] = np.ascontiguousarray(np.concatenate(
        [w_in[:, O_SBQ + 128 * j:O_SBQ + 128 * j + 128], w_in[:, O_SBK + 128 * j:O_SBK + 128 * j + 128],
         w_in[:, O_SBV + 128 * j:O_SBV + 128 * j + 128]], axis=1))
    z64 = np.zeros((D, 64), np.float32)
    kr = w_in[:, O_KR:O_KR + 32]
    krs = np.concatenate([kr[:, 16:32], kr[:, 0:16]], axis=1)
    m["w_mla_in"] = np.ascontiguousarray(np.concatenate(
        [w_in[:, O_CQ:O_CQ + 384], w_in[:, O_CKV:O_CKV + 256], z64, kr, z64, krs], axis=1))
    wuq = d["w_uq"][i].reshape(384, 8, 96)
    wukv = d["w_ukv"][i].reshape(256, 8, 128)
    A, B = [], []
    for h in (2 * j, 2 * j + 1):
        A.append(wuq[:, h, :])
        r = wuq[:, h, 64:96]
        B.append(np.concatenate([np.zeros((384, 64), np.float32), r[:, 16:32], r[:, 0:16]], axis=1))
    m["w_uqA"] = np.ascontiguousarray(np.concatenate(A, axis=1))
    m["w_uqB"] = np.ascontiguousarray(np.concatenate(B, axis=1))
    m["w_ukvk"] = np.ascontiguousarray(np.concatenate([wukv[:, 2 * j, 0:64], wukv[:, 2 * j + 1, 0:64]], axis=1))
    m["w_ukvv"] = np.ascontiguousarray(np.concatenate([wukv[:, 2 * j, 64:128], wukv[:, 2 * j + 1, 64:128]], axis=1))
    m["g_q"] = np.ascontiguousarray(d["q_norm_g"][i].reshape(3, 128).T)
    m["g_kv"] = np.ascontiguousarray(d["kv_norm_g"][i].reshape(2, 128).T)
    m["rope"] = rope_tables_host()
    grp = j // 2
    m["w_ssm"] = np.ascontiguousarray(np.concatenate(
        [w_in[:, O_Z + 256 * j:O_Z + 256 * j + 256], w_in[:, O_X + 256 * j:O_X + 256 * j + 256],
         w_in[:, O_B + 128 * grp:O_B + 128 * grp + 128], w_in[:, O_C + 128 * grp:O_C + 128 * grp + 128]], axis=1))
    m["w_dt"] = np.ascontiguousarray(w_in[:, O_DT + 4 * j:O_DT + 4 * j + 4])
    ch = np.concatenate([np.arange(256 * j, 256 * j + 256), 1024 + 128 * grp + np.arange(128),
                         1280 + 128 * grp + np.arange(128)])
    m["conv_w4"] = np.ascontiguousarray(d["conv_w"][i][:, ch].reshape(4, 4, 128).transpose(2, 1, 0))
    m["conv_b4"] = np.ascontiguousarray(d["conv_b"][i][ch].reshape(4, 128).T)
    bc4 = lambda v: np.ascontiguousarray(np.broadcast_to(v[4 * j:4 * j + 4], (128, 4)))
    m["dt_bias4"] = bc4(d["dt_bias"][i])
    m["a_log4"] = bc4(d["a_log"][i])
    m["d_skip4"] = bc4(d["d_skip"][i])
    return m


_ROPE = []


def rope_tables_host():
    if not _ROPE:
        pos = (np.arange(L) - 112).astype(np.float32)
        inv = (10000.0 ** (-np.arange(0, 32, 2, dtype=np.float32) / 32)).astype(np.float32)
        ang = (pos[None, :] * inv[:, None]).astype(np.float32)
        c, s_ = np.cos(ang).astype(np.float32), np.sin(ang).astype(np.float32)
        _ROPE.append(np.ascontiguousarray(np.stack([np.concatenate([c, c], 0), np.concatenate([-s_, s_], 0)], 0)))
    return _ROPE[0]


def M_KEYS(parts):
    keys = {"consts", "hnT"}
    if "sb" in parts:
        keys |= {"w_sb"}
    if "mla" in parts:
        keys |= {"w_mla_in", "w_uqA", "w_uqB", "w_ukvk", "w_ukvv", "g_q", "g_kv", "rope"}
    if "ssd" in parts:
        keys |= {"w_ssm", "w_dt", "conv_w4", "conv_b4", "dt_bias4", "a_log4", "d_skip4"}
    return keys


def build_fused(depth=DEPTH):
    nc = bass.Bass("TRN2", target_bir_lowering=False)
    P = Prog(nc)
    cdram = P.dram("consts", [128, 896], F32, "ExternalInput")
    sel_d = P.dram("sel", [128, 4], F32, "ExternalInput")
    h_in = P.dram("h_in", [TB, 128, D], F32, "ExternalInput")
    gnext_d = P.dram("g_next", [depth + 1, 128, D], F32, "ExternalInput")
    Wf = {"w_out": P.dram("w_out", [depth, 2048, D], F32, "ExternalInput"),
          "w_gate": P.dram("w_gate", [depth, D, DFF], F32, "ExternalInput"),
          "w_up": P.dram("w_up", [depth, D, DFF], F32, "ExternalInput"),
          "w_down": P.dram("w_down", [depth, DFF, D], F32, "ExternalInput"),
          "g_ssm": P.dram("g_ssm", [depth, 128, 8], F32, "ExternalInput"),
          "g_ffn": P.dram("g_ffn", [depth, 128, D], F32, "ExternalInput")}
    Wm = m_weight_tensors(P, ("sb", "mla", "ssd"), lead=depth)
    y_d = P.dram("y_out", [TB, 128, D], F32, "ExternalOutput")
    y_b = P.buf("y_out")
    P.out_bufs.append(y_b)

    def scr(name, shape, dtype):
        t = nc.dram_tensor(name, list(shape), dtype)
        return t.ap(), P.buf(name)

    h_scr, h_scr_b = scr("h_scr", [TB, 128, D], F32)
    hn0, hn0_b = scr("hn_blk0", [8, 128, 128], BF16)
    hn_in = [[scr(f"hn_in{l}_{q}", [256, 2048], BF16) for q in range(4)] for l in range(depth)]
    hn_all = [[scr(f"hn_all{l}_{q}", [1024, 2048], BF16) for q in range(4)] for l in range(depth)]
    pcw = [128] + [1024] * 8
    om_in = [[scr(f"om_in{l}_{p}", [512, pcw[p]], BF16) for p in range(9)] for l in range(depth)]
    om_all = [[scr(f"om_all{l}_{p}", [2048, pcw[p]], BF16) for p in range(9)] for l in range(depth)]

    C = setup_consts(P, cdram)
    selt = P.sb("sel", [128, 4], F32)
    sel_b = P.buf("sel")
    P.dma("sp", selt[:], sel_d, writes=[sel_b])
    banks = [P.ps(f"bank{i}", [128, 512], F32) for i in range(8)]
    bankb = [P.buf(f"bank{i}", excl=True) for i in range(8)]

    def make_hn_store(l):
        def hn_store(xT, xT_b):
            P.dma("sp", hn0.rearrange("k p t -> p k t"), xT[:, :, 0:128], reads=[xT_b[0]], writes=[hn0_b])
            for q in range(4):
                ap, b = hn_in[l][q]
                P.dma("sp", ap.rearrange("(k p) t -> p k t", p=128), xT[:, 2 * q:2 * q + 2, 128:TT], reads=xT_b[1:], writes=[b])
            for q in range(4):
                P.allgather(hn_in[l][q][0], hn_all[l][q][0], [hn_in[l][q][1]], [hn_all[l][q][1]])
        return hn_store

    def make_hn_load(l):
        def hn_load(hn, hn_b, g):
            b0, nb = QG[g]
            blk = b0
            first = True
            while blk < b0 + nb:
                off = (blk - b0) * 128
                if blk == 0:
                    P.dma("sp", hn[:, :, off:off + 128], hn0.rearrange("k p t -> p k t"), reads=[hn0_b],
                          writes=[hn_b] if first else (), pwrites=() if first else [hn_b])
                    first = False
                    blk += 1
                    continue
                r, c0 = divmod(blk - 1, 16)
                ln = min(b0 + nb - blk, 16 - c0)
                for q in range(4):
                    ap, b = hn_all[l][q]
                    P.dma("sp", hn[:, 2 * q:2 * q + 2, off:off + ln * 128],
                          ap[r * 256:(r + 1) * 256, c0 * 128:(c0 + ln) * 128].rearrange("(k p) t -> p k t", p=128),
                          reads=[b], writes=[hn_b] if first else (), pwrites=() if first else [hn_b])
                    first = False
                blk += ln
        return hn_load

    def make_out_store(l):
        def out_store(cc, o, t0, n, o_b):
            t = t0
            while t < t0 + n:
                if t < 128:
                    pc, col, ln = 0, t, min(t0 + n, 128) - t
                else:
                    off = t - 128
                    pc, col = 1 + off // 1024, off % 1024
                    ln = min(t0 + n - t, 1024 - col)
                ap, b = om_in[l][pc]
                P.dma("sp", ap[cc * 128:(cc + 1) * 128, col:col + ln], o[:, t - t0:t - t0 + ln], reads=[o_b], pwrites=[b])
                t += ln
        return out_store

    def make_om_load(l):
        def om_extra():
            return Rot(P, "omst", [128, 16, 512], BF16, 2)

        def om_load(o, ob, b0, nb, stage):
            n = nb * 128
            lb0 = max(b0, 1)
            dc0 = (lb0 - b0) * 128
            first = True
            if lb0 < b0 + nb:
                for jc in range(4):
                    st, st_b = stage.next()
                    lb = lb0
                    f2 = True
                    while lb < b0 + nb:
                        off = (lb - 1) * 128
                        pc, col = 1 + 2 * jc + off // 1024, off % 1024
                        ln = min((b0 + nb - lb) * 128, 1024 - col)
                        ap, b = om_all[l][pc]
                        dcol = (lb - b0) * 128
                        P.dma("sp", st[:, :, dcol:dcol + ln], ap[:, col:col + ln].rearrange("(c p) t -> p c t", p=128),
                              reads=[b], writes=[st_b] if f2 else (), pwrites=() if f2 else [st_b])
                        f2 = False
                        lb += ln // 128
                    if jc == 0:
                        P.ts("dve", o[:, :, dc0:n], st[:, :, dc0:n], selt[:, 0:1], ALU.mult, [st_b, sel_b], writes=[ob])
                    else:
                        P.stt(o[:, :, dc0:n], st[:, :, dc0:n], selt[:, jc:jc + 1], o[:, :, dc0:n], ALU.mult, ALU.add,
                              [st_b, sel_b, ob], pwrites=[ob])
                first = False
            if b0 == 0:
                ap, b = om_all[l][0]
                P.dma("sp", o[:, :, 0:128], ap.rearrange("(c p) t -> p c t", p=128), reads=[b],
                      writes=[ob] if first else (), pwrites=() if first else [ob])
        return om_load, om_extra

    emit_F(P, C, banks, bankb, "pre", {"h_src": h_in, "g_next": gnext_d[0], "hn_store": make_hn_store(0)}, tag="p")
    for l in range(depth):
        hn_rot = None
        with contextlib.ExitStack() as esl:
            save = P.es
            P.es = esl
            hn_rot = Rot(P, "hn", [128, 8, 512], BF16, 2)
            W = {k: (v if k == "rope" else v[l]) for k, v in Wm.items()}
            emit_M(P, C, banks, bankb, hn_rot, W, make_hn_load(l), make_out_store(l))
            P.barrier()
            P.es = save
        for p in range(9):
            P.allgather(om_in[l][p][0], om_all[l][p][0], [om_in[l][p][1]], [om_all[l][p][1]])
        last = l == depth - 1
        om_load, om_extra = make_om_load(l)
        io = {"h_src": h_in if l == 0 else h_scr, "h_src_b": () if l == 0 else [h_scr_b], "g_next": gnext_d[l + 1],
              "om_load": om_load, "om_extra": om_extra,
              "w_out": Wf["w_out"][l], "w_gate": Wf["w_gate"][l], "w_up": Wf["w_up"][l], "w_down": Wf["w_down"][l],
              "g_ssm": Wf["g_ssm"][l], "g_ffn": Wf["g_ffn"][l]}
        if last:
            io["y_d"], io["y_b"] = y_d, y_b
        else:
            io["h_dst"], io["h_dst_b"] = h_scr, h_scr_b
            io["hn_store"] = make_hn_store(l + 1)
        emit_F(P, C, banks, bankb, "last" if last else "mid", io, tag=str(l))
    P.finish()
    return nc


WOUT_PERM = np.concatenate([np.concatenate([np.arange(128 * r, 128 * r + 128), 512 + np.arange(256 * r, 256 * r + 256),
                                            1536 + np.arange(128 * r, 128 * r + 128)]) for r in range(4)])
_FUSED = []


def kernel(**inp):
    return _run(DEPTH, inp)


def _run(depth, inp):
    d = {k: np.asarray(v) for k, v in inp.items()}
    x = d["x"].astype(np.float32)
    meta = d["meta_tokens"].astype(np.float32)
    consts = host_consts()
    cores = list(range(8))
    blk0 = np.concatenate([np.zeros((112, D), np.float32), meta], 0)
    if not _FUSED:
        _FUSED.append(build_fused(depth))
    nc = _FUSED[0]
    shared = {
        "consts": consts,
        "g_next": np.ascontiguousarray(np.stack([_bc(d["mix_norm_g"][i]) for i in range(depth)] + [_bc(d["final_norm_g"])], 0)),
        "w_out": np.ascontiguousarray(d["w_out"][:depth][:, WOUT_PERM, :]),
        "w_gate": np.ascontiguousarray(d["w_gate"][:depth]), "w_up": np.ascontiguousarray(d["w_up"][:depth]),
        "w_down": np.ascontiguousarray(d["w_down"][:depth]),
        "g_ssm": np.ascontiguousarray(np.stack([d["ssm_norm_g"][i].reshape(8, 128).T for i in range(depth)], 0)),
        "g_ffn": np.ascontiguousarray(np.stack([_bc(d["ffn_norm_g"][i]) for i in range(depth)], 0)),
    }
    mw = {}
    for j in range(4):
        per = [m_inputs(d, i, j) for i in range(depth)]
        mw[j] = {k: (per[0][k] if k in ("rope",) else np.ascontiguousarray(np.stack([p[k] for p in per], 0)))
                 for k in per[0] if k != "consts"}
    in_maps = []
    for c in cores:
        b, j = divmod(c, 4)
        m = dict(shared)
        m.update(mw[j])
        sel = np.zeros((128, 4), np.float32)
        sel[:, j] = 1.0
        m["sel"] = sel
        m["h_in"] = np.ascontiguousarray(np.concatenate([blk0, x[b, 2048 * j:2048 * (j + 1)]], 0).reshape(TB, 128, D))
        in_maps.append(m)
    res = run_bass_kernel_spmd(nc, in_maps, core_ids=cores).results
    y = np.empty((2, 8192, D), np.float32)
    for c in cores:
        b, j = divmod(c, 4)
        y[b, 2048 * j:2048 * (j + 1)] = res[c]["y_out"].reshape(TT, D)[128:]
    return y


def _bc(v):
    return np.ascontiguousarray(np.broadcast_to(np.asarray(v, np.float32), (128, D)))
```

```python
import contextlib
import numpy as np
import ml_dtypes
import concourse.bass as bass
import concourse.mybir as mybir
from concourse.bass_utils import run_bass_kernel_spmd

F32 = mybir.dt.float32
BF16 = mybir.dt.bfloat16
AF = mybir.ActivationFunctionType
ALU = mybir.AluOpType
NPBF = ml_dtypes.bfloat16

D = 1024
L = 8320
NB = 65
DEPTH = 4
DFF = 2816
NFF = 22
EPS = 1e-6
TB = 17
TT = TB * 128


class Buf:
    __slots__ = ("name", "w", "r", "dsem", "excl")

    def __init__(self, name, excl=False):
        self.name = name
        self.excl = excl
        self.w = {}
        self.r = {}
        self.dsem = None


def _merge(d, s, v):
    if d.get(s, 0) < v:
        d[s] = v


class Prog:
    ENG = ("pe", "act", "dve", "pool", "sp")

    def __init__(self, nc):
        self.nc = nc
        self.es = contextlib.ExitStack()
        self.root_es = contextlib.ExitStack()
        self.prog = {k: [] for k in self.ENG}
        self.sem = {}
        self.cnt = {}
        self.seen = {k: {} for k in self.ENG}
        self.nbuf = 0
        for k in ("pe", "act", "dve", "pool"):
            self.newsem("c_" + k)
        self.out_bufs = []
        self.free_d = []
        self.live = []

    def newsem(self, name):
        self.sem[name] = self.root_es.enter_context(self.nc.semaphore(name))
        self.cnt[name] = 0

    def buf(self, name=None, excl=False):
        self.nbuf += 1
        return Buf(f"{name or 'b'}_{self.nbuf}", excl)

    def sb(self, name, shape, dtype):
        self.nbuf += 1
        return self.es.enter_context(self.nc.sbuf_tensor(f"{name}_{self.nbuf}", list(shape), dtype))

    def ps(self, name, shape, dtype):
        self.nbuf += 1
        return self.es.enter_context(self.nc.psum_tensor(f"{name}_{self.nbuf}", list(shape), dtype))

    def dram(self, name, shape, dtype, kind):
        return self.nc.dram_tensor(name, list(shape), dtype, kind=kind).ap()

    def _deps(self, reads, writes, pwrites):
        deps = {}
        for b in reads:
            for s, v in b.w.items():
                _merge(deps, s, v)
        for b in writes:
            for s, v in b.w.items():
                _merge(deps, s, v)
            for s, v in b.r.items():
                _merge(deps, s, v)
        for b in pwrites:
            for s, v in b.r.items():
                _merge(deps, s, v)
        return deps

    def _waits(self, eng, deps, is_dma):
        own = "c_" + eng
        seen = self.seen[eng]
        for s, v in deps.items():
            if s == own and eng == "pe" and not is_dma:
                continue
            if s.startswith("d_"):
                v = self.cnt[s]
            if seen.get(s, 0) >= v:
                continue
            seen[s] = v
            h = self.sem[s]
            self.prog[eng].append(lambda e, h=h, v=v: e.wait_ge(h, v))

    def _mark(self, tok, reads, writes, pwrites):
        s, v = tok
        for b in reads:
            _merge(b.r, s, v)
        for b in writes:
            b.w = {s: v}
            b.r = {}
        for b in pwrites:
            _merge(b.w, s, v)

    def op(self, eng, fn, reads=(), writes=(), pwrites=()):
        if any(b.excl for b in reads):
            writes = tuple(writes) + tuple(b for b in reads if b.excl)
            reads = tuple(b for b in reads if not b.excl)
        self._waits(eng, self._deps(reads, writes, pwrites), False)
        s = "c_" + eng
        self.cnt[s] += 1
        v = self.cnt[s]
        h = self.sem[s]
        self.prog[eng].append(lambda e, fn=fn, h=h: fn(e).then_inc(h, 1))
        self._mark((s, v), reads, writes, pwrites)

    def dma(self, q, out, in_, reads=(), writes=(), pwrites=(), **kw):
        self._waits(q, self._deps(reads, writes, pwrites), True)
        b = (tuple(writes) + tuple(pwrites))[0]
        if b.dsem is None:
            if self.free_d:
                b.dsem = self.free_d.pop()
            else:
                b.dsem = f"d_{len(self.sem)}"
                self.newsem(b.dsem)
            self.live.append(b)
        s = b.dsem
        self.cnt[s] += 16
        v = self.cnt[s]
        h = self.sem[s]
        self.prog[q].append(lambda e, h=h: e.dma_start(out=out, in_=in_, **kw).then_inc(h, 16))
        self._mark((s, v), reads, writes, pwrites)

    def allgather(self, in_ap, out_ap, reads, writes):
        self._waits("pool", self._deps(reads, writes, ()), True)
        s = "d_cc"
        if s not in self.sem:
            self.newsem(s)
        self.cnt[s] += 1
        v = self.cnt[s]
        h = self.sem[s]
        self.prog["pool"].append(lambda e, h=h: e.collective_compute(
            "AllGather", ALU.bypass, replica_groups=[[0, 1, 2, 3], [4, 5, 6, 7]], ins=[in_ap.opt()], outs=[out_ap.opt()]).then_inc(h))
        self._mark((s, v), reads, writes, ())

    def barrier(self):
        allc = {("c_" + k): self.cnt["c_" + k] for k in ("pe", "act", "dve", "pool")}
        for s in self.cnt:
            if s.startswith("d_") and s != "d_cc":
                allc[s] = self.cnt[s]
        for eng in self.ENG:
            self._waits(eng, {s: v for s, v in allc.items() if v > 0}, eng == "sp")
        for b in self.live:
            if b.dsem not in self.free_d:
                self.free_d.append(b.dsem)
            b.dsem = None
        self.live = []

    def mm(self, out, lhsT, rhs, start, stop, reads, writes):
        self.op("pe", lambda e: e.matmul(out, lhsT=lhsT, rhs=rhs, start=start, stop=stop), reads, writes)

    def tr(self, out, in_, ident, reads, writes=(), pwrites=()):
        self.op("pe", lambda e: e.transpose(out, in_, ident), reads, writes, pwrites)

    def act(self, out, in_, func, reads, writes=(), pwrites=(), scale=1.0, bias=0.0, accum_out=None):
        if accum_out is None:
            self.op("act", lambda e: e.activation(out=out, in_=in_, func=func, scale=scale, bias=bias),
                    reads, writes, pwrites)
        else:
            self.op("act", lambda e: e.activation(out=out, in_=in_, func=func, scale=scale, bias=bias,
                                                  accum_out=accum_out), reads, writes, pwrites)

    def tt(self, eng, out, in0, in1, op, reads, writes=(), pwrites=()):
        self.op(eng, lambda e: e.tensor_tensor(out=out, in0=in0, in1=in1, op=op), reads, writes, pwrites)

    def ts(self, eng, out, in0, s1, op0, reads, writes=(), pwrites=(), s2=None, op1=None):
        if op1 is None:
            self.op(eng, lambda e: e.tensor_scalar(out=out, in0=in0, scalar1=s1, scalar2=None, op0=op0),
                    reads, writes, pwrites)
        else:
            self.op(eng, lambda e: e.tensor_scalar(out=out, in0=in0, scalar1=s1, scalar2=s2, op0=op0, op1=op1),
                    reads, writes, pwrites)

    def stt(self, out, in0, scalar, in1, op0, op1, reads, writes=(), pwrites=()):
        self.op("dve", lambda e: e.scalar_tensor_tensor(out=out, in0=in0, scalar=scalar, in1=in1, op0=op0, op1=op1),
                reads, writes, pwrites)

    def copy(self, eng, out, in_, reads, writes=(), pwrites=()):
        if eng == "act":
            self.act(out, in_, AF.Copy, reads, writes, pwrites)
        else:
            self.op(eng, lambda e: e.tensor_copy(out=out, in_=in_), reads, writes, pwrites)

    def memset(self, eng, ap, val, writes=(), pwrites=()):
        self.op(eng, lambda e: e.memset(ap, val), (), writes, pwrites)

    def recip(self, out, in_, reads, writes=(), pwrites=()):
        self.op("dve", lambda e: e.reciprocal(out=out, in_=in_), reads, writes, pwrites)

    def finish(self):
        deps = {}
        for b in self.out_bufs:
            for s, v in b.w.items():
                _merge(deps, s, v)
        self._waits("sp", deps, True)
        nc = self.nc
        with nc.Block() as block:
            @block.sync
            def _(e):
                for f in self.prog["sp"]:
                    f(e)

            @block.tensor
            def _(e):
                for f in self.prog["pe"]:
                    f(e)

            @block.scalar
            def _(e):
                for f in self.prog["act"]:
                    f(e)

            @block.vector
            def _(e):
                for f in self.prog["dve"]:
                    f(e)

            @block.gpsimd
            def _(e):
                for f in self.prog["pool"]:
                    f(e)
        self.es.close()
        self.root_es.close()


class Deferred:
    _DEFER = {"op", "dma", "mm", "tr", "act", "tt", "ts", "stt", "copy", "memset", "recip", "barrier", "allgather"}

    def __init__(self, P):
        self.__dict__["P"] = P
        self.__dict__["ops"] = []

    def __getattr__(self, name):
        a = getattr(self.P, name)
        if name in self._DEFER:
            ops = self.ops

            def f(*args, **kw):
                ops.append((a, args, kw))
            return f
        return a

    def __setattr__(self, name, val):
        setattr(self.P, name, val)


def replay_interleaved(A, B):
    na, nb_ = len(A.ops), len(B.ops)
    j = 0
    for i, (f, a, k) in enumerate(A.ops):
        f(*a, **k)
        tgt = (i + 1) * nb_ // max(na, 1)
        while j < tgt:
            g, a2, k2 = B.ops[j]
            g(*a2, **k2)
            j += 1
    while j < nb_:
        g, a2, k2 = B.ops[j]
        g(*a2, **k2)
        j += 1


@contextlib.contextmanager
def phase_scope(P, scoped):
    if not scoped:
        yield
        return
    with contextlib.ExitStack() as es2:
        save = P.es
        P.es = es2
        yield
        P.barrier()
        P.es = save


class Rot:
    def __init__(self, P, name, shape, dtype, n):
        self.t = [P.sb(f"{name}{i}", shape, dtype) for i in range(n)]
        self.b = [P.buf(f"{name}{i}") for i in range(n)]
        self.i = 0
        self.n = n

    def next(self):
        i = self.i
        self.i = (i + 1) % self.n
        return self.t[i], self.b[i]


def setup_consts(P, cdram):
    C = {}
    c32 = P.sb("c32", [128, 1024], F32)
    b32 = P.buf("c32")
    P.dma("sp", c32[:], cdram, writes=[b32])
    C["c32"], C["b32"] = c32, b32
    cbf = P.sb("cbf", [128, 1024], BF16)
    bbf = P.buf("cbf")
    P.copy("dve", cbf[:], c32[:], [b32], [bbf])
    C["cbf"], C["bbf"] = cbf, bbf
    C["ident32"] = c32[:, 0:128]
    C["tri32"] = c32[:, 128:256]
    C["negmask32"] = c32[:, 512:640]
    C["ones32"] = c32[:, 640:768]
    C["ident"] = cbf[:, 0:128]
    C["strict"] = cbf[:, 256:384]
    C["U"] = cbf[:, 384:512]
    C["ones"] = cbf[:, 640:768]
    C["strict32"] = c32[:, 256:384]
    C["validcol"] = c32[:, 768:769]
    C["ntri32"] = c32[:, 896:1024]
    C["le"] = cbf[:, 128:256]
    return C


def host_consts():
    i = np.arange(128)
    ident = (i[:, None] == i[None, :]).astype(np.float32)
    tri = (i[:, None] <= i[None, :]).astype(np.float32)
    strict = (i[:, None] < i[None, :]).astype(np.float32)
    U = (i[:, None] >= i[None, :]).astype(np.float32)
    negmask = np.where(i[None, :] < i[:, None], -30000.0, 0.0).astype(np.float32)
    ones = np.ones((128, 128), np.float32)
    le = (i[:, None] <= i[None, :]).astype(np.float32)
    valid = np.broadcast_to((i >= 112).astype(np.float32)[:, None], (128, 128))
    return np.ascontiguousarray(np.concatenate([ident, tri, strict, U, negmask, ones, valid, -tri], axis=1))


def rms_rows(P, C, h_ap, hb, g_bc, gb, out_bf, out_b, small):
    junk, junk_b, ssq, ssq_b, sd, sd_b, rs, rs_b = small
    P.act(junk[:], h_ap, AF.Square, [hb], [junk_b, ssq_b], accum_out=ssq[:])
    P.act(sd[:], ssq[:], AF.Sqrt, [ssq_b], [sd_b], scale=1.0 / D, bias=EPS)
    P.recip(rs[:], sd[:], [sd_b], [rs_b])
    P.stt(out_bf, h_ap, rs[:, 0:1], g_bc, ALU.mult, ALU.mult, [hb, rs_b, gb], [out_b])


SSM_POS = [4 * r + 1 + i for r in range(4) for i in range(2)]
FGROUPS = [(0, 4), (4, 4), (8, 4), (12, 4), (16, 1)]


def emit_F(P, C, banks, bankb, variant, io, tag=""):
    do_mix = variant != "pre"
    last = variant == "last"
    groups = FGROUPS
    with contextlib.ExitStack() as es1:
        save1 = P.es
        P.es = es1
        h = P.sb("h", [128, TB, D], F32)
        hb = [P.buf(f"h{i}{tag}") for i in range(TB)]
        for i in range(TB):
            P.dma("sp", h[:, i, :], io["h_src"][i], reads=io.get("h_src_b", ()), writes=[hb[i]])
        gn = P.sb("gn", [128, D], F32)
        gn_b = P.buf("gn" + tag)
        P.dma("sp", gn[:], io["g_next"], writes=[gn_b])
        junk = P.sb("junk", [128, D], BF16)
        small = (junk, P.buf("junk"), P.sb("ssq", [128, 1], F32), P.buf("ssq"), P.sb("sd", [128, 1], F32), P.buf("sd"),
                 P.sb("rs", [128, 1], F32), P.buf("rs"))
        XT = {}

        def alloc_xT():
            XT["xT"] = P.sb("xT", [128, 8, TT], BF16)
            XT["xT_b"] = [P.buf(f"xT{i}{tag}") for i in range(TB)]
            XT["nrm"] = Rot(P, "nrm", [128, D], BF16, 2)

        def norm_T(g_tile, g_b, blk, bank_i):
            xT, xT_b, nrm = XT["xT"], XT["xT_b"], XT["nrm"]
            t, b = nrm.next()
            rms_rows(P, C, h[:, blk, :], hb[blk], g_tile[:], g_b, t[:], b, small)
            pst = banks[bank_i][:].bitcast(BF16)
            for k in range(8):
                P.tr(pst[:, k * 128:(k + 1) * 128], t[:, k * 128:(k + 1) * 128], C["ident"],
                     [b, C["bbf"]], [bankb[bank_i]])
            P.copy("act" if blk % 2 else "dve", xT[:, :, blk * 128:(blk + 1) * 128],
                   pst.rearrange("p (k t) -> p k t", k=8), [bankb[bank_i]], [xT_b[blk]])

        if do_mix:
            with contextlib.ExitStack() as es2:
                P.es = es2
                gssm = P.sb("gssm", [128, 8], F32)
                gssm_b = P.buf("gssm" + tag)
                P.dma("sp", gssm[:], io["g_ssm"], writes=[gssm_b])
                wout = P.sb("wout", [128, 16, D], BF16)
                wout_b = [P.buf(f"wout{i}{tag}") for i in range(4)]
                stg = Rot(P, "stgo" + tag, [128, 2, D], F32, 2)
                wv = io["w_out"].rearrange("(c p) f -> p c f", p=128)
                for i in range(8):
                    t, b = stg.next()
                    P.dma("sp", t[:], wv[:, 2 * i:2 * i + 2, :], writes=[b])
                    P.copy("pool", wout[:, 2 * i:2 * i + 2, :], t[:], [b], pwrites=[wout_b[i // 2]])
                om = Rot(P, "om" + tag, [128, 16, 512], BF16, 2)
                sq = Rot(P, "sq" + tag, [128, 512], BF16, 3)
                sdt = P.sb("sdt", [128, 512], F32)
                sdt_b = P.buf("sdt")
                rst = P.sb("rst", [128, 512], F32)
                rst_b = P.buf("rst")
                omx = io.get("om_extra")
                if omx:
                    omx = omx()
                for (b0, nb) in groups:
                    n = nb * 128
                    o, ob = om.next()
                    io["om_load"](o, ob, b0, nb, omx)
                    for c in range(8):
                        s_, s_b = sq.next()
                        P.act(s_[:, 0:n], o[:, SSM_POS[c], 0:n], AF.Square, [ob], [s_b])
                        P.mm(banks[0][:, 0:n], C["ones"], s_[:, 0:n], c == 0, c == 7, [s_b, C["bbf"]], [bankb[0]])
                    P.act(sdt[:, 0:n], banks[0][:, 0:n], AF.Sqrt, [bankb[0]], [sdt_b], scale=1.0 / D, bias=EPS)
                    P.recip(rst[:, 0:n], sdt[:, 0:n], [sdt_b], [rst_b])
                    for c in range(8):
                        P.stt(o[:, SSM_POS[c], 0:n], o[:, SSM_POS[c], 0:n], gssm[:, c:c + 1], rst[:, 0:n], ALU.mult, ALU.mult,
                              [ob, gssm_b, rst_b], pwrites=[ob])
                    for bi in range(nb):
                        blk = b0 + bi
                        for oc in range(2):
                            bk = 1 + (2 * blk + oc) % 4
                            for c in range(16):
                                P.mm(banks[bk][:], o[:, c, bi * 128:(bi + 1) * 128], wout[:, c, oc * 512:(oc + 1) * 512],
                                     c == 0, c == 15, [ob, wout_b[c // 4]], [bankb[bk]])
                            P.tt("dve", h[:, blk, oc * 512:(oc + 1) * 512], h[:, blk, oc * 512:(oc + 1) * 512],
                                 banks[bk][:], ALU.add, [bankb[bk], hb[blk]], pwrites=[hb[blk]])
                P.barrier()
                P.es = es1
            alloc_xT()
            xT, xT_b = XT["xT"], XT["xT_b"]
            with contextlib.ExitStack() as es2:
                P.es = es2
                gffn = P.sb("gffn", [128, D], F32)
                gffn_b = P.buf("gffn" + tag)
                P.dma("sp", gffn[:], io["g_ffn"], writes=[gffn_b])
                for blk in range(TB):
                    norm_T(gffn, gffn_b, blk, blk % 2)
                stg_g = Rot(P, "stg_g" + tag, [128, 8, 128], F32, 2)
                stg_u = Rot(P, "stg_u" + tag, [128, 8, 128], F32, 2)
                stg_d = Rot(P, "stg_d" + tag, [128, D], F32, 2)
                wgp = Rot(P, "wgp" + tag, [128, 8, 128], BF16, 2)
                wup = Rot(P, "wup" + tag, [128, 8, 128], BF16, 2)
                wdp = Rot(P, "wdp" + tag, [128, D], BF16, 2)
                sg = Rot(P, "sg" + tag, [128, 512], F32, 2)
                actT = Rot(P, "actT" + tag, [128, 512], BF16, 2)
                wgv = io["w_gate"].rearrange("(k p) f -> p k f", p=128)
                wuv = io["w_up"].rearrange("(k p) f -> p k f", p=128)
                wd_d = io["w_down"]
                mmi = 0
                for f in range(NFF):
                    tg, tg_b = stg_g.next()
                    tu, tu_b = stg_u.next()
                    td, td_b = stg_d.next()
                    P.dma("sp", tg[:], wgv[:, :, f * 128:(f + 1) * 128], writes=[tg_b])
                    P.dma("sp", tu[:], wuv[:, :, f * 128:(f + 1) * 128], writes=[tu_b])
                    P.dma("sp", td[:], wd_d[f * 128:(f + 1) * 128, :], writes=[td_b])
                    wgt, wg_b = wgp.next()
                    wut, wu_b = wup.next()
                    wdt, wd_b = wdp.next()
                    P.copy("pool", wgt[:], tg[:], [tg_b], [wg_b])
                    P.copy("pool", wut[:], tu[:], [tu_b], [wu_b])
                    P.copy("pool", wdt[:], td[:], [td_b], [wd_b])
                    for (b0, nb) in groups:
                        n = nb * 128
                        t0 = b0 * 128
                        rb = [xT_b[b0 + i] for i in range(nb)]
                        for k in range(8):
                            P.mm(banks[0][:, 0:n], wgt[:, k, :], xT[:, k, t0:t0 + n], k == 0, k == 7, rb + [wg_b], [bankb[0]])
                        for k in range(8):
                            P.mm(banks[1][:, 0:n], wut[:, k, :], xT[:, k, t0:t0 + n], k == 0, k == 7, rb + [wu_b], [bankb[1]])
                        s_, s_b = sg.next()
                        P.act(s_[:, 0:n], banks[0][:, 0:n], AF.Silu, [bankb[0]], [s_b])
                        a, a_b = actT.next()
                        P.tt("dve", a[:, 0:n], s_[:, 0:n], banks[1][:, 0:n], ALU.mult, [s_b, bankb[1]], [a_b])
                        for bi in range(nb):
                            blk = b0 + bi
                            for oc in range(2):
                                bk = 2 + mmi % 6
                                mmi += 1
                                P.mm(banks[bk][:], a[:, bi * 128:(bi + 1) * 128], wdt[:, oc * 512:(oc + 1) * 512],
                                     True, True, [a_b, wd_b], [bankb[bk]])
                                P.tt("dve", h[:, blk, oc * 512:(oc + 1) * 512], h[:, blk, oc * 512:(oc + 1) * 512],
                                     banks[bk][:], ALU.add, [bankb[bk], hb[blk]], pwrites=[hb[blk]])
                P.barrier()
                P.es = es1

        if not do_mix:
            alloc_xT()
        xT, xT_b = XT["xT"], XT["xT_b"]
        if last:
            yo = Rot(P, "yo" + tag, [128, D], F32, 2)
            junk, junk_b, ssq, ssq_b, sd, sd_b, rs, rs_b = small
            for blk in range(TB):
                t, b = yo.next()
                P.act(junk[:], h[:, blk, :], AF.Square, [hb[blk]], [junk_b, ssq_b], accum_out=ssq[:])
                P.act(sd[:], ssq[:], AF.Sqrt, [ssq_b], [sd_b], scale=1.0 / D, bias=EPS)
                P.recip(rs[:], sd[:], [sd_b], [rs_b])
                P.stt(t[:], h[:, blk, :], rs[:, 0:1], gn[:], ALU.mult, ALU.mult, [hb[blk], rs_b, gn_b], [b])
                P.dma("sp", io["y_d"][blk], t[:], reads=[b], pwrites=[io["y_b"]])
        else:
            for blk in range(TB):
                norm_T(gn, gn_b, blk, blk % 2)
                if do_mix:
                    P.dma("sp", io["h_dst"][blk], h[:, blk, :], reads=[hb[blk]], pwrites=[io["h_dst_b"]])
            io["hn_store"](xT, xT_b)
        P.barrier()
        P.es = save1


def build_F(variant):
    nc = bass.Bass("TRN2", target_bir_lowering=False)
    P = Prog(nc)
    do_mix = variant != "pre"
    last = variant == "last"
    cdram = P.dram("consts", [128, 1024], F32, "ExternalInput")
    io = {"h_src": P.dram("h_in", [TB, 128, D], F32, "ExternalInput"), "g_next": P.dram("g_next", [128, D], F32, "ExternalInput")}
    if do_mix:
        om_d = P.dram("omixT", [16, 128, TT], BF16, "ExternalInput")
        io.update({"w_out": P.dram("w_out", [2048, D], F32, "ExternalInput"), "w_gate": P.dram("w_gate", [D, DFF], F32, "ExternalInput"),
                   "w_up": P.dram("w_up", [D, DFF], F32, "ExternalInput"), "w_down": P.dram("w_down", [DFF, D], F32, "ExternalInput"),
                   "g_ssm": P.dram("g_ssm", [128, 8], F32, "ExternalInput"), "g_ffn": P.dram("g_ffn", [128, D], F32, "ExternalInput")})

        def om_load(o, ob, b0, nb, omx):
            P.dma("sp", o[:, :, 0:nb * 128], om_d[:, :, b0 * 128:(b0 + nb) * 128].rearrange("c p t -> p c t"), writes=[ob])
        io["om_load"] = om_load
    if last:
        io["y_d"] = P.dram("y_out", [TB, 128, D], F32, "ExternalOutput")
        io["y_b"] = P.buf("y_out")
        P.out_bufs.append(io["y_b"])
    else:
        hnT_d = P.dram("hnT_out", [8, 128, TT], BF16, "ExternalOutput")
        hnT_db = P.buf("hnT_out")
        P.out_bufs.append(hnT_db)

        def hn_store(xT, xT_b):
            P.dma("sp", hnT_d.rearrange("k p t -> p k t"), xT[:], reads=xT_b, pwrites=[hnT_db])
        io["hn_store"] = hn_store
        if do_mix:
            io["h_dst"] = P.dram("h_out", [TB, 128, D], F32, "ExternalOutput")
            io["h_dst_b"] = P.buf("h_out")
            P.out_bufs.append(io["h_dst_b"])
    C = setup_consts(P, cdram)
    banks = [P.ps(f"bank{i}", [128, 512], F32) for i in range(8)]
    bankb = [P.buf(f"bank{i}", excl=True) for i in range(8)]
    emit_F(P, C, banks, bankb, variant, io)
    P.finish()
    return nc


QG = [(4 * g, min(4, NB - 4 * g)) for g in range(17)]


def load_cast(P, dst_ap, dst_b, src_ap, shape, name, eng="pool", pw=False):
    t = P.sb(name + "_stg", shape, F32)
    b = P.buf(name + "_stg")
    P.dma("sp", t[:], src_ap, writes=[b])
    if pw:
        P.copy(eng, dst_ap, t[:], [b], pwrites=[dst_b])
    else:
        P.copy(eng, dst_ap, t[:], [b], [dst_b])
    return t, b


SB_BANKS_FULL = dict(z=(0, 1), c=((2, 3), (4, 5)), o=(6, 7))
SB_BANKS_SHARED = dict(z=(0, 1), c=((2, 3), (2, 3)), o=(4, 4))


def emit_sb(P, C, banks, bankb, hn_load, wsb_d, out_store, hn_rot, scoped=True, bk=SB_BANKS_FULL):
    import os
    with phase_scope(P, scoped):
        w = P.sb("sb_w", [128, 8, 384], BF16)
        w_b = P.buf("sb_w")
        with contextlib.ExitStack() as es3:
            prev_es = P.es
            P.es = es3
            load_cast(P, w[:], w_b, wsb_d.rearrange("(k p) f -> p k f", p=128), [128, 8, 384], "sb_w")
            P.barrier()
            P.es = prev_es
        zer = P.sb("sb_zero", [128, 512], BF16)
        zer_b = P.buf("sb_zero")
        P.memset("dve", zer[:], 0.0, [zer_b])
        qT = P.sb("sb_qT", [128, L], BF16)
        kT = P.sb("sb_kT", [128, L], BF16)
        vE = P.sb("sb_vE", [128, NB, 128], BF16)
        vO = P.sb("sb_vO", [128, NB, 128], BF16)
        qT_b = [P.buf(f"sb_qT{g}") for g in range(17)]
        kT_b = [P.buf(f"sb_kT{g}") for g in range(17)]
        v_b = [P.buf(f"sb_v{g}") for g in range(17)]
        P.memset("dve", vE[:].rearrange("p b f -> p (b f)"), 0.0, pwrites=v_b)
        P.memset("pool", vO[:].rearrange("p b f -> p (b f)"), 0.0, pwrites=v_b)
        for g, (b0, nb) in enumerate(QG[:int(os.environ.get("SB_NPROJ", "17"))]):
            n = nb * 128
            t0 = b0 * 128
            hn, hn_b = hn_rot.next()
            hn_load(P, hn, hn_b, g)
            for k in range(8):
                P.mm(banks[0][:, 0:n], w[:, k, 0:128], hn[:, k, 0:n], k == 0, k == 7, [w_b, hn_b], [bankb[0]])
            P.copy("act", qT[:, t0:t0 + n], banks[0][:, 0:n], [bankb[0]], [qT_b[g]])
            for k in range(8):
                P.mm(banks[1][:, 0:n], w[:, k, 128:256], hn[:, k, 0:n], k == 0, k == 7, [w_b, hn_b], [bankb[1]])
            P.copy("dve", kT[:, t0:t0 + n], banks[1][:, 0:n], [bankb[1]], [kT_b[g]])
            for bi in range(nb):
                for k in range(8):
                    P.mm(banks[2][:, bi * 128:(bi + 1) * 128], hn[:, k, bi * 128:(bi + 1) * 128], w[:, k, 256:384],
                         k == 0, k == 7, [w_b, hn_b], [bankb[2]])
            pv = banks[2][:, 0:n].rearrange("p (b f) -> p b f", f=128)
            P.copy("act", vE[:, b0:b0 + nb, 0:64], pv[:, :, 0:64], [bankb[2]], pwrites=[v_b[g]])
            P.copy("dve", vO[:, b0:b0 + nb, 64:128], pv[:, :, 64:128], [bankb[2]], pwrites=[v_b[g]])
        e_r = Rot(P, "sb_e", [128, 512], F32, 3)
        sp_r = Rot(P, "sb_sp", [128, 512], BF16, 4)
        x_r = Rot(P, "sb_x", [128, 512], F32, 2)
        a_r = Rot(P, "sb_a", [128, 512], BF16, 4)
        o_r = Rot(P, "sb_o", [128, 512], BF16, 2)
        steps = []
        for g, (b0, nb) in enumerate(QG):
            gend = b0 + nb
            first = True
            for kb in range(gend - 1, -1, -1):
                for hd in range(2):
                    qb0 = max(kb, b0)
                    steps.append(dict(g=g, hd=hd, kb=kb, qs=qb0 * 128, n=(gend - qb0) * 128, co=(qb0 - b0) * 128,
                                      diag=kb >= b0, first=(kb == gend - 1), firstg=(kb == gend - 1 and hd == 0),
                                      lastg=(kb == 0 and hd == 1), ncols=nb * 128, t0=b0 * 128))
        import os
        if os.environ.get("SB_MAXG"):
            steps = [st for st in steps if st["g"] < int(os.environ["SB_MAXG"])]
        S = len(steps)
        for st in steps:
            st["e"] = st["sp"] = st["x"] = st["a"] = None

        def stageA(i):
            st = steps[i]
            hs = slice(64 * st["hd"], 64 * st["hd"] + 64)
            n, qs, kb = st["n"], st["qs"], st["kb"]
            zb = bk["z"][i % 2]
            P.mm(banks[zb][:, 0:n], kT[hs, kb * 128:(kb + 1) * 128], qT[hs, qs:qs + n], True, True,
                 [kT_b[kb // 4], qT_b[st["g"]]], [bankb[zb]])
            e, e_b = e_r.next()
            P.act(e[:, 0:n], banks[zb][:, 0:n], AF.Exp, [bankb[zb]], [e_b], scale=0.125)
            if st["diag"]:
                P.tt("dve", e[:, 0:128], e[:, 0:128], C["strict32"], ALU.mult, [e_b, C["b32"]], pwrites=[e_b])
            sp, sp_b = sp_r.next()
            P.act(sp[:, 0:n], e[:, 0:n], AF.Ln, [e_b], [sp_b], bias=1.0)
            st["e"], st["sp"] = (e, e_b), (sp, sp_b)

        def cbank(st):
            return bk["c"][st["g"] % 2][st["hd"]]

        def stageB(i):
            st = steps[i]
            n, co = st["n"], st["co"]
            cb = cbank(st)
            if st["first"]:
                P.mm(banks[cb][:, 0:st["ncols"]], zer[:, 0:128], zer[:, 0:st["ncols"]], True, True, [zer_b], [bankb[cb]])
            sp, sp_b = st["sp"]
            P.mm(banks[cb][:, co:co + n], C["U"], sp[:, 0:n], False, True, [sp_b, C["bbf"]], [bankb[cb]])
            x, x_b = x_r.next()
            P.act(x[:, 0:n], banks[cb][:, co:co + n], AF.Exp, [bankb[cb]], [x_b], scale=-1.0)
            e, e_b = st["e"]
            a, a_b = a_r.next()
            P.tt("dve", a[:, 0:n], e[:, 0:n], x[:, 0:n], ALU.mult, [e_b, x_b], [a_b])
            st["a"] = (a, a_b)

        def stageC(i):
            st = steps[i]
            n, co = st["n"], st["co"]
            cb = cbank(st)
            sp, sp_b = st["sp"]
            P.mm(banks[cb][:, co:co + n], C["strict"], sp[:, 0:n], False, True, [sp_b, C["bbf"]], [bankb[cb]])

        def stageD(i):
            st = steps[i]
            n, co, kb, g = st["n"], st["co"], st["kb"], st["g"]
            ob = bk["o"][g % 2]
            if st["firstg"]:
                P.mm(banks[ob][:, 0:st["ncols"]], zer[:, 0:128], zer[:, 0:st["ncols"]], True, True, [zer_b], [bankb[ob]])
            a, a_b = st["a"]
            vv = vE if st["hd"] == 0 else vO
            P.mm(banks[ob][:, co:co + n], vv[:, kb, :], a[:, 0:n], False, True, [a_b, v_b[kb // 4]], [bankb[ob]])
            if st["lastg"]:
                nc_ = st["ncols"]
                o, o_b = o_r.next()
                P.copy("act", o[:, 0:nc_], banks[ob][:, 0:nc_], [bankb[ob]], [o_b])
                out_store(P, 0, o, st["t0"], nc_, o_b)

        if S:
            stageA(0)
        for i in range(S + 2):
            if i + 1 < S:
                stageA(i + 1)
            if i < S:
                stageB(i)
            if 0 <= i - 1 < S:
                stageC(i - 1)
            if 0 <= i - 2 < S:
                stageD(i - 2)


def emit_mla(P, C, banks, bankb, hn_load, W, out_store, hn_rot):
    SC = 1.0 / np.sqrt(96.0)
    with contextlib.ExitStack() as es2:
        save_es = P.es
        P.es = es2
        wm = P.sb("ml_wm", [128, 8, 832], BF16)
        wm_b = P.buf("ml_wm")
        wA = P.sb("ml_wA", [128, 3, 192], BF16)
        wB = P.sb("ml_wB", [128, 3, 192], BF16)
        wK = P.sb("ml_wK", [128, 2, 128], BF16)
        wV = P.sb("ml_wV", [128, 2, 128], BF16)
        wu_b = P.buf("ml_wu")
        gq = P.sb("ml_gq", [128, 3], F32)
        gkv = P.sb("ml_gkv", [128, 2], F32)
        g_b = P.buf("ml_g")
        P.dma("sp", gq[:], W["g_q"], pwrites=[g_b])
        P.dma("sp", gkv[:], W["g_kv"], pwrites=[g_b])
        with contextlib.ExitStack() as es3:
            P.es = es3
            stg = Rot(P, "ml_stg", [128, 2, 832], F32, 2)
            wv_ = W["w_mla_in"].rearrange("(k p) f -> p k f", p=128)
            for i in range(4):
                t, b = stg.next()
                P.dma("sp", t[:], wv_[:, 2 * i:2 * i + 2, :], writes=[b])
                P.copy("pool", wm[:, 2 * i:2 * i + 2, :], t[:], [b], pwrites=[wm_b])
            sA = P.sb("ml_sA", [128, 3, 192], F32)
            sB = P.sb("ml_sB", [128, 3, 192], F32)
            sK = P.sb("ml_sK", [128, 2, 128], F32)
            sV = P.sb("ml_sV", [128, 2, 128], F32)
            s_b = P.buf("ml_s")
            P.dma("sp", sA[:], W["w_uqA"].rearrange("(k p) f -> p k f", p=128), pwrites=[s_b])
            P.dma("sp", sB[:], W["w_uqB"].rearrange("(k p) f -> p k f", p=128), pwrites=[s_b])
            P.dma("sp", sK[:], W["w_ukvk"].rearrange("(k p) f -> p k f", p=128), pwrites=[s_b])
            P.dma("sp", sV[:], W["w_ukvv"].rearrange("(k p) f -> p k f", p=128), pwrites=[s_b])
            for c in range(3):
                P.ts("dve", wA[:, c, :], sA[:, c, :], gq[:, c:c + 1], ALU.mult, [s_b, g_b], pwrites=[wu_b])
                P.ts("dve", wB[:, c, :], sB[:, c, :], gq[:, c:c + 1], ALU.mult, [s_b, g_b], pwrites=[wu_b])
            for c in range(2):
                P.ts("dve", wK[:, c, :], sK[:, c, :], gkv[:, c:c + 1], ALU.mult, [s_b, g_b], pwrites=[wu_b])
                P.ts("dve", wV[:, c, :], sV[:, c, :], gkv[:, c:c + 1], ALU.mult, [s_b, g_b], pwrites=[wu_b])
            P.barrier()
            P.es = es2
        qT = P.sb("ml_qT", [128, 2, L], BF16)
        kT = P.sb("ml_kT", [128, 2, L], BF16)
        va = P.sb("ml_va", [128, NB, 2, 128], BF16)
        qT_b = [P.buf(f"ml_qT{g}") for g in range(17)]
        kT_b = [P.buf(f"ml_kT{g}") for g in range(17)]
        v_b = [P.buf(f"ml_v{g}") for g in range(17)]
        P.memset("dve", va[:].rearrange("p b h f -> p (b h f)"), 1.0, pwrites=v_b)
        cq = P.sb("ml_cq", [128, 3, 512], BF16)
        sqq = P.sb("ml_sqq", [128, 3, 512], BF16)
        ckv = P.sb("ml_ckv", [128, 2, 512], BF16)
        sqkv = P.sb("ml_sqkv", [128, 2, 512], BF16)
        c_b = P.buf("ml_c")
        sdq = P.sb("ml_sdq", [128, 512], F32)
        rq = P.sb("ml_rq", [128, 512], F32)
        sdk = P.sb("ml_sdk", [128, 512], F32)
        rkv = P.sb("ml_rkv", [128, 512], F32)
        sdt = P.sb("ml_sdt", [128, 4], F32)
        rkt = P.sb("ml_rkt", [128, 4], F32)
        r_b = P.buf("ml_r")
        rp = P.sb("ml_rp", [128, 2, 512], F32)
        rp_b = P.buf("ml_rp")
        cs = P.sb("ml_cs", [128, 2, 512], F32)
        cs_b = P.buf("ml_cs")
        t1 = P.sb("ml_t1", [128, 512], F32)
        t2 = P.sb("ml_t2", [128, 512], F32)
        t_b = P.buf("ml_t")
        bc = [0]

        def nbk():
            bc[0] = (bc[0] + 1) % 8
            return bc[0]

        for g, (b0, nb) in enumerate(QG):
            n = nb * 128
            t0 = b0 * 128
            hn, hn_b = hn_rot.next()
            hn_load(P, hn, hn_b, g)
            P.dma("sp", rp[64:96, :, 0:n], W["rope"][:, :, t0:t0 + n].rearrange("a r t -> r a t"), writes=[rp_b])
            for c in range(3):
                bk = nbk()
                for k in range(8):
                    P.mm(banks[bk][:, 0:n], wm[:, k, c * 128:(c + 1) * 128], hn[:, k, 0:n], k == 0, k == 7,
                         [wm_b, hn_b], [bankb[bk]])
                P.act(sqq[:, c, 0:n], banks[bk][:, 0:n], AF.Square, [bankb[bk]], writes=[c_b] if c == 0 else (),
                      pwrites=() if c == 0 else [c_b])
                P.copy("dve", cq[:, c, 0:n], banks[bk][:, 0:n], [bankb[bk]], pwrites=[c_b])
            for c in range(2):
                bk = nbk()
                for k in range(8):
                    P.mm(banks[bk][:, 0:n], wm[:, k, 384 + c * 128:384 + (c + 1) * 128], hn[:, k, 0:n], k == 0, k == 7,
                         [wm_b, hn_b], [bankb[bk]])
                P.act(sqkv[:, c, 0:n], banks[bk][:, 0:n], AF.Square, [bankb[bk]], pwrites=[c_b])
                P.copy("dve", ckv[:, c, 0:n], banks[bk][:, 0:n], [bankb[bk]], pwrites=[c_b])
            bk = nbk()
            for c in range(3):
                P.mm(banks[bk][:, 0:n], C["ones"], sqq[:, c, 0:n], c == 0, c == 2, [c_b, C["bbf"]], [bankb[bk]])
            P.act(sdq[:, 0:n], banks[bk][:, 0:n], AF.Sqrt, [bankb[bk]], [r_b], scale=1.0 / 384, bias=EPS)
            P.recip(rq[:, 0:n], sdq[:, 0:n], [r_b], pwrites=[r_b])
            bk = nbk()
            for c in range(2):
                P.mm(banks[bk][:, 0:n], C["ones"], sqkv[:, c, 0:n], c == 0, c == 1, [c_b, C["bbf"]], [bankb[bk]])
            P.act(sdk[:, 0:n], banks[bk][:, 0:n], AF.Sqrt, [bankb[bk]], pwrites=[r_b], scale=1.0 / 256, bias=EPS)
            P.recip(rkv[:, 0:n], sdk[:, 0:n], [r_b], pwrites=[r_b])
            bk = nbk()
            for bi in range(nb):
                for c in range(2):
                    P.mm(banks[bk][:, bi:bi + 1], sqkv[:, c, bi * 128:(bi + 1) * 128], C["ones"][:, 0:1], c == 0, c == 1,
                         [c_b, C["bbf"]], [bankb[bk]])
            P.act(sdt[:, 0:nb], banks[bk][:, 0:nb], AF.Sqrt, [bankb[bk]], pwrites=[r_b], scale=1.0 / 256, bias=EPS)
            P.recip(rkt[:, 0:nb], sdt[:, 0:nb], [r_b], pwrites=[r_b])
            P.tt("dve", cs[64:96, 0, 0:n], rp[64:96, 0, 0:n], rq[64:96, 0:n], ALU.mult, [rp_b, r_b], [cs_b])
            P.tt("dve", cs[64:96, 1, 0:n], rp[64:96, 1, 0:n], rq[64:96, 0:n], ALU.mult, [rp_b, r_b], pwrites=[cs_b])
            for h in range(2):
                ba = nbk()
                for c in range(3):
                    P.mm(banks[ba][0:96, 0:n], wA[:, c, h * 96:(h + 1) * 96], cq[:, c, 0:n], c == 0, c == 2,
                         [wu_b, c_b], [bankb[ba]])
                P.tt("dve", qT[0:64, h, t0:t0 + n], banks[ba][0:64, 0:n], rq[0:64, 0:n], ALU.mult, [bankb[ba], r_b],
                     pwrites=[qT_b[g]])
                P.tt("dve", t1[64:96, 0:n], banks[ba][64:96, 0:n], cs[64:96, 0, 0:n], ALU.mult, [bankb[ba], cs_b], [t_b])
                bb = nbk()
                for c in range(3):
                    P.mm(banks[bb][0:96, 0:n], wB[:, c, h * 96:(h + 1) * 96], cq[:, c, 0:n], c == 0, c == 2,
                         [wu_b, c_b], [bankb[bb]])
                P.tt("dve", t2[64:96, 0:n], banks[bb][64:96, 0:n], cs[64:96, 1, 0:n], ALU.mult, [bankb[bb], cs_b],
                     pwrites=[t_b])
                P.tt("dve", qT[64:96, h, t0:t0 + n], t1[64:96, 0:n], t2[64:96, 0:n], ALU.add, [t_b], pwrites=[qT_b[g]])
            for h in range(2):
                bk = nbk()
                for c in range(2):
                    P.mm(banks[bk][0:64, 0:n], wK[:, c, h * 64:(h + 1) * 64], ckv[:, c, 0:n], c == 0, c == 1,
                         [wu_b, c_b], [bankb[bk]])
                P.tt("dve", kT[0:64, h, t0:t0 + n], banks[bk][0:64, 0:n], rkv[0:64, 0:n], ALU.mult, [bankb[bk], r_b],
                     pwrites=[kT_b[g]])
            b1 = nbk()
            for k in range(8):
                P.mm(banks[b1][0:96, 0:n], wm[:, k, 640:736], hn[:, k, 0:n], k == 0, k == 7, [wm_b, hn_b], [bankb[b1]])
            P.tt("dve", t1[64:96, 0:n], banks[b1][64:96, 0:n], rp[64:96, 0, 0:n], ALU.mult, [bankb[b1], rp_b], [t_b])
            b2 = nbk()
            for k in range(8):
                P.mm(banks[b2][0:96, 0:n], wm[:, k, 736:832], hn[:, k, 0:n], k == 0, k == 7, [wm_b, hn_b], [bankb[b2]])
            P.tt("dve", t2[64:96, 0:n], banks[b2][64:96, 0:n], rp[64:96, 1, 0:n], ALU.mult, [bankb[b2], rp_b],
                 pwrites=[t_b])
            P.tt("dve", kT[64:96, 0, t0:t0 + n], t1[64:96, 0:n], t2[64:96, 0:n], ALU.add, [t_b], pwrites=[kT_b[g]])
            P.tt("pool", kT[64:96, 1, t0:t0 + n], t1[64:96, 0:n], t2[64:96, 0:n], ALU.add, [t_b], pwrites=[kT_b[g]])
            bk = nbk()
            for bi in range(nb):
                for c in range(2):
                    P.mm(banks[bk][:, bi * 128:(bi + 1) * 128], ckv[:, c, bi * 128:(bi + 1) * 128], wV[:, c, :],
                         c == 0, c == 1, [wu_b, c_b], [bankb[bk]])
            for bi in range(nb):
                blk = b0 + bi
                P.ts("dve", va[:, blk, 0, 0:64], banks[bk][:, bi * 128:bi * 128 + 64], rkt[:, bi:bi + 1], ALU.mult,
                     [bankb[bk], r_b], pwrites=[v_b[g]])
                P.ts("dve", va[:, blk, 1, 64:128], banks[bk][:, bi * 128 + 64:bi * 128 + 128], rkt[:, bi:bi + 1], ALU.mult,
                     [bankb[bk], r_b], pwrites=[v_b[g]])
            if g == 0:
                v0 = va[:, 0, :, :].rearrange("p h f -> p (h f)")
                P.ts("dve", v0, v0, C["validcol"], ALU.mult, [C["b32"]], pwrites=[v_b[0]])
        p_r = Rot(P, "ml_p", [128, 512], BF16, 4)
        o_r = Rot(P, "ml_o", [128, 512], BF16, 2)
        rc = P.sb("ml_rc", [128, 512], F32)
        rc_b = P.buf("ml_rc")
        steps = []
        for g, (b0, nb) in enumerate(QG):
            gend = b0 + nb
            for hd in range(2):
                for kb in range(gend):
                    qb0 = max(kb, b0)
                    steps.append(dict(g=g, hd=hd, kb=kb, qs=qb0 * 128, n=(gend - qb0) * 128, co=(qb0 - b0) * 128,
                                      diag=kb >= b0, first=(kb == 0), last=(kb == gend - 1), ncols=nb * 128, t0=b0 * 128))
        S = len(steps)
        cur_o = {}

        def stA(i):
            st = steps[i]
            n, qs, kb, hd = st["n"], st["qs"], st["kb"], st["hd"]
            zb = i % 2
            P.mm(banks[zb][:, 0:n], kT[0:96, hd, kb * 128:(kb + 1) * 128], qT[0:96, hd, qs:qs + n], True, True,
                 [kT_b[kb // 4], qT_b[st["g"]]], [bankb[zb]])
            p, p_b = p_r.next()
            P.act(p[:, 0:n], banks[zb][:, 0:n], AF.Exp, [bankb[zb]], [p_b], scale=SC)
            if st["diag"]:
                P.tt("dve", p[:, 0:128], p[:, 0:128], C["le"], ALU.mult, [p_b, C["bbf"]], pwrites=[p_b])
            st["p"] = (p, p_b)

        def stB(i):
            st = steps[i]
            n, co, kb, hd, g = st["n"], st["co"], st["kb"], st["hd"], st["g"]
            ob = 2 + 2 * (g % 2) + hd
            p, p_b = st["p"]
            P.mm(banks[ob][:, co:co + n], va[:, kb, hd, :], p[:, 0:n], st["first"], st["last"], [p_b, v_b[kb // 4]],
                 [bankb[ob]])
            if st["last"]:
                nc_ = st["ncols"]
                if hd == 0:
                    cur_o[g] = o_r.next()
                o, o_b = cur_o[g]
                dn = slice(64, 128) if hd == 0 else slice(0, 64)
                nm = slice(0, 64) if hd == 0 else slice(64, 128)
                P.ts("dve", rc[dn, 0:nc_], banks[ob][dn, 0:nc_], 1e-30, ALU.add, [bankb[ob]], [rc_b])
                P.recip(rc[dn, 0:nc_], rc[dn, 0:nc_], [rc_b], pwrites=[rc_b])
                P.tt("dve", o[nm, 0:nc_], banks[ob][nm, 0:nc_], rc[dn, 0:nc_], ALU.mult, [bankb[ob], rc_b],
                     writes=[o_b] if hd == 0 else (), pwrites=() if hd == 0 else [o_b])
                if hd == 1:
                    out_store(P, 3, o, st["t0"], nc_, o_b)

        if S:
            stA(0)
        for i in range(S):
            if i + 1 < S:
                stA(i + 1)
            stB(i)
        P.barrier()
        P.es = save_es


def emit_ssd(P, C, banks, bankb, hn_load, W, out_store, hn_rot, scoped=True, bank_list=tuple(range(8))):
    with phase_scope(P, scoped):
        ws = P.sb("sd_ws", [128, 8, 768], BF16)
        ws_b = P.buf("sd_ws")
        wdt = P.sb("sd_wdt", [128, 8, 4], BF16)
        wdt_b = P.buf("sd_wdt")
        cw = P.sb("sd_cw", [128, 4, 4], F32)
        cb = P.sb("sd_cb", [128, 4], F32)
        dtb = P.sb("sd_dtb", [128, 4], F32)
        alog = P.sb("sd_alog", [128, 4], F32)
        abc = P.sb("sd_abc", [128, 4], F32)
        dsk = P.sb("sd_dsk", [128, 4], F32)
        sm_b = P.buf("sd_small")
        P.dma("sp", cw[:], W["conv_w"], pwrites=[sm_b])
        P.dma("sp", cb[:], W["conv_b"], pwrites=[sm_b])
        P.dma("sp", dtb[:], W["dt_bias"], pwrites=[sm_b])
        P.dma("sp", alog[:], W["a_log"], pwrites=[sm_b])
        P.dma("sp", dsk[:], W["d_skip"], pwrites=[sm_b])
        a_b = P.buf("sd_a")
        P.act(abc[:], alog[:], AF.Exp, [sm_b], [a_b])
        P.ts("dve", abc[:], abc[:], -1.0, ALU.mult, [a_b], pwrites=[a_b])
        with contextlib.ExitStack() as es3:
            prev_es = P.es
            P.es = es3
            stg = Rot(P, "sd_stg", [128, 2, 768], F32, 2)
            wv_ = W["w_ssm"].rearrange("(k p) f -> p k f", p=128)
            for i in range(4):
                t, b = stg.next()
                P.dma("sp", t[:], wv_[:, 2 * i:2 * i + 2, :], writes=[b])
                P.copy("pool", ws[:, 2 * i:2 * i + 2, :], t[:], [b], pwrites=[ws_b])
            sdt_ = P.sb("sd_sdt", [128, 8, 4], F32)
            sdt_b = P.buf("sd_sdt")
            P.dma("sp", sdt_[:], W["w_dt"].rearrange("(k p) f -> p k f", p=128), writes=[sdt_b])
            P.copy("pool", wdt[:], sdt_[:], [sdt_b], [wdt_b])
            P.barrier()
            P.es = prev_es
        raw = P.sb("sd_raw", [128, 4, 515], F32)
        raw_b = [P.buf(f"sd_raw{c}") for c in range(4)]
        P.memset("dve", raw[:].rearrange("p c t -> p (c t)"), 0.0, pwrites=raw_b)
        nm4 = P.sb("sd_nm4", [128, 4, 128], F32)
        nm4_b = P.buf("sd_nm4")
        for h in range(4):
            P.copy("pool", nm4[:, h, :], C["negmask32"], [C["b32"]], pwrites=[nm4_b])
        xcT = P.sb("sd_xcT", [128, 4, 512], BF16)
        xcT_b = P.buf("sd_xcT")
        sz = P.sb("sd_sz", [128, 2, 512], F32)
        sz_b = P.buf("sd_sz")
        xtok = P.sb("sd_xtok", [128, 4, 256], BF16)
        btok = P.sb("sd_btok", [128, 4, 128], BF16)
        tok_b = P.buf("sd_tok")
        dtt = P.sb("sd_dtt", [128, 4, 4], F32)
        dte = P.sb("sd_dte", [128, 4, 4], F32)
        dt = P.sb("sd_dt", [128, 4, 4], F32)
        dA = P.sb("sd_dA", [128, 4, 4], F32)
        dt_b = P.buf("sd_dt")
        acs = P.sb("sd_acs", [128, 4, 4], F32)
        eA = P.sb("sd_eA", [128, 4, 4], F32)
        dtmp = P.sb("sd_dtmp", [128, 4, 4], F32)
        dend = P.sb("sd_dend", [128, 4, 4], F32)
        cd = P.sb("sd_cd", [128, 4, 4], F32)
        dec_b = P.buf("sd_dec")
        dAbc_r = Rot(P, "sd_dAbc", [128, 4, 128], F32, 2)
        Ld_r = Rot(P, "sd_Ld", [128, 4, 128], F32, 2)
        GT_r = Rot(P, "sd_GT", [128, 4, 128], BF16, 4)
        Xd_r = Rot(P, "sd_Xd", [128, 256], BF16, 4)
        Xdd_r = Rot(P, "sd_Xdd", [128, 256], BF16, 2)
        xsk_r = Rot(P, "sd_xsk", [128, 256], F32, 2)
        yt_r = Rot(P, "sd_yt", [128, 256], F32, 2)
        y_r = Rot(P, "sd_y", [128, 256], F32, 2)
        Sbf_r = Rot(P, "sd_Sbf", [128, 256], BF16, 5)
        acc_r = Rot(P, "sd_acc", [128, 512], F32, 2)
        S = P.sb("sd_S", [128, 256], F32)
        S_b = P.buf("sd_S")
        P.memset("dve", S[:], 0.0, [S_b])
        Sprev = Sbf_r.next()
        P.memset("dve", Sprev[0][:], 0.0, [Sprev[1]])
        og = Rot(P, "sd_og", [128, 2, 512], BF16, 2)
        bc = [0]

        def nbk():
            bc[0] = (bc[0] + 1) % len(bank_list)
            return bank_list[bc[0]]

        def bc3(ap2, nlast):
            return ap2.unsqueeze(2).to_broadcast([128, 4, nlast])

        for g, (b0, nb) in enumerate(QG):
            n = nb * 128
            t0 = b0 * 128
            hn, hn_b = hn_rot.next()
            hn_load(P, hn, hn_b, g)
            for c in range(2):
                bk = nbk()
                for k in range(8):
                    P.mm(banks[bk][:, 0:n], ws[:, k, c * 128:(c + 1) * 128], hn[:, k, 0:n], k == 0, k == 7,
                         [ws_b, hn_b], [bankb[bk]])
                P.act(sz[:, c, 0:n], banks[bk][:, 0:n], AF.Silu, [bankb[bk]], writes=[sz_b] if c == 0 else (),
                      pwrites=() if c == 0 else [sz_b])
            for cp in range(2):
                accs = []
                for c in (2 * cp, 2 * cp + 1):
                    bk = nbk()
                    for k in range(8):
                        P.mm(banks[bk][:, 0:n], ws[:, k, 256 + c * 128:256 + (c + 1) * 128], hn[:, k, 0:n], k == 0, k == 7,
                             [ws_b, hn_b], [bankb[bk]])
                    P.copy("act", raw[:, c, 3:3 + n], banks[bk][:, 0:n], [bankb[bk]], pwrites=[raw_b[c]])
                    accs.append(acc_r.next())
                for i, c in enumerate((2 * cp, 2 * cp + 1)):
                    acc, acc_b = accs[i]
                    P.ts("dve", acc[:, 0:n], raw[:, c, 3:3 + n], cw[:, c, 3:4], ALU.mult, [raw_b[c], sm_b], [acc_b],
                         s2=cb[:, c:c + 1], op1=ALU.add)
                for tap in (2, 1, 0):
                    for i, c in enumerate((2 * cp, 2 * cp + 1)):
                        acc, acc_b = accs[i]
                        P.stt(acc[:, 0:n], raw[:, c, tap:tap + n], cw[:, c, tap:tap + 1], acc[:, 0:n], ALU.mult, ALU.add,
                              [raw_b[c], sm_b, acc_b], pwrites=[acc_b])
                for i, c in enumerate((2 * cp, 2 * cp + 1)):
                    acc, acc_b = accs[i]
                    P.act(xcT[:, c, 0:n], acc[:, 0:n], AF.Silu, [acc_b], writes=[xcT_b] if c == 0 else (),
                          pwrites=() if c == 0 else [xcT_b])
                    P.copy("pool", raw[:, c, 0:3], raw[:, c, n:n + 3], [raw_b[c], acc_b], pwrites=[raw_b[c]])
            bk = nbk()
            for bi in range(nb):
                for k in range(8):
                    P.mm(banks[bk][:, bi * 4:(bi + 1) * 4], hn[:, k, bi * 128:(bi + 1) * 128], wdt[:, k, :], k == 0, k == 7,
                         [wdt_b, hn_b], [bankb[bk]])
            pdt = banks[bk][:, 0:4 * nb].rearrange("p (b h) -> p b h", h=4)
            P.tt("dve", dtt[:, 0:nb, :], pdt, dtb[:].unsqueeze(1).to_broadcast([128, nb, 4]), ALU.add,
                 [bankb[bk], sm_b], [dt_b])
            P.act(dte[:, 0:nb, :], dtt[:, 0:nb, :], AF.Exp, [dt_b], pwrites=[dt_b])
            P.act(dt[:, 0:nb, :], dte[:, 0:nb, :], AF.Ln, [dt_b], pwrites=[dt_b], bias=1.0)
            if g == 0:
                P.ts("dve", dt[:, 0, :], dt[:, 0, :], C["validcol"], ALU.mult, [dt_b, C["b32"]], pwrites=[dt_b])
            P.tt("dve", dA[:, 0:nb, :], dt[:, 0:nb, :], abc[:].unsqueeze(1).to_broadcast([128, nb, 4]), ALU.mult,
                 [dt_b, a_b], pwrites=[dt_b])
            for bi in range(nb):
                bk = nbk()
                pst = banks[bk][:].bitcast(BF16)
                for c in range(3):
                    P.tr(pst[:, c * 128:(c + 1) * 128], xcT[:, c, bi * 128:(bi + 1) * 128], C["ident"],
                         [xcT_b, C["bbf"]], [bankb[bk]])
                P.copy("act", xtok[:, bi, :], pst[:, 0:256], [bankb[bk]], writes=[tok_b] if bi == 0 else (),
                       pwrites=() if bi == 0 else [tok_b])
                P.copy("act", btok[:, bi, :], pst[:, 256:384], [bankb[bk]], pwrites=[tok_b])
            o, o_b = og.next()
            bA = nbk()
            for bi in range(nb):
                P.mm(banks[bA][:, bi * 8:bi * 8 + 4], C["tri32"], dA[:, bi, :], True, True, [dt_b, C["b32"]], [bankb[bA]])
                P.mm(banks[bA][:, bi * 8 + 4:bi * 8 + 8], C["ones32"], dA[:, bi, :], True, True, [dt_b, C["b32"]], [bankb[bA]])
            pA = banks[bA][:, 0:8 * nb].rearrange("p (b e) -> p b e", e=8)
            P.act(eA[:, 0:nb, :], pA[:, :, 0:4], AF.Exp, [bankb[bA]], [dec_b])
            P.act(acs[:, 0:nb, :], pA[:, :, 0:4], AF.Identity, [bankb[bA]], pwrites=[dec_b])
            P.act(cd[:, 0:nb, :], pA[:, :, 4:8], AF.Exp, [bankb[bA]], pwrites=[dec_b])
            P.tt("dve", dtmp[:, 0:nb, :], pA[:, :, 4:8], acs[:, 0:nb, :], ALU.subtract, [bankb[bA], dec_b], pwrites=[dec_b])
            P.act(dend[:, 0:nb, :], dtmp[:, 0:nb, :], AF.Exp, [dec_b], pwrites=[dec_b])
            GTs, Xds, Xdds, xsks = [], [], [], []
            for bi in range(nb):
                cols = slice(bi * 128, (bi + 1) * 128)
                dAb, dAb_b = dAbc_r.next()
                P.copy("pool", dAb[:], bc3(dA[:, bi, :], 128), [dt_b], [dAb_b])
                bs = nbk()
                P.mm(banks[bs][:], C["ntri32"], dAb[:].rearrange("p h l -> p (h l)"), True, False,
                     [dAb_b, C["b32"]], [bankb[bs]])
                P.mm(banks[bs][:], C["ident32"], nm4[:].rearrange("p h l -> p (h l)"), False, False,
                     [nm4_b, C["b32"]], [bankb[bs]])
                for h in range(4):
                    P.mm(banks[bs][:, h * 128:(h + 1) * 128], dAb[:, h, :], C["tri32"], False, h == 3,
                         [dAb_b, C["b32"]], [bankb[bs]])
                Ld, Ld_b = Ld_r.next()
                P.act(Ld[:].rearrange("p h l -> p (h l)"), banks[bs][:], AF.Exp, [bankb[bs]], [Ld_b])
                bcb = nbk()
                P.mm(banks[bcb][:, 0:128], xcT[:, 2, cols], xcT[:, 3, cols], True, True, [xcT_b], [bankb[bcb]])
                GT, GT_b = GT_r.next()
                P.tt("dve", GT[:], Ld[:], banks[bcb][:, 0:128].unsqueeze(1).to_broadcast([128, 4, 128]), ALU.mult,
                     [Ld_b, bankb[bcb]], [GT_b])
                GTs.append((GT, GT_b))
                Xd, Xd_b = Xd_r.next()
                P.tt("pool", Xd[:].rearrange("p (h q) -> p h q", h=4), xtok[:, bi, :].rearrange("p (h q) -> p h q", h=4),
                     bc3(dt[:, bi, :], 64), ALU.mult, [tok_b, dt_b], [Xd_b])
                Xds.append((Xd, Xd_b))
            Sb = [Sprev]
            for bi in range(nb):
                Xd, Xd_b = Xds[bi]
                Xdd, Xdd_b = Xdd_r.next()
                P.tt("pool", Xdd[:].rearrange("p (h q) -> p h q", h=4), Xd[:].rearrange("p (h q) -> p h q", h=4),
                     bc3(dend[:, bi, :], 64), ALU.mult, [Xd_b, dec_b], [Xdd_b])
                bst = nbk()
                P.mm(banks[bst][:, 0:256], btok[:, bi, :], Xdd[:], True, True, [tok_b, Xdd_b], [bankb[bst]])
                P.tt("dve", S[:].rearrange("p (h q) -> p h q", h=4), S[:].rearrange("p (h q) -> p h q", h=4),
                     bc3(cd[:, bi, :], 64), ALU.mult, [S_b, dec_b], pwrites=[S_b])
                P.tt("dve", S[:], S[:], banks[bst][:, 0:256], ALU.add, [S_b, bankb[bst]], pwrites=[S_b])
                Sn = Sbf_r.next()
                P.copy("act", Sn[0][:], S[:], [S_b], [Sn[1]])
                Sb.append(Sn)
            Sprev = Sb[nb]
            for bi in range(nb):
                cols = slice(bi * 128, (bi + 1) * 128)
                GT, GT_b = GTs[bi]
                Xd, Xd_b = Xds[bi]
                by = nbk()
                for h in range(4):
                    P.mm(banks[by][:, h * 64:(h + 1) * 64], GT[:, h, :], Xd[:, h * 64:(h + 1) * 64], True, True,
                         [GT_b, Xd_b], [bankb[by]])
                P.mm(banks[by][:, 256:512], xcT[:, 3, cols], Sb[bi][0][:], True, True, [xcT_b, Sb[bi][1]], [bankb[by]])
                xsk, xsk_b = xsk_r.next()
                P.tt("pool", xsk[:].rearrange("p (h q) -> p h q", h=4), xtok[:, bi, :].rearrange("p (h q) -> p h q", h=4),
                     bc3(dsk[:], 64), ALU.mult, [tok_b, sm_b], [xsk_b])
                yt, yt_b = yt_r.next()
                y, y_b = y_r.next()
                P.tt("dve", yt[:].rearrange("p (h q) -> p h q", h=4), banks[by][:, 256:512].rearrange("p (h q) -> p h q", h=4),
                     bc3(eA[:, bi, :], 64), ALU.mult, [bankb[by], dec_b], [yt_b])
                P.tt("dve", y[:], banks[by][:, 0:256], yt[:], ALU.add, [bankb[by], yt_b], [y_b])
                P.tt("pool", y[:], y[:], xsk[:], ALU.add, [y_b, xsk_b], pwrites=[y_b])
                bt = nbk()
                for c in range(2):
                    P.tr(banks[bt][:, c * 128:(c + 1) * 128], y[:, c * 128:(c + 1) * 128], C["ident32"],
                         [y_b, C["b32"]], [bankb[bt]])
                P.tt("dve", o[:, :, cols], banks[bt][:, 0:256].rearrange("p (c t) -> p c t", c=2), sz[:, :, cols], ALU.mult,
                     [bankb[bt], sz_b], writes=[o_b] if bi == 0 else (), pwrites=() if bi == 0 else [o_b])
            out_store(P, 1, o[:, 0, :], t0, n, o_b)
            out_store(P, 2, o[:, 1, :], t0, n, o_b)


def m_weight_tensors(P, parts, lead=None):
    def T(name, shape):
        return P.dram(name, ([lead] if lead else []) + shape, F32, "ExternalInput")
    W = {}
    if "sb" in parts:
        W["w_sb"] = T("w_sb", [D, 384])
    if "mla" in parts:
        W.update({"w_mla_in": T("w_mla_in", [D, 832]), "w_uqA": T("w_uqA", [384, 192]), "w_uqB": T("w_uqB", [384, 192]),
                  "w_ukvk": T("w_ukvk", [256, 128]), "w_ukvv": T("w_ukvv", [256, 128]), "g_q": T("g_q", [128, 3]),
                  "g_kv": T("g_kv", [128, 2])})
        W["rope"] = P.dram("rope", [2, 32, L], F32, "ExternalInput")
    if "ssd" in parts:
        W.update({"w_ssm": T("w_ssm", [D, 768]), "w_dt": T("w_dt", [D, 4]), "conv_w": T("conv_w4", [128, 4, 4]),
                  "conv_b": T("conv_b4", [128, 4]), "dt_bias": T("dt_bias4", [128, 4]), "a_log": T("a_log4", [128, 4]),
                  "d_skip": T("d_skip4", [128, 4])})
    return W


def emit_M(P, C, banks, bankb, hn_rot, W, hn_load, out_store, parts=("sb", "mla", "ssd")):
    import os
    if "sb" in parts:
        emit_sb(P, C, banks, bankb, hn_load, W["w_sb"], out_store, hn_rot,
                bk=SB_BANKS_SHARED if os.environ.get("SB_SHARED") else SB_BANKS_FULL)
    if "mla" in parts:
        emit_mla(P, C, banks, bankb, hn_load, W, out_store, hn_rot)
    if "ssd" in parts:
        emit_ssd(P, C, banks, bankb, hn_load, W, out_store, hn_rot)


def build_M(parts=("sb", "mla", "ssd")):
    nc = bass.Bass("TRN2", target_bir_lowering=False)
    P = Prog(nc)
    cdram = P.dram("consts", [128, 1024], F32, "ExternalInput")
    hnT_d = P.dram("hnT", [8, 128, L], BF16, "ExternalInput")
    out_d = P.dram("omixT_out", [4, 128, L], BF16, "ExternalOutput")
    out_b = [P.buf(f"omix_out{i}") for i in range(4)]
    P.out_bufs += out_b
    C = setup_consts(P, cdram)
    banks = [P.ps(f"bank{i}", [128, 512], F32) for i in range(8)]
    bankb = [P.buf(f"bank{i}", excl=True) for i in range(8)]
    hn_rot = Rot(P, "hn", [128, 8, 512], BF16, 2)
    W = m_weight_tensors(P, parts)

    def hn_load(P, hn, hn_b, g):
        b0, nb = QG[g]
        P.dma("sp", hn[:, :, 0:nb * 128], hnT_d[:, :, b0 * 128:(b0 + nb) * 128].rearrange("k p t -> p k t"), writes=[hn_b])

    def out_store(P, cc, o, t0, n, o_b):
        P.dma("sp", out_d[cc][:, t0:t0 + n], o[:, 0:n], reads=[o_b], pwrites=[out_b[cc]])

    import os
    if os.environ.get("M_INTER"):
        rot_b = Rot(P, "hnb", [128, 8, 512], BF16, 2)
        DA, DB = Deferred(P), Deferred(P)
        emit_sb(DA, C, banks, bankb, hn_load, W["w_sb"], out_store, hn_rot, scoped=False, bk=SB_BANKS_SHARED)
        emit_ssd(DB, C, banks, bankb, hn_load, W, out_store, rot_b, scoped=False, bank_list=(5, 6, 7))
        print("ops", len(DA.ops), len(DB.ops))
        replay_interleaved(DA, DB)
        P.barrier()
    else:
        emit_M(P, C, banks, bankb, hn_rot, W, hn_load, out_store, parts)
    P.finish()
    return nc


O_SBQ, O_SBK, O_SBV, O_Z, O_X, O_B, O_C, O_DT, O_CQ, O_CKV, O_KR = 0, 512, 1024, 1536, 2560, 3584, 3840, 4096, 4112, 4496, 4752


def m_inputs(d, i, j):
    w_in = d["w_in"][i]
    m = {"consts": host_consts()}
    m["w_sb"] = np.ascontiguousarray(np.concatenate(
        [w_in[:, O_SBQ + 128 * j:O_SBQ + 128 * j + 128], w_in[:, O_SBK + 128 * j:O_SBK + 128 * j + 128],
         w_in[:, O_SBV + 128 * j:O_SBV + 128 * j + 128]], axis=1))
    z64 = np.zeros((D, 64), np.float32)
    kr = w_in[:, O_KR:O_KR + 32]
    krs = np.concatenate([kr[:, 16:32], kr[:, 0:16]], axis=1)
    m["w_mla_in"] = np.ascontiguousarray(np.concatenate(
        [w_in[:, O_CQ:O_CQ + 384], w_in[:, O_CKV:O_CKV + 256], z64, kr, z64, krs], axis=1))
    wuq = d["w_uq"][i].reshape(384, 8, 96)
    wukv = d["w_ukv"][i].reshape(256, 8, 128)
    A, B = [], []
    for h in (2 * j, 2 * j + 1):
        A.append(wuq[:, h, :])
        r = wuq[:, h, 64:96]
        B.append(np.concatenate([np.zeros((384, 64), np.float32), r[:, 16:32], r[:, 0:16]], axis=1))
    m["w_uqA"] = np.ascontiguousarray(np.concatenate(A, axis=1))
    m["w_uqB"] = np.ascontiguousarray(np.concatenate(B, axis=1))
    m["w_ukvk"] = np.ascontiguousarray(np.concatenate([wukv[:, 2 * j, 0:64], wukv[:, 2 * j + 1, 0:64]], axis=1))
    m["w_ukvv"] = np.ascontiguousarray(np.concatenate([wukv[:, 2 * j, 64:128], wukv[:, 2 * j + 1, 64:128]], axis=1))
    m["g_q"] = np.ascontiguousarray(d["q_norm_g"][i].reshape(3, 128).T)
    m["g_kv"] = np.ascontiguousarray(d["kv_norm_g"][i].reshape(2, 128).T)
    m["rope"] = rope_tables_host()
    grp = j // 2
    m["w_ssm"] = np.ascontiguousarray(np.concatenate(
        [w_in[:, O_Z + 256 * j:O_Z + 256 * j + 256], w_in[:, O_X + 256 * j:O_X + 256 * j + 256],
         w_in[:, O_B + 128 * grp:O_B + 128 * grp + 128], w_in[:, O_C + 128 * grp:O_C + 128 * grp + 128]], axis=1))
    m["w_dt"] = np.ascontiguousarray(w_in[:, O_DT + 4 * j:O_DT + 4 * j + 4])
    ch = np.concatenate([np.arange(256 * j, 256 * j + 256), 1024 + 128 * grp + np.arange(128),
                         1280 + 128 * grp + np.arange(128)])
    m["conv_w4"] = np.ascontiguousarray(d["conv_w"][i][:, ch].reshape(4, 4, 128).transpose(2, 1, 0))
    m["conv_b4"] = np.ascontiguousarray(d["conv_b"][i][ch].reshape(4, 128).T)
    bc4 = lambda v: np.ascontiguousarray(np.broadcast_to(v[4 * j:4 * j + 4], (128, 4)))
    m["dt_bias4"] = bc4(d["dt_bias"][i])
    m["a_log4"] = bc4(d["a_log"][i])
    m["d_skip4"] = bc4(d["d_skip"][i])
    return m


_ROPE = []


def rope_tables_host():
    if not _ROPE:
        pos = (np.arange(L) - 112).astype(np.float32)
        inv = (10000.0 ** (-np.arange(0, 32, 2, dtype=np.float32) / 32)).astype(np.float32)
        ang = (pos[None, :] * inv[:, None]).astype(np.float32)
        c, s_ = np.cos(ang).astype(np.float32), np.sin(ang).astype(np.float32)
        _ROPE.append(np.ascontiguousarray(np.stack([np.concatenate([c, c], 0), np.concatenate([-s_, s_], 0)], 0)))
    return _ROPE[0]


def M_KEYS(parts):
    keys = {"consts", "hnT"}
    if "sb" in parts:
        keys |= {"w_sb"}
    if "mla" in parts:
        keys |= {"w_mla_in", "w_uqA", "w_uqB", "w_ukvk", "w_ukvv", "g_q", "g_kv", "rope"}
    if "ssd" in parts:
        keys |= {"w_ssm", "w_dt", "conv_w4", "conv_b4", "dt_bias4", "a_log4", "d_skip4"}
    return keys


def build_fused(depth=DEPTH):
    nc = bass.Bass("TRN2", target_bir_lowering=False)
    P = Prog(nc)
    cdram = P.dram("consts", [128, 1024], F32, "ExternalInput")
    sel_d = P.dram("sel", [128, 4], F32, "ExternalInput")
    h_in = P.dram("h_in", [TB, 128, D], F32, "ExternalInput")
    gnext_d = P.dram("g_next", [depth + 1, 128, D], F32, "ExternalInput")
    Wf = {"w_out": P.dram("w_out", [depth, 2048, D], F32, "ExternalInput"),
          "w_gate": P.dram("w_gate", [depth, D, DFF], F32, "ExternalInput"),
          "w_up": P.dram("w_up", [depth, D, DFF], F32, "ExternalInput"),
          "w_down": P.dram("w_down", [depth, DFF, D], F32, "ExternalInput"),
          "g_ssm": P.dram("g_ssm", [depth, 128, 8], F32, "ExternalInput"),
          "g_ffn": P.dram("g_ffn", [depth, 128, D], F32, "ExternalInput")}
    Wm = m_weight_tensors(P, ("sb", "mla", "ssd"), lead=depth)
    y_d = P.dram("y_out", [TB, 128, D], F32, "ExternalOutput")
    y_b = P.buf("y_out")
    P.out_bufs.append(y_b)

    def scr(name, shape, dtype):
        t = nc.dram_tensor(name, list(shape), dtype)
        return t.ap(), P.buf(name)

    h_scr, h_scr_b = scr("h_scr", [TB, 128, D], F32)
    hn0, hn0_b = scr("hn_blk0", [8, 128, 128], BF16)
    hn_in = [[scr(f"hn_in{l}_{q}", [256, 2048], BF16) for q in range(4)] for l in range(depth)]
    hn_all = [[scr(f"hn_all{l}_{q}", [1024, 2048], BF16) for q in range(4)] for l in range(depth)]
    pcw = [128, 4096, 4096]
    om_in = [[[scr(f"om_in{l}_{c}_{p}", [128, pcw[p]], BF16) for p in range(3)] for c in range(4)] for l in range(depth)]
    om_all = [[[scr(f"om_all{l}_{c}_{p}", [512, pcw[p]], BF16) for p in range(3)] for c in range(4)] for l in range(depth)]

    C = setup_consts(P, cdram)
    selt = P.sb("sel", [128, 4], F32)
    sel_b = P.buf("sel")
    P.dma("sp", selt[:], sel_d, writes=[sel_b])
    banks = [P.ps(f"bank{i}", [128, 512], F32) for i in range(8)]
    bankb = [P.buf(f"bank{i}", excl=True) for i in range(8)]

    def make_hn_store(l):
        def hn_store(xT, xT_b):
            P.dma("sp", hn0.rearrange("k p t -> p k t"), xT[:, :, 0:128], reads=[xT_b[0]], writes=[hn0_b])
            for q in range(4):
                ap, b = hn_in[l][q]
                P.dma("sp", ap.rearrange("(k p) t -> p k t", p=128), xT[:, 2 * q:2 * q + 2, 128:TT], reads=xT_b[1:], writes=[b])
            for q in range(4):
                P.allgather(hn_in[l][q][0], hn_all[l][q][0], [hn_in[l][q][1]], [hn_all[l][q][1]])
        return hn_store

    def make_hn_load(l):
        def hn_load(P, hn, hn_b, g):
            b0, nb = QG[g]
            blk = b0
            first = True
            while blk < b0 + nb:
                off = (blk - b0) * 128
                if blk == 0:
                    P.dma("sp", hn[:, :, off:off + 128], hn0.rearrange("k p t -> p k t"), reads=[hn0_b],
                          writes=[hn_b] if first else (), pwrites=() if first else [hn_b])
                    first = False
                    blk += 1
                    continue
                r, c0 = divmod(blk - 1, 16)
                ln = min(b0 + nb - blk, 16 - c0)
                for q in range(4):
                    ap, b = hn_all[l][q]
                    P.dma("sp", hn[:, 2 * q:2 * q + 2, off:off + ln * 128],
                          ap[r * 256:(r + 1) * 256, c0 * 128:(c0 + ln) * 128].rearrange("(k p) t -> p k t", p=128),
                          reads=[b], writes=[hn_b] if first else (), pwrites=() if first else [hn_b])
                    first = False
                blk += ln
        return hn_load

    def make_out_store(l):
        def out_store(P, cc, o, t0, n, o_b):
            t = t0
            while t < t0 + n:
                if t < 128:
                    pc, col, ln = 0, t, min(t0 + n, 128) - t
                else:
                    off = t - 128
                    pc, col = 1 + off // 4096, off % 4096
                    ln = min(t0 + n - t, 4096 - col)
                ap, b = om_in[l][cc][pc]
                P.dma("sp", ap[:, col:col + ln], o[:, t - t0:t - t0 + ln], reads=[o_b], pwrites=[b])
                t += ln
        return out_store

    def make_om_load(l):
        def om_extra():
            return Rot(P, "omst", [128, 16, 512], BF16, 2)

        def om_load(o, ob, b0, nb, stage):
            n = nb * 128
            lb0 = max(b0, 1)
            dc0 = (lb0 - b0) * 128
            ln = n - dc0
            first = True
            if ln > 0:
                for jc in range(4):
                    st, st_b = stage.next()
                    off = 2048 * jc + (lb0 - 1) * 128
                    pc, col = 1 + off // 4096, off % 4096
                    for cc in range(4):
                        ap, b = om_all[l][cc][pc]
                        P.dma("sp", st[:, cc:16:4, dc0:n], ap[:, col:col + ln].rearrange("(r p) t -> p r t", p=128),
                              reads=[b], writes=[st_b] if cc == 0 else (), pwrites=() if cc == 0 else [st_b])
                    if jc == 0:
                        P.ts("dve", o[:, :, dc0:n], st[:, :, dc0:n], selt[:, 0:1], ALU.mult, [st_b, sel_b], writes=[ob])
                    else:
                        P.stt(o[:, :, dc0:n], st[:, :, dc0:n], selt[:, jc:jc + 1], o[:, :, dc0:n], ALU.mult, ALU.add,
                              [st_b, sel_b, ob], pwrites=[ob])
                first = False
            if b0 == 0:
                for cc in range(4):
                    ap, b = om_all[l][cc][0]
                    P.dma("sp", o[:, cc:16:4, 0:128], ap.rearrange("(r p) t -> p r t", p=128), reads=[b],
                          writes=[ob] if first else (), pwrites=() if first else [ob])
                    first = False
        return om_load, om_extra

    def gather_cc(l, cc):
        for p in range(3):
            P.allgather(om_in[l][cc][p][0], om_all[l][cc][p][0], [om_in[l][cc][p][1]], [om_all[l][cc][p][1]])

    emit_F(P, C, banks, bankb, "pre", {"h_src": h_in, "g_next": gnext_d[0], "hn_store": make_hn_store(0)}, tag="p")
    for l in range(depth):
        W = {k: (v if k == "rope" else v[l]) for k, v in Wm.items()}
        hn_load, out_store = make_hn_load(l), make_out_store(l)
        with contextlib.ExitStack() as esl:
            save = P.es
            P.es = esl
            rot_a = Rot(P, "hna", [128, 8, 512], BF16, 2)
            rot_b = Rot(P, "hnb", [128, 8, 512], BF16, 2)
            DA, DB = Deferred(P), Deferred(P)
            emit_sb(DA, C, banks, bankb, hn_load, W["w_sb"], out_store, rot_a, scoped=False, bk=SB_BANKS_SHARED)
            emit_ssd(DB, C, banks, bankb, hn_load, W, out_store, rot_b, scoped=False, bank_list=(5, 6, 7))
            replay_interleaved(DA, DB)
            P.barrier()
            P.es = save
        for cc in (0, 1, 2):
            gather_cc(l, cc)
        with contextlib.ExitStack() as esl:
            save = P.es
            P.es = esl
            rot_c = Rot(P, "hnc", [128, 8, 512], BF16, 2)
            emit_mla(P, C, banks, bankb, hn_load, W, out_store, rot_c)
            P.barrier()
            P.es = save
        gather_cc(l, 3)
        last = l == depth - 1
        om_load, om_extra = make_om_load(l)
        io = {"h_src": h_in if l == 0 else h_scr, "h_src_b": () if l == 0 else [h_scr_b], "g_next": gnext_d[l + 1],
              "om_load": om_load, "om_extra": om_extra,
              "w_out": Wf["w_out"][l], "w_gate": Wf["w_gate"][l], "w_up": Wf["w_up"][l], "w_down": Wf["w_down"][l],
              "g_ssm": Wf["g_ssm"][l], "g_ffn": Wf["g_ffn"][l]}
        if last:
            io["y_d"], io["y_b"] = y_d, y_b
        else:
            io["h_dst"], io["h_dst_b"] = h_scr, h_scr_b
            io["hn_store"] = make_hn_store(l + 1)
        emit_F(P, C, banks, bankb, "last" if last else "mid", io, tag=str(l))
    P.finish()
    return nc


WOUT_PERM = np.concatenate([np.concatenate([np.arange(128 * r, 128 * r + 128), 512 + np.arange(256 * r, 256 * r + 256),
                                            1536 + np.arange(128 * r, 128 * r + 128)]) for r in range(4)])
_FUSED = []


def kernel(**inp):
    return _run(DEPTH, inp)


def _run(depth, inp):
    d = {k: np.asarray(v) for k, v in inp.items()}
    x = d["x"].astype(np.float32)
    meta = d["meta_tokens"].astype(np.float32)
    consts = host_consts()
    cores = list(range(8))
    blk0 = np.concatenate([np.zeros((112, D), np.float32), meta], 0)
    if not _FUSED:
        _FUSED.append(build_fused(depth))
    nc = _FUSED[0]
    shared = {
        "consts": consts,
        "g_next": np.ascontiguousarray(np.stack([_bc(d["mix_norm_g"][i]) for i in range(depth)] + [_bc(d["final_norm_g"])], 0)),
        "w_out": np.ascontiguousarray(d["w_out"][:depth][:, WOUT_PERM, :]),
        "w_gate": np.ascontiguousarray(d["w_gate"][:depth]), "w_up": np.ascontiguousarray(d["w_up"][:depth]),
        "w_down": np.ascontiguousarray(d["w_down"][:depth]),
        "g_ssm": np.ascontiguousarray(np.stack([d["ssm_norm_g"][i].reshape(8, 128).T for i in range(depth)], 0)),
        "g_ffn": np.ascontiguousarray(np.stack([_bc(d["ffn_norm_g"][i]) for i in range(depth)], 0)),
    }
    mw = {}
    for j in range(4):
        per = [m_inputs(d, i, j) for i in range(depth)]
        mw[j] = {k: (per[0][k] if k in ("rope",) else np.ascontiguousarray(np.stack([p[k] for p in per], 0)))
                 for k in per[0] if k != "consts"}
    in_maps = []
    for c in cores:
        b, j = divmod(c, 4)
        m = dict(shared)
        m.update(mw[j])
        sel = np.zeros((128, 4), np.float32)
        sel[:, j] = 1.0
        m["sel"] = sel
        m["h_in"] = np.ascontiguousarray(np.concatenate([blk0, x[b, 2048 * j:2048 * (j + 1)]], 0).reshape(TB, 128, D))
        in_maps.append(m)
    import os
    if os.environ.get("K_TRACE"):
        rr = run_bass_kernel_spmd(nc, in_maps, core_ids=cores, trace=True)
        print("EXEC_NS", rr.exec_time_ns)
        res = rr.results
    else:
        res = run_bass_kernel_spmd(nc, in_maps, core_ids=cores).results
    y = np.empty((2, 8192, D), np.float32)
    for c in cores:
        b, j = divmod(c, 4)
        y[b, 2048 * j:2048 * (j + 1)] = res[c]["y_out"].reshape(TT, D)[128:]
    return y


def _bc(v):
    return np.ascontiguousarray(np.broadcast_to(np.asarray(v, np.float32), (128, D)))
```

```python
import contextlib
import numpy as np
import ml_dtypes
import concourse.bass as bass
import concourse.mybir as mybir
from concourse.bass_utils import run_bass_kernel_spmd

F32 = mybir.dt.float32
BF16 = mybir.dt.bfloat16
AF = mybir.ActivationFunctionType
ALU = mybir.AluOpType
NPBF = ml_dtypes.bfloat16

D = 1024
L = 8320
NB = 65
DEPTH = 4
DFF = 2816
NFF = 22
EPS = 1e-6
TB = 17
TT = TB * 128


class Buf:
    __slots__ = ("name", "w", "r", "dsem", "excl")

    def __init__(self, name, excl=False):
        self.name = name
        self.excl = excl
        self.w = {}
        self.r = {}
        self.dsem = None


def _merge(d, s, v):
    if d.get(s, 0) < v:
        d[s] = v


class Prog:
    ENG = ("pe", "act", "dve", "pool", "sp")

    def __init__(self, nc):
        self.nc = nc
        self.es = contextlib.ExitStack()
        self.root_es = contextlib.ExitStack()
        self.prog = {k: [] for k in self.ENG}
        self.sem = {}
        self.cnt = {}
        self.seen = {k: {} for k in self.ENG}
        self.nbuf = 0
        for k in ("pe", "act", "dve", "pool"):
            self.newsem("c_" + k)
        self.out_bufs = []
        self.free_d = []
        self.live = []
        self.gpos = 0
        self.tokpos = {}
        self.probe = False
        self.probe_deps = None

    def newsem(self, name):
        self.sem[name] = self.root_es.enter_context(self.nc.semaphore(name))
        self.cnt[name] = 0

    def buf(self, name=None, excl=False):
        self.nbuf += 1
        return Buf(f"{name or 'b'}_{self.nbuf}", excl)

    def sb(self, name, shape, dtype):
        self.nbuf += 1
        return self.es.enter_context(self.nc.sbuf_tensor(f"{name}_{self.nbuf}", list(shape), dtype))

    def ps(self, name, shape, dtype):
        self.nbuf += 1
        return self.es.enter_context(self.nc.psum_tensor(f"{name}_{self.nbuf}", list(shape), dtype))

    def dram(self, name, shape, dtype, kind):
        return self.nc.dram_tensor(name, list(shape), dtype, kind=kind).ap()

    def _deps(self, reads, writes, pwrites):
        deps = {}
        for b in reads:
            for s, v in b.w.items():
                _merge(deps, s, v)
        for b in writes:
            for s, v in b.w.items():
                _merge(deps, s, v)
            for s, v in b.r.items():
                _merge(deps, s, v)
        for b in pwrites:
            for s, v in b.r.items():
                _merge(deps, s, v)
        return deps

    def _waits(self, eng, deps, is_dma):
        own = "c_" + eng
        seen = self.seen[eng]
        for s, v in deps.items():
            if s == own and eng == "pe" and not is_dma:
                continue
            if s.startswith("d_"):
                v = self.cnt[s]
            if seen.get(s, 0) >= v:
                continue
            seen[s] = v
            h = self.sem[s]
            self.prog[eng].append(lambda e, h=h, v=v: e.wait_ge(h, v))

    def _mark(self, tok, reads, writes, pwrites):
        s, v = tok
        self.gpos += 1
        self.tokpos[tok] = self.gpos
        for b in reads:
            _merge(b.r, s, v)
        for b in writes:
            b.w = {s: v}
            b.r = {}
        for b in pwrites:
            _merge(b.w, s, v)

    def op(self, eng, fn, reads=(), writes=(), pwrites=()):
        if any(b.excl for b in reads):
            writes = tuple(writes) + tuple(b for b in reads if b.excl)
            reads = tuple(b for b in reads if not b.excl)
        if self.probe:
            self.probe_deps = self._deps(reads, writes, pwrites)
            return
        self._waits(eng, self._deps(reads, writes, pwrites), False)
        s = "c_" + eng
        self.cnt[s] += 1
        v = self.cnt[s]
        h = self.sem[s]
        self.prog[eng].append(lambda e, fn=fn, h=h: fn(e).then_inc(h, 1))
        self._mark((s, v), reads, writes, pwrites)

    def dma(self, q, out, in_, reads=(), writes=(), pwrites=(), **kw):
        if self.probe:
            self.probe_deps = self._deps(reads, writes, pwrites)
            return
        self._waits(q, self._deps(reads, writes, pwrites), True)
        b = (tuple(writes) + tuple(pwrites))[0]
        if b.dsem is None:
            if self.free_d:
                b.dsem = self.free_d.pop()
            else:
                b.dsem = f"d_{len(self.sem)}"
                self.newsem(b.dsem)
            self.live.append(b)
        s = b.dsem
        self.cnt[s] += 16
        v = self.cnt[s]
        h = self.sem[s]
        self.prog[q].append(lambda e, h=h: e.dma_start(out=out, in_=in_, **kw).then_inc(h, 16))
        self._mark((s, v), reads, writes, pwrites)

    def allgather(self, in_ap, out_ap, reads, writes):
        self._waits("pool", self._deps(reads, writes, ()), True)
        s = "d_cc"
        if s not in self.sem:
            self.newsem(s)
        self.cnt[s] += 1
        v = self.cnt[s]
        h = self.sem[s]
        self.prog["pool"].append(lambda e, h=h: e.collective_compute(
            "AllGather", ALU.bypass, replica_groups=[[0, 1, 2, 3], [4, 5, 6, 7]], ins=[in_ap.opt()], outs=[out_ap.opt()]).then_inc(h))
        self._mark((s, v), reads, writes, ())

    def barrier(self):
        if self.probe:
            self.probe_deps = {}
            return
        allc = {("c_" + k): self.cnt["c_" + k] for k in ("pe", "act", "dve", "pool")}
        for s in self.cnt:
            if s.startswith("d_") and s != "d_cc":
                allc[s] = self.cnt[s]
        for eng in self.ENG:
            self._waits(eng, {s: v for s, v in allc.items() if v > 0}, eng == "sp")
        for b in self.live:
            if b.dsem not in self.free_d:
                self.free_d.append(b.dsem)
            b.dsem = None
        self.live = []

    def mm(self, out, lhsT, rhs, start, stop, reads, writes):
        self.op("pe", lambda e: e.matmul(out, lhsT=lhsT, rhs=rhs, start=start, stop=stop), reads, writes)

    def tr(self, out, in_, ident, reads, writes=(), pwrites=()):
        self.op("pe", lambda e: e.transpose(out, in_, ident), reads, writes, pwrites)

    def act(self, out, in_, func, reads, writes=(), pwrites=(), scale=1.0, bias=0.0, accum_out=None):
        if accum_out is None:
            self.op("act", lambda e: e.activation(out=out, in_=in_, func=func, scale=scale, bias=bias),
                    reads, writes, pwrites)
        else:
            self.op("act", lambda e: e.activation(out=out, in_=in_, func=func, scale=scale, bias=bias,
                                                  accum_out=accum_out), reads, writes, pwrites)

    def tt(self, eng, out, in0, in1, op, reads, writes=(), pwrites=()):
        self.op(eng, lambda e: e.tensor_tensor(out=out, in0=in0, in1=in1, op=op), reads, writes, pwrites)

    def ts(self, eng, out, in0, s1, op0, reads, writes=(), pwrites=(), s2=None, op1=None):
        if op1 is None:
            self.op(eng, lambda e: e.tensor_scalar(out=out, in0=in0, scalar1=s1, scalar2=None, op0=op0),
                    reads, writes, pwrites)
        else:
            self.op(eng, lambda e: e.tensor_scalar(out=out, in0=in0, scalar1=s1, scalar2=s2, op0=op0, op1=op1),
                    reads, writes, pwrites)

    def stt(self, out, in0, scalar, in1, op0, op1, reads, writes=(), pwrites=()):
        self.op("dve", lambda e: e.scalar_tensor_tensor(out=out, in0=in0, scalar=scalar, in1=in1, op0=op0, op1=op1),
                reads, writes, pwrites)

    def copy(self, eng, out, in_, reads, writes=(), pwrites=()):
        if eng == "act":
            self.act(out, in_, AF.Copy, reads, writes, pwrites)
        else:
            self.op(eng, lambda e: e.tensor_copy(out=out, in_=in_), reads, writes, pwrites)

    def memset(self, eng, ap, val, writes=(), pwrites=()):
        self.op(eng, lambda e: e.memset(ap, val), (), writes, pwrites)

    def recip(self, out, in_, reads, writes=(), pwrites=()):
        self.op("dve", lambda e: e.reciprocal(out=out, in_=in_), reads, writes, pwrites)

    def finish(self):
        deps = {}
        for b in self.out_bufs:
            for s, v in b.w.items():
                _merge(deps, s, v)
        self._waits("sp", deps, True)
        nc = self.nc
        with nc.Block() as block:
            @block.sync
            def _(e):
                for f in self.prog["sp"]:
                    f(e)

            @block.tensor
            def _(e):
                for f in self.prog["pe"]:
                    f(e)

            @block.scalar
            def _(e):
                for f in self.prog["act"]:
                    f(e)

            @block.vector
            def _(e):
                for f in self.prog["dve"]:
                    f(e)

            @block.gpsimd
            def _(e):
                for f in self.prog["pool"]:
                    f(e)
        self.es.close()
        self.root_es.close()


class Deferred:
    _DEFER = {"op", "dma", "mm", "tr", "act", "tt", "ts", "stt", "copy", "memset", "recip", "barrier", "allgather"}

    def __init__(self, P):
        self.__dict__["P"] = P
        self.__dict__["ops"] = []

    def __getattr__(self, name):
        a = getattr(self.P, name)
        if name in self._DEFER:
            ops = self.ops

            def f(*args, **kw):
                ops.append((a, args, kw))
            return f
        return a

    def __setattr__(self, name, val):
        setattr(self.P, name, val)


def replay_interleaved(A, B, P=None, K=14):
    na, nb_ = len(A.ops), len(B.ops)
    j = 0

    def ready(op, lag):
        if P is None:
            return True
        k_eff = K if lag < 40 else (K // 2 if lag < 150 else 0)
        if k_eff == 0:
            return True
        g, a2, k2 = op
        P.probe = True
        P.probe_deps = None
        g(*a2, **k2)
        P.probe = False
        deps = P.probe_deps or {}
        newest = max([P.tokpos.get((s_, v_), -10 ** 9) for s_, v_ in deps.items()], default=-10 ** 9)
        return P.gpos - newest >= k_eff

    for i, (f, a, k) in enumerate(A.ops):
        f(*a, **k)
        tgt = (i + 1) * nb_ // max(na, 1)
        while j < tgt and ready(B.ops[j], tgt - j):
            g, a2, k2 = B.ops[j]
            g(*a2, **k2)
            j += 1
    while j < nb_:
        g, a2, k2 = B.ops[j]
        g(*a2, **k2)
        j += 1
    return j


@contextlib.contextmanager
def phase_scope(P, scoped):
    if not scoped:
        yield
        return
    with contextlib.ExitStack() as es2:
        save = P.es
        P.es = es2
        yield
        P.barrier()
        P.es = save


C_DBL = [None]


def alloc_banks(P):
    dbl = [P.ps(f"dbank{i}", [128, 1024], F32) for i in range(4)]
    banks = []
    for d in dbl:
        banks += [d[:, 0:512], d[:, 512:1024]]
    return banks, dbl


class Rot:
    def __init__(self, P, name, shape, dtype, n):
        self.t = [P.sb(f"{name}{i}", shape, dtype) for i in range(n)]
        self.b = [P.buf(f"{name}{i}") for i in range(n)]
        self.i = 0
        self.n = n

    def next(self):
        i = self.i
        self.i = (i + 1) % self.n
        return self.t[i], self.b[i]


def setup_consts(P, cdram):
    C = {}
    c32 = P.sb("c32", [128, 1024], F32)
    b32 = P.buf("c32")
    P.dma("sp", c32[:], cdram, writes=[b32])
    C["c32"], C["b32"] = c32, b32
    cbf = P.sb("cbf", [128, 1024], BF16)
    bbf = P.buf("cbf")
    P.copy("dve", cbf[:], c32[:], [b32], [bbf])
    C["cbf"], C["bbf"] = cbf, bbf
    C["ident32"] = c32[:, 0:128]
    C["tri32"] = c32[:, 128:256]
    C["negmask32"] = c32[:, 512:640]
    C["ones32"] = c32[:, 640:768]
    C["ident"] = cbf[:, 0:128]
    C["strict"] = cbf[:, 256:384]
    C["U"] = cbf[:, 384:512]
    C["ones"] = cbf[:, 640:768]
    C["strict32"] = c32[:, 256:384]
    C["validcol"] = c32[:, 768:769]
    C["ntri32"] = c32[:, 896:1024]
    C["le"] = cbf[:, 128:256]
    return C


def host_consts():
    i = np.arange(128)
    ident = (i[:, None] == i[None, :]).astype(np.float32)
    tri = (i[:, None] <= i[None, :]).astype(np.float32)
    strict = (i[:, None] < i[None, :]).astype(np.float32)
    U = (i[:, None] >= i[None, :]).astype(np.float32)
    negmask = np.where(i[None, :] < i[:, None], -30000.0, 0.0).astype(np.float32)
    ones = np.ones((128, 128), np.float32)
    le = (i[:, None] <= i[None, :]).astype(np.float32)
    valid = np.broadcast_to((i >= 112).astype(np.float32)[:, None], (128, 128))
    return np.ascontiguousarray(np.concatenate([ident, tri, strict, U, negmask, ones, valid, -tri], axis=1))


def rms_rows(P, C, h_ap, hb, g_bc, gb, out_bf, out_b, small):
    junk, junk_b, ssq, ssq_b, sd, sd_b, rs, rs_b = small
    P.act(junk[:], h_ap, AF.Square, [hb], [junk_b, ssq_b], accum_out=ssq[:])
    P.act(sd[:], ssq[:], AF.Sqrt, [ssq_b], [sd_b], scale=1.0 / D, bias=EPS)
    P.recip(rs[:], sd[:], [sd_b], [rs_b])
    P.stt(out_bf, h_ap, rs[:, 0:1], g_bc, ALU.mult, ALU.mult, [hb, rs_b, gb], [out_b])


SSM_POS = [4 * r + 1 + i for r in range(4) for i in range(2)]
FGROUPS = [(0, 4), (4, 4), (8, 4), (12, 4), (16, 1)]


def emit_F(P, C, banks, bankb, variant, io, tag=""):
    do_mix = variant != "pre"
    last = variant == "last"
    groups = FGROUPS
    with contextlib.ExitStack() as es1:
        save1 = P.es
        P.es = es1
        h = P.sb("h", [128, TB, D], F32)
        hb = [P.buf(f"h{i}{tag}") for i in range(TB)]
        for i in range(TB):
            P.dma("sp", h[:, i, :], io["h_src"][i], reads=io.get("h_src_b", ()), writes=[hb[i]])
        gn = P.sb("gn", [128, D], F32)
        gn_b = P.buf("gn" + tag)
        P.dma("sp", gn[:], io["g_next"], writes=[gn_b])
        junk = P.sb("junk", [128, D], BF16)
        small = (junk, P.buf("junk"), P.sb("ssq", [128, 1], F32), P.buf("ssq"), P.sb("sd", [128, 1], F32), P.buf("sd"),
                 P.sb("rs", [128, 1], F32), P.buf("rs"))
        XT = {}

        def alloc_xT():
            XT["xT"] = P.sb("xT", [128, 8, TT], BF16)
            XT["xT_b"] = [P.buf(f"xT{i}{tag}") for i in range(TB)]
            XT["nrm"] = Rot(P, "nrm", [128, D], BF16, 2)

        def norm_T(g_tile, g_b, blk, bank_i):
            xT, xT_b, nrm = XT["xT"], XT["xT_b"], XT["nrm"]
            t, b = nrm.next()
            rms_rows(P, C, h[:, blk, :], hb[blk], g_tile[:], g_b, t[:], b, small)
            pst = banks[bank_i][:].bitcast(BF16)
            for k in range(8):
                P.tr(pst[:, k * 128:(k + 1) * 128], t[:, k * 128:(k + 1) * 128], C["ident"],
                     [b, C["bbf"]], [bankb[bank_i]])
            P.copy("act" if blk % 2 else "dve", xT[:, :, blk * 128:(blk + 1) * 128],
                   pst.rearrange("p (k t) -> p k t", k=8), [bankb[bank_i]], [xT_b[blk]])

        if do_mix:
            with contextlib.ExitStack() as es2:
                P.es = es2
                gssm = P.sb("gssm", [128, 8], F32)
                gssm_b = P.buf("gssm" + tag)
                P.dma("sp", gssm[:], io["g_ssm"], writes=[gssm_b])
                wout = P.sb("wout", [128, 16, D], BF16)
                wout_b = [P.buf(f"wout{i}{tag}") for i in range(4)]
                stg = Rot(P, "stgo" + tag, [128, 2, D], F32, 2)
                wv = io["w_out"].rearrange("(c p) f -> p c f", p=128)
                for i in range(8):
                    t, b = stg.next()
                    P.dma("sp", t[:], wv[:, 2 * i:2 * i + 2, :], writes=[b])
                    P.copy("pool", wout[:, 2 * i:2 * i + 2, :], t[:], [b], pwrites=[wout_b[i // 2]])
                om = Rot(P, "om" + tag, [128, 16, 512], BF16, 2)
                sq = Rot(P, "sq" + tag, [128, 512], BF16, 3)
                sdt = P.sb("sdt", [128, 512], F32)
                sdt_b = P.buf("sdt")
                rst = P.sb("rst", [128, 512], F32)
                rst_b = P.buf("rst")
                omx = io.get("om_extra")
                if omx:
                    omx = omx()
                for (b0, nb) in groups:
                    n = nb * 128
                    o, ob = om.next()
                    io["om_load"](o, ob, b0, nb, omx)
                    for c in range(8):
                        s_, s_b = sq.next()
                        P.act(s_[:, 0:n], o[:, SSM_POS[c], 0:n], AF.Square, [ob], [s_b])
                        P.mm(banks[0][:, 0:n], C["ones"], s_[:, 0:n], c == 0, c == 7, [s_b, C["bbf"]], [bankb[0]])
                    P.act(sdt[:, 0:n], banks[0][:, 0:n], AF.Sqrt, [bankb[0]], [sdt_b], scale=1.0 / D, bias=EPS)
                    P.recip(rst[:, 0:n], sdt[:, 0:n], [sdt_b], [rst_b])
                    for c in range(8):
                        P.stt(o[:, SSM_POS[c], 0:n], o[:, SSM_POS[c], 0:n], gssm[:, c:c + 1], rst[:, 0:n], ALU.mult, ALU.mult,
                              [ob, gssm_b, rst_b], pwrites=[ob])
                    for bi in range(nb):
                        blk = b0 + bi
                        for oc in range(2):
                            bk = 1 + (2 * blk + oc) % 4
                            for c in range(16):
                                P.mm(banks[bk][:], o[:, c, bi * 128:(bi + 1) * 128], wout[:, c, oc * 512:(oc + 1) * 512],
                                     c == 0, c == 15, [ob, wout_b[c // 4]], [bankb[bk]])
                            P.tt("dve", h[:, blk, oc * 512:(oc + 1) * 512], h[:, blk, oc * 512:(oc + 1) * 512],
                                 banks[bk][:], ALU.add, [bankb[bk], hb[blk]], pwrites=[hb[blk]])
                P.barrier()
                P.es = es1
            alloc_xT()
            xT, xT_b = XT["xT"], XT["xT_b"]
            with contextlib.ExitStack() as es2:
                P.es = es2
                gffn = P.sb("gffn", [128, D], F32)
                gffn_b = P.buf("gffn" + tag)
                P.dma("sp", gffn[:], io["g_ffn"], writes=[gffn_b])
                for blk in range(TB):
                    norm_T(gffn, gffn_b, blk, blk % 2)
                stg_g = Rot(P, "stg_g" + tag, [128, 8, 128], F32, 2)
                stg_u = Rot(P, "stg_u" + tag, [128, 8, 128], F32, 2)
                stg_d = Rot(P, "stg_d" + tag, [128, D], F32, 2)
                wgp = Rot(P, "wgp" + tag, [128, 8, 128], BF16, 2)
                wup = Rot(P, "wup" + tag, [128, 8, 128], BF16, 2)
                wdp = Rot(P, "wdp" + tag, [128, D], BF16, 2)
                sg = Rot(P, "sg" + tag, [128, 512], F32, 2)
                actT = Rot(P, "actT" + tag, [128, 512], BF16, 2)
                wgv = io["w_gate"].rearrange("(k p) f -> p k f", p=128)
                wuv = io["w_up"].rearrange("(k p) f -> p k f", p=128)
                wd_d = io["w_down"]
                mmi = 0
                for f in range(NFF):
                    tg, tg_b = stg_g.next()
                    tu, tu_b = stg_u.next()
                    td, td_b = stg_d.next()
                    P.dma("sp", tg[:], wgv[:, :, f * 128:(f + 1) * 128], writes=[tg_b])
                    P.dma("sp", tu[:], wuv[:, :, f * 128:(f + 1) * 128], writes=[tu_b])
                    P.dma("sp", td[:], wd_d[f * 128:(f + 1) * 128, :], writes=[td_b])
                    wgt, wg_b = wgp.next()
                    wut, wu_b = wup.next()
                    wdt, wd_b = wdp.next()
                    P.copy("pool", wgt[:], tg[:], [tg_b], [wg_b])
                    P.copy("pool", wut[:], tu[:], [tu_b], [wu_b])
                    P.copy("pool", wdt[:], td[:], [td_b], [wd_b])
                    for (b0, nb) in groups:
                        n = nb * 128
                        t0 = b0 * 128
                        rb = [xT_b[b0 + i] for i in range(nb)]
                        for k in range(8):
                            P.mm(banks[0][:, 0:n], wgt[:, k, :], xT[:, k, t0:t0 + n], k == 0, k == 7, rb + [wg_b], [bankb[0]])
                        for k in range(8):
                            P.mm(banks[1][:, 0:n], wut[:, k, :], xT[:, k, t0:t0 + n], k == 0, k == 7, rb + [wu_b], [bankb[1]])
                        s_, s_b = sg.next()
                        P.act(s_[:, 0:n], banks[0][:, 0:n], AF.Silu, [bankb[0]], [s_b])
                        a, a_b = actT.next()
                        P.tt("dve", a[:, 0:n], s_[:, 0:n], banks[1][:, 0:n], ALU.mult, [s_b, bankb[1]], [a_b])
                        for bi in range(nb):
                            blk = b0 + bi
                            for oc in range(2):
                                bk = 2 + mmi % 6
                                mmi += 1
                                P.mm(banks[bk][:], a[:, bi * 128:(bi + 1) * 128], wdt[:, oc * 512:(oc + 1) * 512],
                                     True, True, [a_b, wd_b], [bankb[bk]])
                                P.tt("dve", h[:, blk, oc * 512:(oc + 1) * 512], h[:, blk, oc * 512:(oc + 1) * 512],
                                     banks[bk][:], ALU.add, [bankb[bk], hb[blk]], pwrites=[hb[blk]])
                P.barrier()
                P.es = es1

        if not do_mix:
            alloc_xT()
        xT, xT_b = XT["xT"], XT["xT_b"]
        if last:
            yo = Rot(P, "yo" + tag, [128, D], F32, 2)
            junk, junk_b, ssq, ssq_b, sd, sd_b, rs, rs_b = small
            for blk in range(TB):
                t, b = yo.next()
                P.act(junk[:], h[:, blk, :], AF.Square, [hb[blk]], [junk_b, ssq_b], accum_out=ssq[:])
                P.act(sd[:], ssq[:], AF.Sqrt, [ssq_b], [sd_b], scale=1.0 / D, bias=EPS)
                P.recip(rs[:], sd[:], [sd_b], [rs_b])
                P.stt(t[:], h[:, blk, :], rs[:, 0:1], gn[:], ALU.mult, ALU.mult, [hb[blk], rs_b, gn_b], [b])
                P.dma("sp", io["y_d"][blk], t[:], reads=[b], pwrites=[io["y_b"]])
        else:
            for blk in range(TB):
                norm_T(gn, gn_b, blk, blk % 2)
                if do_mix:
                    P.dma("sp", io["h_dst"][blk], h[:, blk, :], reads=[hb[blk]], pwrites=[io["h_dst_b"]])
            io["hn_store"](xT, xT_b)
        P.barrier()
        P.es = save1


def build_F(variant):
    nc = bass.Bass("TRN2", target_bir_lowering=False)
    P = Prog(nc)
    do_mix = variant != "pre"
    last = variant == "last"
    cdram = P.dram("consts", [128, 1024], F32, "ExternalInput")
    io = {"h_src": P.dram("h_in", [TB, 128, D], F32, "ExternalInput"), "g_next": P.dram("g_next", [128, D], F32, "ExternalInput")}
    if do_mix:
        om_d = P.dram("omixT", [16, 128, TT], BF16, "ExternalInput")
        io.update({"w_out": P.dram("w_out", [2048, D], F32, "ExternalInput"), "w_gate": P.dram("w_gate", [D, DFF], F32, "ExternalInput"),
                   "w_up": P.dram("w_up", [D, DFF], F32, "ExternalInput"), "w_down": P.dram("w_down", [DFF, D], F32, "ExternalInput"),
                   "g_ssm": P.dram("g_ssm", [128, 8], F32, "ExternalInput"), "g_ffn": P.dram("g_ffn", [128, D], F32, "ExternalInput")})

        def om_load(o, ob, b0, nb, omx):
            P.dma("sp", o[:, :, 0:nb * 128], om_d[:, :, b0 * 128:(b0 + nb) * 128].rearrange("c p t -> p c t"), writes=[ob])
        io["om_load"] = om_load
    if last:
        io["y_d"] = P.dram("y_out", [TB, 128, D], F32, "ExternalOutput")
        io["y_b"] = P.buf("y_out")
        P.out_bufs.append(io["y_b"])
    else:
        hnT_d = P.dram("hnT_out", [8, 128, TT], BF16, "ExternalOutput")
        hnT_db = P.buf("hnT_out")
        P.out_bufs.append(hnT_db)

        def hn_store(xT, xT_b):
            P.dma("sp", hnT_d.rearrange("k p t -> p k t"), xT[:], reads=xT_b, pwrites=[hnT_db])
        io["hn_store"] = hn_store
        if do_mix:
            io["h_dst"] = P.dram("h_out", [TB, 128, D], F32, "ExternalOutput")
            io["h_dst_b"] = P.buf("h_out")
            P.out_bufs.append(io["h_dst_b"])
    C = setup_consts(P, cdram)
    banks, dbl = alloc_banks(P)
    C_DBL[0] = dbl
    bankb = [P.buf(f"bank{i}", excl=True) for i in range(8)]
    emit_F(P, C, banks, bankb, variant, io)
    P.finish()
    return nc


QG = [(4 * g, min(4, NB - 4 * g)) for g in range(17)]


def load_cast(P, dst_ap, dst_b, src_ap, shape, name, eng="pool", pw=False):
    t = P.sb(name + "_stg", shape, F32)
    b = P.buf(name + "_stg")
    P.dma("sp", t[:], src_ap, writes=[b])
    if pw:
        P.copy(eng, dst_ap, t[:], [b], pwrites=[dst_b])
    else:
        P.copy(eng, dst_ap, t[:], [b], [dst_b])
    return t, b


SB_BANKS_FULL = dict(z=(0, 1), c=((2, 3), (4, 5)), o=(6, 7))
SB_BANKS_SHARED = dict(z=(0, 1), c=((2, 3), (2, 3)), o=(4, 4))


def emit_sb(P, C, banks, bankb, hn_load, wsb_d, out_store, hn_rot, scoped=True, bk=SB_BANKS_FULL):
    import os
    with phase_scope(P, scoped):
        w = P.sb("sb_w", [128, 8, 384], BF16)
        w_b = P.buf("sb_w")
        with contextlib.ExitStack() as es3:
            prev_es = P.es
            P.es = es3
            load_cast(P, w[:], w_b, wsb_d.rearrange("(k p) f -> p k f", p=128), [128, 8, 384], "sb_w")
            P.barrier()
            P.es = prev_es
        zer = P.sb("sb_zero", [128, 512], BF16)
        zer_b = P.buf("sb_zero")
        P.memset("dve", zer[:], 0.0, [zer_b])
        qT = P.sb("sb_qT", [128, L], BF16)
        kT = P.sb("sb_kT", [128, L], BF16)
        vE = P.sb("sb_vE", [128, NB, 128], BF16)
        vO = P.sb("sb_vO", [128, NB, 128], BF16)
        qT_b = [P.buf(f"sb_qT{g}") for g in range(17)]
        kT_b = [P.buf(f"sb_kT{g}") for g in range(17)]
        v_b = [P.buf(f"sb_v{g}") for g in range(17)]
        P.memset("dve", vE[:].rearrange("p b f -> p (b f)"), 0.0, pwrites=v_b)
        P.memset("pool", vO[:].rearrange("p b f -> p (b f)"), 0.0, pwrites=v_b)
        for g, (b0, nb) in enumerate(QG[:int(os.environ.get("SB_NPROJ", "17"))]):
            n = nb * 128
            t0 = b0 * 128
            hn, hn_b = hn_rot.next()
            hn_load(P, hn, hn_b, g)
            for k in range(8):
                P.mm(banks[0][:, 0:n], w[:, k, 0:128], hn[:, k, 0:n], k == 0, k == 7, [w_b, hn_b], [bankb[0]])
            P.copy("act", qT[:, t0:t0 + n], banks[0][:, 0:n], [bankb[0]], [qT_b[g]])
            for k in range(8):
                P.mm(banks[1][:, 0:n], w[:, k, 128:256], hn[:, k, 0:n], k == 0, k == 7, [w_b, hn_b], [bankb[1]])
            P.copy("dve", kT[:, t0:t0 + n], banks[1][:, 0:n], [bankb[1]], [kT_b[g]])
            for bi in range(nb):
                for k in range(8):
                    P.mm(banks[2][:, bi * 128:(bi + 1) * 128], hn[:, k, bi * 128:(bi + 1) * 128], w[:, k, 256:384],
                         k == 0, k == 7, [w_b, hn_b], [bankb[2]])
            pv = banks[2][:, 0:n].rearrange("p (b f) -> p b f", f=128)
            P.copy("act", vE[:, b0:b0 + nb, 0:64], pv[:, :, 0:64], [bankb[2]], pwrites=[v_b[g]])
            P.copy("dve", vO[:, b0:b0 + nb, 64:128], pv[:, :, 64:128], [bankb[2]], pwrites=[v_b[g]])
        e_r = Rot(P, "sb_e", [128, 512], F32, 3)
        sp_r = Rot(P, "sb_sp", [128, 512], BF16, 4)
        x_r = Rot(P, "sb_x", [128, 512], F32, 2)
        a_r = Rot(P, "sb_a", [128, 512], BF16, 4)
        o_r = Rot(P, "sb_o", [128, 512], BF16, 2)
        steps = []
        for g, (b0, nb) in enumerate(QG):
            gend = b0 + nb
            first = True
            for kb in range(gend - 1, -1, -1):
                for hd in range(2):
                    qb0 = max(kb, b0)
                    steps.append(dict(g=g, hd=hd, kb=kb, qs=qb0 * 128, n=(gend - qb0) * 128, co=(qb0 - b0) * 128,
                                      diag=kb >= b0, first=(kb == gend - 1), firstg=(kb == gend - 1 and hd == 0),
                                      lastg=(kb == 0 and hd == 1), ncols=nb * 128, t0=b0 * 128))
        if os.environ.get("SB_MAXG"):
            steps = [st for st in steps if st["g"] < int(os.environ["SB_MAXG"])]
        S = len(steps)
        for st in steps:
            st["e"] = st["sp"] = st["x"] = st["a"] = None

        def stageA(i):
            st = steps[i]
            hs = slice(64 * st["hd"], 64 * st["hd"] + 64)
            n, qs, kb = st["n"], st["qs"], st["kb"]
            zb = bk["z"][i % 2]
            P.mm(banks[zb][:, 0:n], kT[hs, kb * 128:(kb + 1) * 128], qT[hs, qs:qs + n], True, True,
                 [kT_b[kb // 4], qT_b[st["g"]]], [bankb[zb]])
            e, e_b = e_r.next()
            P.act(e[:, 0:n], banks[zb][:, 0:n], AF.Exp, [bankb[zb]], [e_b], scale=0.125)
            if st["diag"]:
                P.tt("dve", e[:, 0:128], e[:, 0:128], C["strict32"], ALU.mult, [e_b, C["b32"]], pwrites=[e_b])
            sp, sp_b = sp_r.next()
            P.act(sp[:, 0:n], e[:, 0:n], AF.Ln, [e_b], [sp_b], bias=1.0)
            st["e"], st["sp"] = (e, e_b), (sp, sp_b)

        def cbank(st):
            return bk["c"][st["g"] % 2][st["hd"]]

        def stageB(i):
            st = steps[i]
            n, co = st["n"], st["co"]
            cb = cbank(st)
            if st["first"]:
                P.mm(banks[cb][:, 0:st["ncols"]], zer[:, 0:128], zer[:, 0:st["ncols"]], True, True, [zer_b], [bankb[cb]])
            sp, sp_b = st["sp"]
            P.mm(banks[cb][:, co:co + n], C["U"], sp[:, 0:n], False, True, [sp_b, C["bbf"]], [bankb[cb]])
            x, x_b = x_r.next()
            P.act(x[:, 0:n], banks[cb][:, co:co + n], AF.Exp, [bankb[cb]], [x_b], scale=-1.0)
            e, e_b = st["e"]
            a, a_b = a_r.next()
            P.tt("dve", a[:, 0:n], e[:, 0:n], x[:, 0:n], ALU.mult, [e_b, x_b], [a_b])
            st["a"] = (a, a_b)

        def stageC(i):
            st = steps[i]
            n, co = st["n"], st["co"]
            cb = cbank(st)
            sp, sp_b = st["sp"]
            P.mm(banks[cb][:, co:co + n], C["strict"], sp[:, 0:n], False, True, [sp_b, C["bbf"]], [bankb[cb]])

        def stageD(i):
            st = steps[i]
            n, co, kb, g = st["n"], st["co"], st["kb"], st["g"]
            ob = bk["o"][g % 2]
            if st["firstg"]:
                P.mm(banks[ob][:, 0:st["ncols"]], zer[:, 0:128], zer[:, 0:st["ncols"]], True, True, [zer_b], [bankb[ob]])
            a, a_b = st["a"]
            vv = vE if st["hd"] == 0 else vO
            P.mm(banks[ob][:, co:co + n], vv[:, kb, :], a[:, 0:n], False, True, [a_b, v_b[kb // 4]], [bankb[ob]])
            if st["lastg"]:
                nc_ = st["ncols"]
                o, o_b = o_r.next()
                P.copy("act", o[:, 0:nc_], banks[ob][:, 0:nc_], [bankb[ob]], [o_b])
                out_store(P, 0, o, st["t0"], nc_, o_b)

        if S:
            stageA(0)
        for i in range(S + 2):
            if i + 1 < S:
                stageA(i + 1)
            if i < S:
                stageB(i)
            if 0 <= i - 1 < S:
                stageC(i - 1)
            if 0 <= i - 2 < S:
                stageD(i - 2)


MLA_STAB = False


def emit_mla(P, C, banks, bankb, hn_load, W, out_store, hn_rot):
    SC = 1.0 / np.sqrt(96.0)
    with contextlib.ExitStack() as es2:
        save_es = P.es
        P.es = es2
        wm = P.sb("ml_wm", [128, 8, 832], BF16)
        wm_b = P.buf("ml_wm")
        wA = P.sb("ml_wA", [128, 3, 192], BF16)
        wB = P.sb("ml_wB", [128, 3, 192], BF16)
        wK = P.sb("ml_wK", [128, 2, 128], BF16)
        wV = P.sb("ml_wV", [128, 2, 128], BF16)
        wu_b = P.buf("ml_wu")
        gq = P.sb("ml_gq", [128, 3], F32)
        gkv = P.sb("ml_gkv", [128, 2], F32)
        g_b = P.buf("ml_g")
        P.dma("sp", gq[:], W["g_q"], pwrites=[g_b])
        P.dma("sp", gkv[:], W["g_kv"], pwrites=[g_b])
        with contextlib.ExitStack() as es3:
            P.es = es3
            stg = Rot(P, "ml_stg", [128, 2, 832], F32, 2)
            wv_ = W["w_mla_in"].rearrange("(k p) f -> p k f", p=128)
            for i in range(4):
                t, b = stg.next()
                P.dma("sp", t[:], wv_[:, 2 * i:2 * i + 2, :], writes=[b])
                P.copy("pool", wm[:, 2 * i:2 * i + 2, :], t[:], [b], pwrites=[wm_b])
            sA = P.sb("ml_sA", [128, 3, 192], F32)
            sB = P.sb("ml_sB", [128, 3, 192], F32)
            sK = P.sb("ml_sK", [128, 2, 128], F32)
            sV = P.sb("ml_sV", [128, 2, 128], F32)
            s_b = P.buf("ml_s")
            P.dma("sp", sA[:], W["w_uqA"].rearrange("(k p) f -> p k f", p=128), pwrites=[s_b])
            P.dma("sp", sB[:], W["w_uqB"].rearrange("(k p) f -> p k f", p=128), pwrites=[s_b])
            P.dma("sp", sK[:], W["w_ukvk"].rearrange("(k p) f -> p k f", p=128), pwrites=[s_b])
            P.dma("sp", sV[:], W["w_ukvv"].rearrange("(k p) f -> p k f", p=128), pwrites=[s_b])
            for c in range(3):
                P.ts("dve", wA[:, c, :], sA[:, c, :], gq[:, c:c + 1], ALU.mult, [s_b, g_b], pwrites=[wu_b])
                P.ts("dve", wB[:, c, :], sB[:, c, :], gq[:, c:c + 1], ALU.mult, [s_b, g_b], pwrites=[wu_b])
            for c in range(2):
                P.ts("dve", wK[:, c, :], sK[:, c, :], gkv[:, c:c + 1], ALU.mult, [s_b, g_b], pwrites=[wu_b])
                P.ts("dve", wV[:, c, :], sV[:, c, :], gkv[:, c:c + 1], ALU.mult, [s_b, g_b], pwrites=[wu_b])
            P.barrier()
            P.es = es2
        qT = P.sb("ml_qT", [128, 2, L], BF16)
        kT = P.sb("ml_kT", [128, 2, L], BF16)
        va = P.sb("ml_va", [128, NB, 2, 128], BF16)
        qT_b = [P.buf(f"ml_qT{g}") for g in range(17)]
        kT_b = [P.buf(f"ml_kT{g}") for g in range(17)]
        v_b = [P.buf(f"ml_v{g}") for g in range(17)]
        P.memset("dve", va[:].rearrange("p b h f -> p (b h f)"), 1.0, pwrites=v_b)
        cq = P.sb("ml_cq", [128, 3, 512], BF16)
        sqq = P.sb("ml_sqq", [128, 3, 512], BF16)
        ckv = P.sb("ml_ckv", [128, 2, 512], BF16)
        sqkv = P.sb("ml_sqkv", [128, 2, 512], BF16)
        c_b = P.buf("ml_c")
        sdq = P.sb("ml_sdq", [128, 512], F32)
        rq = P.sb("ml_rq", [128, 512], F32)
        sdk = P.sb("ml_sdk", [128, 512], F32)
        rkv = P.sb("ml_rkv", [128, 512], F32)
        sdt = P.sb("ml_sdt", [128, 4], F32)
        rkt = P.sb("ml_rkt", [128, 4], F32)
        r_b = P.buf("ml_r")
        rp = P.sb("ml_rp", [128, 2, 512], F32)
        rp_b = P.buf("ml_rp")
        cs = P.sb("ml_cs", [128, 2, 512], F32)
        cs_b = P.buf("ml_cs")
        t1 = P.sb("ml_t1", [128, 512], F32)
        t2 = P.sb("ml_t2", [128, 512], F32)
        t_b = P.buf("ml_t")
        bc = [0]
        nsq = P.sb("ml_nsq", [128, 512], BF16)
        nsq_b = P.buf("ml_nsq")
        nmx = P.sb("ml_nmx", [128, 1], F32)
        nmx_b = P.buf("ml_nmx")
        nrun = P.sb("ml_nrun", [128, 4], F32)
        nrun_b = P.buf("ml_nrun")
        negm = P.sb("ml_negm", [128, 2], F32)
        negm_b = P.buf("ml_negm")
        P.memset("dve", nrun[:], 0.0, [nrun_b])

        def nbk():
            bc[0] = (bc[0] + 1) % 8
            return bc[0]

        def norm_max(src, h, col, n, src_b):
            P.act(nsq[0:96, 0:n], src, AF.Square, [src_b], [nsq_b])
            bk = nbk()
            P.mm(banks[bk][:, 0:n], C["ones"][0:96, :], nsq[0:96, 0:n], True, True, [nsq_b, C["bbf"]], [bankb[bk]])
            P.op("dve", lambda e, bk=bk, n=n: e.tensor_reduce(out=nmx[:], in_=banks[bk][:, 0:n], axis=mybir.AxisListType.X,
                                                              op=ALU.max), [], [bankb[bk], nmx_b])
            P.tt("dve", nrun[:, col:col + 1], nrun[:, col:col + 1], nmx[:], ALU.max, [nmx_b, nrun_b], pwrites=[nrun_b])

        for g, (b0, nb) in enumerate(QG):
            n = nb * 128
            t0 = b0 * 128
            hn, hn_b = hn_rot.next()
            hn_load(P, hn, hn_b, g)
            P.dma("sp", rp[64:96, :, 0:n], W["rope"][:, :, t0:t0 + n].rearrange("a r t -> r a t"), writes=[rp_b])
            for c in range(3):
                bk = nbk()
                for k in range(8):
                    P.mm(banks[bk][:, 0:n], wm[:, k, c * 128:(c + 1) * 128], hn[:, k, 0:n], k == 0, k == 7,
                         [wm_b, hn_b], [bankb[bk]])
                P.act(sqq[:, c, 0:n], banks[bk][:, 0:n], AF.Square, [bankb[bk]], writes=[c_b] if c == 0 else (),
                      pwrites=() if c == 0 else [c_b])
                P.copy("dve", cq[:, c, 0:n], banks[bk][:, 0:n], [bankb[bk]], pwrites=[c_b])
            for c in range(2):
                bk = nbk()
                for k in range(8):
                    P.mm(banks[bk][:, 0:n], wm[:, k, 384 + c * 128:384 + (c + 1) * 128], hn[:, k, 0:n], k == 0, k == 7,
                         [wm_b, hn_b], [bankb[bk]])
                P.act(sqkv[:, c, 0:n], banks[bk][:, 0:n], AF.Square, [bankb[bk]], pwrites=[c_b])
                P.copy("dve", ckv[:, c, 0:n], banks[bk][:, 0:n], [bankb[bk]], pwrites=[c_b])
            bk = nbk()
            for c in range(3):
                P.mm(banks[bk][:, 0:n], C["ones"], sqq[:, c, 0:n], c == 0, c == 2, [c_b, C["bbf"]], [bankb[bk]])
            P.act(sdq[:, 0:n], banks[bk][:, 0:n], AF.Sqrt, [bankb[bk]], [r_b], scale=1.0 / 384, bias=EPS)
            P.recip(rq[:, 0:n], sdq[:, 0:n], [r_b], pwrites=[r_b])
            bk = nbk()
            for c in range(2):
                P.mm(banks[bk][:, 0:n], C["ones"], sqkv[:, c, 0:n], c == 0, c == 1, [c_b, C["bbf"]], [bankb[bk]])
            P.act(sdk[:, 0:n], banks[bk][:, 0:n], AF.Sqrt, [bankb[bk]], pwrites=[r_b], scale=1.0 / 256, bias=EPS)
            P.recip(rkv[:, 0:n], sdk[:, 0:n], [r_b], pwrites=[r_b])
            bk = nbk()
            for bi in range(nb):
                for c in range(2):
                    P.mm(banks[bk][:, bi:bi + 1], sqkv[:, c, bi * 128:(bi + 1) * 128], C["ones"][:, 0:1], c == 0, c == 1,
                         [c_b, C["bbf"]], [bankb[bk]])
            P.act(sdt[:, 0:nb], banks[bk][:, 0:nb], AF.Sqrt, [bankb[bk]], pwrites=[r_b], scale=1.0 / 256, bias=EPS)
            P.recip(rkt[:, 0:nb], sdt[:, 0:nb], [r_b], pwrites=[r_b])
            P.tt("dve", cs[64:96, 0, 0:n], rp[64:96, 0, 0:n], rq[64:96, 0:n], ALU.mult, [rp_b, r_b], [cs_b])
            P.tt("dve", cs[64:96, 1, 0:n], rp[64:96, 1, 0:n], rq[64:96, 0:n], ALU.mult, [rp_b, r_b], pwrites=[cs_b])
            for h in range(2):
                ba = nbk()
                for c in range(3):
                    P.mm(banks[ba][0:96, 0:n], wA[:, c, h * 96:(h + 1) * 96], cq[:, c, 0:n], c == 0, c == 2,
                         [wu_b, c_b], [bankb[ba]])
                P.tt("dve", qT[0:64, h, t0:t0 + n], banks[ba][0:64, 0:n], rq[0:64, 0:n], ALU.mult, [bankb[ba], r_b],
                     pwrites=[qT_b[g]])
                P.tt("dve", t1[64:96, 0:n], banks[ba][64:96, 0:n], cs[64:96, 0, 0:n], ALU.mult, [bankb[ba], cs_b], [t_b])
                bb = nbk()
                for c in range(3):
                    P.mm(banks[bb][0:96, 0:n], wB[:, c, h * 96:(h + 1) * 96], cq[:, c, 0:n], c == 0, c == 2,
                         [wu_b, c_b], [bankb[bb]])
                P.tt("dve", t2[64:96, 0:n], banks[bb][64:96, 0:n], cs[64:96, 1, 0:n], ALU.mult, [bankb[bb], cs_b],
                     pwrites=[t_b])
                P.tt("dve", qT[64:96, h, t0:t0 + n], t1[64:96, 0:n], t2[64:96, 0:n], ALU.add, [t_b], pwrites=[qT_b[g]])
            for h in range(2):
                bk = nbk()
                for c in range(2):
                    P.mm(banks[bk][0:64, 0:n], wK[:, c, h * 64:(h + 1) * 64], ckv[:, c, 0:n], c == 0, c == 1,
                         [wu_b, c_b], [bankb[bk]])
                P.tt("dve", kT[0:64, h, t0:t0 + n], banks[bk][0:64, 0:n], rkv[0:64, 0:n], ALU.mult, [bankb[bk], r_b],
                     pwrites=[kT_b[g]])
            b1 = nbk()
            for k in range(8):
                P.mm(banks[b1][0:96, 0:n], wm[:, k, 640:736], hn[:, k, 0:n], k == 0, k == 7, [wm_b, hn_b], [bankb[b1]])
            P.tt("dve", t1[64:96, 0:n], banks[b1][64:96, 0:n], rp[64:96, 0, 0:n], ALU.mult, [bankb[b1], rp_b], [t_b])
            b2 = nbk()
            for k in range(8):
                P.mm(banks[b2][0:96, 0:n], wm[:, k, 736:832], hn[:, k, 0:n], k == 0, k == 7, [wm_b, hn_b], [bankb[b2]])
            P.tt("dve", t2[64:96, 0:n], banks[b2][64:96, 0:n], rp[64:96, 1, 0:n], ALU.mult, [bankb[b2], rp_b],
                 pwrites=[t_b])
            P.tt("dve", kT[64:96, 0, t0:t0 + n], t1[64:96, 0:n], t2[64:96, 0:n], ALU.add, [t_b], pwrites=[kT_b[g]])
            P.tt("pool", kT[64:96, 1, t0:t0 + n], t1[64:96, 0:n], t2[64:96, 0:n], ALU.add, [t_b], pwrites=[kT_b[g]])
            if MLA_STAB:
                for h in range(2):
                    norm_max(qT[0:96, h, t0:t0 + n], h, h, n, qT_b[g])
                    norm_max(kT[0:96, h, t0:t0 + n], h, 2 + h, n, kT_b[g])
            bk = nbk()
            for bi in range(nb):
                for c in range(2):
                    P.mm(banks[bk][:, bi * 128:(bi + 1) * 128], ckv[:, c, bi * 128:(bi + 1) * 128], wV[:, c, :],
                         c == 0, c == 1, [wu_b, c_b], [bankb[bk]])
            for bi in range(nb):
                blk = b0 + bi
                P.ts("dve", va[:, blk, 0, 0:64], banks[bk][:, bi * 128:bi * 128 + 64], rkt[:, bi:bi + 1], ALU.mult,
                     [bankb[bk], r_b], pwrites=[v_b[g]])
                P.ts("dve", va[:, blk, 1, 64:128], banks[bk][:, bi * 128 + 64:bi * 128 + 128], rkt[:, bi:bi + 1], ALU.mult,
                     [bankb[bk], r_b], pwrites=[v_b[g]])
            if g == 0:
                v0 = va[:, 0, :, :].rearrange("p h f -> p (h f)")
                P.ts("dve", v0, v0, C["validcol"], ALU.mult, [C["b32"]], pwrites=[v_b[0]])
        if MLA_STAB:
            P.tt("dve", negm[:], nrun[:, 0:2], nrun[:, 2:4], ALU.mult, [nrun_b], [negm_b])
            P.act(negm[:], negm[:], AF.Sqrt, [negm_b], pwrites=[negm_b], scale=float(SC * SC))
            P.ts("dve", negm[:], negm[:], -1.0, ALU.mult, [negm_b], pwrites=[negm_b])
        p_r = Rot(P, "ml_p", [128, 512], BF16, 4)
        o_r = Rot(P, "ml_o", [128, 512], BF16, 2)
        rc = P.sb("ml_rc", [128, 512], F32)
        rc_b = P.buf("ml_rc")
        steps = []
        for g, (b0, nb) in enumerate(QG):
            gend = b0 + nb
            for hd in range(2):
                for kb in range(gend):
                    qb0 = max(kb, b0)
                    steps.append(dict(g=g, hd=hd, kb=kb, qs=qb0 * 128, n=(gend - qb0) * 128, co=(qb0 - b0) * 128,
                                      diag=kb >= b0, first=(kb == 0), last=(kb == gend - 1), ncols=nb * 128, t0=b0 * 128))
        import os
        if os.environ.get("ML_MAXG"):
            steps = [st for st in steps if st["g"] < int(os.environ["ML_MAXG"])]
        S = len(steps)
        cur_o = {}

        def stA(i):
            st = steps[i]
            n, qs, kb, hd = st["n"], st["qs"], st["kb"], st["hd"]
            zb = i % 2
            P.mm(banks[zb][:, 0:n], kT[0:96, hd, kb * 128:(kb + 1) * 128], qT[0:96, hd, qs:qs + n], True, True,
                 [kT_b[kb // 4], qT_b[st["g"]]], [bankb[zb]])
            p, p_b = p_r.next()
            if MLA_STAB:
                P.act(p[:, 0:n], banks[zb][:, 0:n], AF.Exp, [bankb[zb], negm_b], [p_b], scale=SC, bias=negm[:, hd:hd + 1])
            else:
                P.act(p[:, 0:n], banks[zb][:, 0:n], AF.Exp, [bankb[zb]], [p_b], scale=SC)
            if st["diag"]:
                P.tt("dve", p[:, 0:128], p[:, 0:128], C["le"], ALU.mult, [p_b, C["bbf"]], pwrites=[p_b])
            st["p"] = (p, p_b)

        def stB(i):
            st = steps[i]
            n, co, kb, hd, g = st["n"], st["co"], st["kb"], st["hd"], st["g"]
            ob = 2 + 2 * (g % 2) + hd
            p, p_b = st["p"]
            P.mm(banks[ob][:, co:co + n], va[:, kb, hd, :], p[:, 0:n], st["first"], st["last"], [p_b, v_b[kb // 4]],
                 [bankb[ob]])
            if st["last"]:
                nc_ = st["ncols"]
                if hd == 0:
                    cur_o[g] = o_r.next()
                o, o_b = cur_o[g]
                dn = slice(64, 128) if hd == 0 else slice(0, 64)
                nm = slice(0, 64) if hd == 0 else slice(64, 128)
                P.ts("dve", rc[dn, 0:nc_], banks[ob][dn, 0:nc_], 1e-30, ALU.add, [bankb[ob]], [rc_b])
                P.recip(rc[dn, 0:nc_], rc[dn, 0:nc_], [rc_b], pwrites=[rc_b])
                P.tt("dve", o[nm, 0:nc_], banks[ob][nm, 0:nc_], rc[dn, 0:nc_], ALU.mult, [bankb[ob], rc_b],
                     writes=[o_b] if hd == 0 else (), pwrites=() if hd == 0 else [o_b])
                if hd == 1:
                    out_store(P, 3, o, st["t0"], nc_, o_b)

        if S:
            stA(0)
        for i in range(S):
            if i + 1 < S:
                stA(i + 1)
            stB(i)
        P.barrier()
        P.es = save_es


def emit_ssd(P, C, banks, bankb, hn_load, W, out_store, hn_rot, scoped=True, bank_list=tuple(range(8))):
    with phase_scope(P, scoped):
        ws = P.sb("sd_ws", [128, 8, 768], BF16)
        ws_b = P.buf("sd_ws")
        wdt = P.sb("sd_wdt", [128, 8, 4], BF16)
        wdt_b = P.buf("sd_wdt")
        cw = P.sb("sd_cw", [128, 4, 4], F32)
        cb = P.sb("sd_cb", [128, 4], F32)
        dtb = P.sb("sd_dtb", [128, 4], F32)
        alog = P.sb("sd_alog", [128, 4], F32)
        abc = P.sb("sd_abc", [128, 4], F32)
        dsk = P.sb("sd_dsk", [128, 4], F32)
        sm_b = P.buf("sd_small")
        P.dma("sp", cw[:], W["conv_w"], pwrites=[sm_b])
        P.dma("sp", cb[:], W["conv_b"], pwrites=[sm_b])
        P.dma("sp", dtb[:], W["dt_bias"], pwrites=[sm_b])
        P.dma("sp", alog[:], W["a_log"], pwrites=[sm_b])
        P.dma("sp", dsk[:], W["d_skip"], pwrites=[sm_b])
        a_b = P.buf("sd_a")
        P.act(abc[:], alog[:], AF.Exp, [sm_b], [a_b])
        P.ts("dve", abc[:], abc[:], -1.0, ALU.mult, [a_b], pwrites=[a_b])
        with contextlib.ExitStack() as es3:
            prev_es = P.es
            P.es = es3
            stg = Rot(P, "sd_stg", [128, 2, 768], F32, 2)
            wv_ = W["w_ssm"].rearrange("(k p) f -> p k f", p=128)
            for i in range(4):
                t, b = stg.next()
                P.dma("sp", t[:], wv_[:, 2 * i:2 * i + 2, :], writes=[b])
                P.copy("pool", ws[:, 2 * i:2 * i + 2, :], t[:], [b], pwrites=[ws_b])
            sdt_ = P.sb("sd_sdt", [128, 8, 4], F32)
            sdt_b = P.buf("sd_sdt")
            P.dma("sp", sdt_[:], W["w_dt"].rearrange("(k p) f -> p k f", p=128), writes=[sdt_b])
            P.copy("pool", wdt[:], sdt_[:], [sdt_b], [wdt_b])
            P.barrier()
            P.es = prev_es
        raw = P.sb("sd_raw", [128, 4, 515], F32)
        raw_b = [P.buf(f"sd_raw{c}") for c in range(4)]
        P.memset("dve", raw[:].rearrange("p c t -> p (c t)"), 0.0, pwrites=raw_b)
        nm4 = P.sb("sd_nm4", [128, 4, 128], F32)
        nm4_b = P.buf("sd_nm4")
        for h in range(4):
            P.copy("pool", nm4[:, h, :], C["negmask32"], [C["b32"]], pwrites=[nm4_b])
        xcT = P.sb("sd_xcT", [128, 4, 512], BF16)
        xcT_b = P.buf("sd_xcT")
        sz = P.sb("sd_sz", [128, 2, 512], F32)
        sz_b = P.buf("sd_sz")
        xtok = P.sb("sd_xtok", [128, 4, 256], BF16)
        btok = P.sb("sd_btok", [128, 4, 128], BF16)
        tok_b = P.buf("sd_tok")
        dtt = P.sb("sd_dtt", [128, 4, 4], F32)
        dte = P.sb("sd_dte", [128, 4, 4], F32)
        dt = P.sb("sd_dt", [128, 4, 4], F32)
        dA = P.sb("sd_dA", [128, 4, 4], F32)
        dt_b = P.buf("sd_dt")
        acs = P.sb("sd_acs", [128, 4, 4], F32)
        eA = P.sb("sd_eA", [128, 4, 4], F32)
        dtmp = P.sb("sd_dtmp", [128, 4, 4], F32)
        dend = P.sb("sd_dend", [128, 4, 4], F32)
        cd = P.sb("sd_cd", [128, 4, 4], F32)
        dec_b = P.buf("sd_dec")
        dAbc_r = Rot(P, "sd_dAbc", [128, 4, 128], F32, 2)
        Ld_r = Rot(P, "sd_Ld", [128, 4, 128], F32, 2)
        GT_r = Rot(P, "sd_GT", [128, 4, 128], BF16, 4)
        Xd_r = Rot(P, "sd_Xd", [128, 256], BF16, 4)
        Xdd_r = Rot(P, "sd_Xdd", [128, 256], BF16, 2)
        xsk_r = Rot(P, "sd_xsk", [128, 256], F32, 2)
        yt_r = Rot(P, "sd_yt", [128, 256], F32, 2)
        y_r = Rot(P, "sd_y", [128, 256], F32, 2)
        Sbf_r = Rot(P, "sd_Sbf", [128, 256], BF16, 5)
        acc_r = Rot(P, "sd_acc", [128, 512], F32, 2)
        sig_r = Rot(P, "sd_sig", [128, 512], F32, 2)
        S = P.sb("sd_S", [128, 256], F32)
        S_b = P.buf("sd_S")
        P.memset("dve", S[:], 0.0, [S_b])
        Sprev = Sbf_r.next()
        P.memset("dve", Sprev[0][:], 0.0, [Sprev[1]])
        og = Rot(P, "sd_og", [128, 2, 512], BF16, 2)
        bc = [0]

        def nbk():
            bc[0] = (bc[0] + 1) % len(bank_list)
            return bank_list[bc[0]]

        def bc3(ap2, nlast):
            return ap2.unsqueeze(2).to_broadcast([128, 4, nlast])

        for g, (b0, nb) in enumerate(QG):
            n = nb * 128
            t0 = b0 * 128
            hn, hn_b = hn_rot.next()
            hn_load(P, hn, hn_b, g)
            for c in range(2):
                bk = nbk()
                for k in range(8):
                    P.mm(banks[bk][:, 0:n], ws[:, k, c * 128:(c + 1) * 128], hn[:, k, 0:n], k == 0, k == 7,
                         [ws_b, hn_b], [bankb[bk]])
                sg, sg_b = sig_r.next()
                P.act(sg[:, 0:n], banks[bk][:, 0:n], AF.Exp, [bankb[bk]], [sg_b], scale=-1.0)
                P.act(sg[:, 0:n], sg[:, 0:n], AF.Ln, [sg_b], pwrites=[sg_b], bias=1.0)
                P.act(sg[:, 0:n], sg[:, 0:n], AF.Exp, [sg_b], pwrites=[sg_b], scale=-1.0)
                P.tt("dve", sz[:, c, 0:n], banks[bk][:, 0:n], sg[:, 0:n], ALU.mult, [bankb[bk], sg_b],
                     writes=[sz_b] if c == 0 else (), pwrites=() if c == 0 else [sz_b])
            for cp in range(2):
                accs = []
                for c in (2 * cp, 2 * cp + 1):
                    bk = nbk()
                    for k in range(8):
                        P.mm(banks[bk][:, 0:n], ws[:, k, 256 + c * 128:256 + (c + 1) * 128], hn[:, k, 0:n], k == 0, k == 7,
                             [ws_b, hn_b], [bankb[bk]])
                    P.copy("dve", raw[:, c, 3:3 + n], banks[bk][:, 0:n], [bankb[bk]], pwrites=[raw_b[c]])
                    accs.append(acc_r.next())
                for i, c in enumerate((2 * cp, 2 * cp + 1)):
                    acc, acc_b = accs[i]
                    P.ts("dve", acc[:, 0:n], raw[:, c, 3:3 + n], cw[:, c, 3:4], ALU.mult, [raw_b[c], sm_b], [acc_b],
                         s2=cb[:, c:c + 1], op1=ALU.add)
                for tap in (2, 1, 0):
                    for i, c in enumerate((2 * cp, 2 * cp + 1)):
                        acc, acc_b = accs[i]
                        P.stt(acc[:, 0:n], raw[:, c, tap:tap + n], cw[:, c, tap:tap + 1], acc[:, 0:n], ALU.mult, ALU.add,
                              [raw_b[c], sm_b, acc_b], pwrites=[acc_b])
                for i, c in enumerate((2 * cp, 2 * cp + 1)):
                    acc, acc_b = accs[i]
                    sg, sg_b = sig_r.next()
                    P.act(sg[:, 0:n], acc[:, 0:n], AF.Exp, [acc_b], [sg_b], scale=-1.0)
                    P.act(sg[:, 0:n], sg[:, 0:n], AF.Ln, [sg_b], pwrites=[sg_b], bias=1.0)
                    P.act(sg[:, 0:n], sg[:, 0:n], AF.Exp, [sg_b], pwrites=[sg_b], scale=-1.0)
                    P.tt("pool", xcT[:, c, 0:n], acc[:, 0:n], sg[:, 0:n], ALU.mult, [acc_b, sg_b],
                         writes=[xcT_b] if c == 0 else (), pwrites=() if c == 0 else [xcT_b])
                    P.copy("pool", raw[:, c, 0:3], raw[:, c, n:n + 3], [raw_b[c], acc_b], pwrites=[raw_b[c]])
            bk = nbk()
            for bi in range(nb):
                for k in range(8):
                    P.mm(banks[bk][:, bi * 4:(bi + 1) * 4], hn[:, k, bi * 128:(bi + 1) * 128], wdt[:, k, :], k == 0, k == 7,
                         [wdt_b, hn_b], [bankb[bk]])
            pdt = banks[bk][:, 0:4 * nb].rearrange("p (b h) -> p b h", h=4)
            P.tt("dve", dtt[:, 0:nb, :], pdt, dtb[:].unsqueeze(1).to_broadcast([128, nb, 4]), ALU.add,
                 [bankb[bk], sm_b], [dt_b])
            P.act(dte[:, 0:nb, :], dtt[:, 0:nb, :], AF.Exp, [dt_b], pwrites=[dt_b])
            P.act(dt[:, 0:nb, :], dte[:, 0:nb, :], AF.Ln, [dt_b], pwrites=[dt_b], bias=1.0)
            if g == 0:
                P.ts("dve", dt[:, 0, :], dt[:, 0, :], C["validcol"], ALU.mult, [dt_b, C["b32"]], pwrites=[dt_b])
            P.tt("dve", dA[:, 0:nb, :], dt[:, 0:nb, :], abc[:].unsqueeze(1).to_broadcast([128, nb, 4]), ALU.mult,
                 [dt_b, a_b], pwrites=[dt_b])
            for bi in range(nb):
                bk = nbk()
                pst = banks[bk][:].bitcast(BF16)
                for c in range(3):
                    P.tr(pst[:, c * 128:(c + 1) * 128], xcT[:, c, bi * 128:(bi + 1) * 128], C["ident"],
                         [xcT_b, C["bbf"]], [bankb[bk]])
                P.copy("dve", xtok[:, bi, :], pst[:, 0:256], [bankb[bk]], writes=[tok_b] if bi == 0 else (),
                       pwrites=() if bi == 0 else [tok_b])
                P.copy("dve", btok[:, bi, :], pst[:, 256:384], [bankb[bk]], pwrites=[tok_b])
            o, o_b = og.next()
            bA = nbk()
            for bi in range(nb):
                P.mm(banks[bA][:, bi * 8:bi * 8 + 4], C["tri32"], dA[:, bi, :], True, True, [dt_b, C["b32"]], [bankb[bA]])
                P.mm(banks[bA][:, bi * 8 + 4:bi * 8 + 8], C["ones32"], dA[:, bi, :], True, True, [dt_b, C["b32"]], [bankb[bA]])
            pA = banks[bA][:, 0:8 * nb].rearrange("p (b e) -> p b e", e=8)
            P.act(eA[:, 0:nb, :], pA[:, :, 0:4], AF.Exp, [bankb[bA]], [dec_b])
            P.copy("dve", acs[:, 0:nb, :], pA[:, :, 0:4], [bankb[bA]], pwrites=[dec_b])
            P.act(cd[:, 0:nb, :], pA[:, :, 4:8], AF.Exp, [bankb[bA]], pwrites=[dec_b])
            P.tt("dve", dtmp[:, 0:nb, :], pA[:, :, 4:8], acs[:, 0:nb, :], ALU.subtract, [bankb[bA], dec_b], pwrites=[dec_b])
            P.act(dend[:, 0:nb, :], dtmp[:, 0:nb, :], AF.Exp, [dec_b], pwrites=[dec_b])
            GTs, Xds, Xdds, xsks = [], [], [], []
            for bi in range(nb):
                cols = slice(bi * 128, (bi + 1) * 128)
                dAb, dAb_b = dAbc_r.next()
                P.copy("pool", dAb[:], bc3(dA[:, bi, :], 128), [dt_b], [dAb_b])
                bs = nbk()
                P.mm(banks[bs][:], C["ntri32"], dAb[:].rearrange("p h l -> p (h l)"), True, False,
                     [dAb_b, C["b32"]], [bankb[bs]])
                P.mm(banks[bs][:], C["ident32"], nm4[:].rearrange("p h l -> p (h l)"), False, False,
                     [nm4_b, C["b32"]], [bankb[bs]])
                for h in range(4):
                    P.mm(banks[bs][:, h * 128:(h + 1) * 128], dAb[:, h, :], C["tri32"], False, h == 3,
                         [dAb_b, C["b32"]], [bankb[bs]])
                Ld, Ld_b = Ld_r.next()
                P.act(Ld[:].rearrange("p h l -> p (h l)"), banks[bs][:], AF.Exp, [bankb[bs]], [Ld_b])
                bcb = nbk()
                P.mm(banks[bcb][:, 0:128], xcT[:, 2, cols], xcT[:, 3, cols], True, True, [xcT_b], [bankb[bcb]])
                GT, GT_b = GT_r.next()
                P.tt("dve", GT[:], Ld[:], banks[bcb][:, 0:128].unsqueeze(1).to_broadcast([128, 4, 128]), ALU.mult,
                     [Ld_b, bankb[bcb]], [GT_b])
                GTs.append((GT, GT_b))
                Xd, Xd_b = Xd_r.next()
                P.tt("pool", Xd[:].rearrange("p (h q) -> p h q", h=4), xtok[:, bi, :].rearrange("p (h q) -> p h q", h=4),
                     bc3(dt[:, bi, :], 64), ALU.mult, [tok_b, dt_b], [Xd_b])
                Xds.append((Xd, Xd_b))
            Sb = [Sprev]
            for bi in range(nb):
                Xd, Xd_b = Xds[bi]
                Xdd, Xdd_b = Xdd_r.next()
                P.tt("pool", Xdd[:].rearrange("p (h q) -> p h q", h=4), Xd[:].rearrange("p (h q) -> p h q", h=4),
                     bc3(dend[:, bi, :], 64), ALU.mult, [Xd_b, dec_b], [Xdd_b])
                bst = nbk()
                P.mm(banks[bst][:, 0:256], btok[:, bi, :], Xdd[:], True, True, [tok_b, Xdd_b], [bankb[bst]])
                P.tt("dve", S[:].rearrange("p (h q) -> p h q", h=4), S[:].rearrange("p (h q) -> p h q", h=4),
                     bc3(cd[:, bi, :], 64), ALU.mult, [S_b, dec_b], pwrites=[S_b])
                P.tt("dve", S[:], S[:], banks[bst][:, 0:256], ALU.add, [S_b, bankb[bst]], pwrites=[S_b])
                Sn = Sbf_r.next()
                P.copy("pool", Sn[0][:], S[:], [S_b], [Sn[1]])
                Sb.append(Sn)
            Sprev = Sb[nb]
            for bi in range(nb):
                cols = slice(bi * 128, (bi + 1) * 128)
                GT, GT_b = GTs[bi]
                Xd, Xd_b = Xds[bi]
                by = nbk()
                for h in range(4):
                    P.mm(banks[by][:, h * 64:(h + 1) * 64], GT[:, h, :], Xd[:, h * 64:(h + 1) * 64], True, True,
                         [GT_b, Xd_b], [bankb[by]])
                P.mm(banks[by][:, 256:512], xcT[:, 3, cols], Sb[bi][0][:], True, True, [xcT_b, Sb[bi][1]], [bankb[by]])
                xsk, xsk_b = xsk_r.next()
                P.tt("pool", xsk[:].rearrange("p (h q) -> p h q", h=4), xtok[:, bi, :].rearrange("p (h q) -> p h q", h=4),
                     bc3(dsk[:], 64), ALU.mult, [tok_b, sm_b], [xsk_b])
                yt, yt_b = yt_r.next()
                y, y_b = y_r.next()
                P.tt("dve", yt[:].rearrange("p (h q) -> p h q", h=4), banks[by][:, 256:512].rearrange("p (h q) -> p h q", h=4),
                     bc3(eA[:, bi, :], 64), ALU.mult, [bankb[by], dec_b], [yt_b])
                P.tt("dve", y[:], banks[by][:, 0:256], yt[:], ALU.add, [bankb[by], yt_b], [y_b])
                P.tt("pool", y[:], y[:], xsk[:], ALU.add, [y_b, xsk_b], pwrites=[y_b])
                bt = nbk()
                for c in range(2):
                    P.tr(banks[bt][:, c * 128:(c + 1) * 128], y[:, c * 128:(c + 1) * 128], C["ident32"],
                         [y_b, C["b32"]], [bankb[bt]])
                P.tt("dve", o[:, :, cols], banks[bt][:, 0:256].rearrange("p (c t) -> p c t", c=2), sz[:, :, cols], ALU.mult,
                     [bankb[bt], sz_b], writes=[o_b] if bi == 0 else (), pwrites=() if bi == 0 else [o_b])
            out_store(P, 1, o[:, 0, :], t0, n, o_b)
            out_store(P, 2, o[:, 1, :], t0, n, o_b)


def m_weight_tensors(P, parts, lead=None):
    def T(name, shape):
        return P.dram(name, ([lead] if lead else []) + shape, F32, "ExternalInput")
    W = {}
    if "sb" in parts:
        W["w_sb"] = T("w_sb", [D, 384])
    if "mla" in parts:
        W.update({"w_mla_in": T("w_mla_in", [D, 832]), "w_uqA": T("w_uqA", [384, 192]), "w_uqB": T("w_uqB", [384, 192]),
                  "w_ukvk": T("w_ukvk", [256, 128]), "w_ukvv": T("w_ukvv", [256, 128]), "g_q": T("g_q", [128, 3]),
                  "g_kv": T("g_kv", [128, 2])})
        W["rope"] = P.dram("rope", [2, 32, L], F32, "ExternalInput")
    if "ssd" in parts:
        W.update({"w_ssm": T("w_ssm", [D, 768]), "w_dt": T("w_dt", [D, 4]), "conv_w": T("conv_w4", [128, 4, 4]),
                  "conv_b": T("conv_b4", [128, 4]), "dt_bias": T("dt_bias4", [128, 4]), "a_log": T("a_log4", [128, 4]),
                  "d_skip": T("d_skip4", [128, 4])})
    return W


def emit_M(P, C, banks, bankb, hn_rot, W, hn_load, out_store, parts=("sb", "mla", "ssd")):
    import os
    if "sb" in parts:
        emit_sb(P, C, banks, bankb, hn_load, W["w_sb"], out_store, hn_rot,
                bk=SB_BANKS_SHARED if os.environ.get("SB_SHARED") else SB_BANKS_FULL)
    if "mla" in parts:
        emit_mla(P, C, banks, bankb, hn_load, W, out_store, hn_rot)
    if "ssd" in parts:
        emit_ssd(P, C, banks, bankb, hn_load, W, out_store, hn_rot)


def build_M(parts=("sb", "mla", "ssd")):
    nc = bass.Bass("TRN2", target_bir_lowering=False)
    P = Prog(nc)
    cdram = P.dram("consts", [128, 1024], F32, "ExternalInput")
    hnT_d = P.dram("hnT", [8, 128, L], BF16, "ExternalInput")
    out_d = P.dram("omixT_out", [4, 128, L], BF16, "ExternalOutput")
    out_b = [P.buf(f"omix_out{i}") for i in range(4)]
    P.out_bufs += out_b
    C = setup_consts(P, cdram)
    banks, dbl = alloc_banks(P)
    C_DBL[0] = dbl
    bankb = [P.buf(f"bank{i}", excl=True) for i in range(8)]
    hn_rot = Rot(P, "hn", [128, 8, 512], BF16, 2)
    W = m_weight_tensors(P, parts)

    def hn_load(P, hn, hn_b, g):
        b0, nb = QG[g]
        P.dma("sp", hn[:, :, 0:nb * 128], hnT_d[:, :, b0 * 128:(b0 + nb) * 128].rearrange("k p t -> p k t"), writes=[hn_b])

    def out_store(P, cc, o, t0, n, o_b):
        P.dma("sp", out_d[cc][:, t0:t0 + n], o[:, 0:n], reads=[o_b], pwrites=[out_b[cc]])

    import os
    if os.environ.get("M_INTER"):
        rot_b = Rot(P, "hnb", [128, 8, 512], BF16, 2)
        DA, DB = Deferred(P), Deferred(P)
        emit_sb(DA, C, banks, bankb, hn_load, W["w_sb"], out_store, hn_rot, scoped=False, bk=SB_BANKS_SHARED)
        emit_ssd(DB, C, banks, bankb, hn_load, W, out_store, rot_b, scoped=False, bank_list=(5, 6, 7))
        print("ops", len(DA.ops), len(DB.ops))
        replay_interleaved(DA, DB, P, int(os.environ.get('M_K', '14')))
        P.barrier()
    else:
        emit_M(P, C, banks, bankb, hn_rot, W, hn_load, out_store, parts)
    P.finish()
    return nc


O_SBQ, O_SBK, O_SBV, O_Z, O_X, O_B, O_C, O_DT, O_CQ, O_CKV, O_KR = 0, 512, 1024, 1536, 2560, 3584, 3840, 4096, 4112, 4496, 4752


def m_inputs(d, i, j):
    w_in = d["w_in"][i]
    m = {"consts": host_consts()}
    m["w_sb"] = np.ascontiguousarray(np.concatenate(
        [w_in[:, O_SBQ + 128 * j:O_SBQ + 128 * j + 128], w_in[:, O_SBK + 128 * j:O_SBK + 128 * j + 128],
         w_in[:, O_SBV + 128 * j:O_SBV + 128 * j + 128]], axis=1))
    z64 = np.zeros((D, 64), np.float32)
    kr = w_in[:, O_KR:O_KR + 32]
    krs = np.concatenate([kr[:, 16:32], kr[:, 0:16]], axis=1)
    m["w_mla_in"] = np.ascontiguousarray(np.concatenate(
        [w_in[:, O_CQ:O_CQ + 384], w_in[:, O_CKV:O_CKV + 256], z64, kr, z64, krs], axis=1))
    wuq = d["w_uq"][i].reshape(384, 8, 96)
    wukv = d["w_ukv"][i].reshape(256, 8, 128)
    A, B = [], []
    for h in (2 * j, 2 * j + 1):
        A.append(wuq[:, h, :])
        r = wuq[:, h, 64:96]
        B.append(np.concatenate([np.zeros((384, 64), np.float32), r[:, 16:32], r[:, 0:16]], axis=1))
    m["w_uqA"] = np.ascontiguousarray(np.concatenate(A, axis=1))
    m["w_uqB"] = np.ascontiguousarray(np.concatenate(B, axis=1))
    m["w_ukvk"] = np.ascontiguousarray(np.concatenate([wukv[:, 2 * j, 0:64], wukv[:, 2 * j + 1, 0:64]], axis=1))
    m["w_ukvv"] = np.ascontiguousarray(np.concatenate([wukv[:, 2 * j, 64:128], wukv[:, 2 * j + 1, 64:128]], axis=1))
    m["g_q"] = np.ascontiguousarray(d["q_norm_g"][i].reshape(3, 128).T)
    m["g_kv"] = np.ascontiguousarray(d["kv_norm_g"][i].reshape(2, 128).T)
    m["rope"] = rope_tables_host()
    grp = j // 2
    m["w_ssm"] = np.ascontiguousarray(np.concatenate(
        [w_in[:, O_Z + 256 * j:O_Z + 256 * j + 256], w_in[:, O_X + 256 * j:O_X + 256 * j + 256],
         w_in[:, O_B + 128 * grp:O_B + 128 * grp + 128], w_in[:, O_C + 128 * grp:O_C + 128 * grp + 128]], axis=1))
    m["w_dt"] = np.ascontiguousarray(w_in[:, O_DT + 4 * j:O_DT + 4 * j + 4])
    ch = np.concatenate([np.arange(256 * j, 256 * j + 256), 1024 + 128 * grp + np.arange(128),
                         1280 + 128 * grp + np.arange(128)])
    m["conv_w4"] = np.ascontiguousarray(d["conv_w"][i][:, ch].reshape(4, 4, 128).transpose(2, 1, 0))
    m["conv_b4"] = np.ascontiguousarray(d["conv_b"][i][ch].reshape(4, 128).T)
    bc4 = lambda v: np.ascontiguousarray(np.broadcast_to(v[4 * j:4 * j + 4], (128, 4)))
    m["dt_bias4"] = bc4(d["dt_bias"][i])
    m["a_log4"] = bc4(d["a_log"][i])
    m["d_skip4"] = bc4(d["d_skip"][i])
    return m


_ROPE = []


def rope_tables_host():
    if not _ROPE:
        pos = (np.arange(L) - 112).astype(np.float32)
        inv = (10000.0 ** (-np.arange(0, 32, 2, dtype=np.float32) / 32)).astype(np.float32)
        ang = (pos[None, :] * inv[:, None]).astype(np.float32)
        c, s_ = np.cos(ang).astype(np.float32), np.sin(ang).astype(np.float32)
        _ROPE.append(np.ascontiguousarray(np.stack([np.concatenate([c, c], 0), np.concatenate([-s_, s_], 0)], 0)))
    return _ROPE[0]


def M_KEYS(parts):
    keys = {"consts", "hnT"}
    if "sb" in parts:
        keys |= {"w_sb"}
    if "mla" in parts:
        keys |= {"w_mla_in", "w_uqA", "w_uqB", "w_ukvk", "w_ukvv", "g_q", "g_kv", "rope"}
    if "ssd" in parts:
        keys |= {"w_ssm", "w_dt", "conv_w4", "conv_b4", "dt_bias4", "a_log4", "d_skip4"}
    return keys


INTER_K = 14


def build_fused(depth=DEPTH):
    nc = bass.Bass("TRN2", target_bir_lowering=False)
    P = Prog(nc)
    cdram = P.dram("consts", [128, 1024], F32, "ExternalInput")
    sel_d = P.dram("sel", [128, 4], F32, "ExternalInput")
    h_in = P.dram("h_in", [TB, 128, D], F32, "ExternalInput")
    gnext_d = P.dram("g_next", [depth + 1, 128, D], F32, "ExternalInput")
    Wf = {"w_out": P.dram("w_out", [depth, 2048, D], F32, "ExternalInput"),
          "w_gate": P.dram("w_gate", [depth, D, DFF], F32, "ExternalInput"),
          "w_up": P.dram("w_up", [depth, D, DFF], F32, "ExternalInput"),
          "w_down": P.dram("w_down", [depth, DFF, D], F32, "ExternalInput"),
          "g_ssm": P.dram("g_ssm", [depth, 128, 8], F32, "ExternalInput"),
          "g_ffn": P.dram("g_ffn", [depth, 128, D], F32, "ExternalInput")}
    Wm = m_weight_tensors(P, ("sb", "mla", "ssd"), lead=depth)
    y_d = P.dram("y_out", [TB, 128, D], F32, "ExternalOutput")
    y_b = P.buf("y_out")
    P.out_bufs.append(y_b)

    def scr(name, shape, dtype):
        t = nc.dram_tensor(name, list(shape), dtype)
        return t.ap(), P.buf(name)

    h_scr, h_scr_b = scr("h_scr", [TB, 128, D], F32)
    hn0, hn0_b = scr("hn_blk0", [8, 128, 128], BF16)
    hn_in = [[scr(f"hn_in{l}_{q}", [256, 2048], BF16) for q in range(4)] for l in range(depth)]
    hn_all = [[scr(f"hn_all{l}_{q}", [1024, 2048], BF16) for q in range(4)] for l in range(depth)]
    pcw = [128, 4096, 4096]
    om_in = [[[scr(f"om_in{l}_{c}_{p}", [128, pcw[p]], BF16) for p in range(3)] for c in range(4)] for l in range(depth)]
    om_all = [[[scr(f"om_all{l}_{c}_{p}", [512, pcw[p]], BF16) for p in range(3)] for c in range(4)] for l in range(depth)]

    C = setup_consts(P, cdram)
    selt = P.sb("sel", [128, 4], F32)
    sel_b = P.buf("sel")
    P.dma("sp", selt[:], sel_d, writes=[sel_b])
    banks, dbl = alloc_banks(P)
    C_DBL[0] = dbl
    bankb = [P.buf(f"bank{i}", excl=True) for i in range(8)]

    def make_hn_store(l):
        def hn_store(xT, xT_b):
            P.dma("sp", hn0.rearrange("k p t -> p k t"), xT[:, :, 0:128], reads=[xT_b[0]], writes=[hn0_b])
            for q in range(4):
                ap, b = hn_in[l][q]
                P.dma("sp", ap.rearrange("(k p) t -> p k t", p=128), xT[:, 2 * q:2 * q + 2, 128:TT], reads=xT_b[1:], writes=[b])
            for q in range(4):
                P.allgather(hn_in[l][q][0], hn_all[l][q][0], [hn_in[l][q][1]], [hn_all[l][q][1]])
        return hn_store

    def make_hn_load(l):
        def hn_load(P, hn, hn_b, g):
            b0, nb = QG[g]
            blk = b0
            first = True
            while blk < b0 + nb:
                off = (blk - b0) * 128
                if blk == 0:
                    P.dma("sp", hn[:, :, off:off + 128], hn0.rearrange("k p t -> p k t"), reads=[hn0_b],
                          writes=[hn_b] if first else (), pwrites=() if first else [hn_b])
                    first = False
                    blk += 1
                    continue
                r, c0 = divmod(blk - 1, 16)
                ln = min(b0 + nb - blk, 16 - c0)
                for q in range(4):
                    ap, b = hn_all[l][q]
                    P.dma("sp", hn[:, 2 * q:2 * q + 2, off:off + ln * 128],
                          ap[r * 256:(r + 1) * 256, c0 * 128:(c0 + ln) * 128].rearrange("(k p) t -> p k t", p=128),
                          reads=[b], writes=[hn_b] if first else (), pwrites=() if first else [hn_b])
                    first = False
                blk += ln
        return hn_load

    def make_out_store(l):
        def out_store(P, cc, o, t0, n, o_b):
            t = t0
            while t < t0 + n:
                if t < 128:
                    pc, col, ln = 0, t, min(t0 + n, 128) - t
                else:
                    off = t - 128
                    pc, col = 1 + off // 4096, off % 4096
                    ln = min(t0 + n - t, 4096 - col)
                ap, b = om_in[l][cc][pc]
                P.dma("sp", ap[:, col:col + ln], o[:, t - t0:t - t0 + ln], reads=[o_b], pwrites=[b])
                t += ln
        return out_store

    def make_om_load(l):
        def om_extra():
            return Rot(P, "omst", [128, 16, 512], BF16, 2)

        def om_load(o, ob, b0, nb, stage):
            n = nb * 128
            lb0 = max(b0, 1)
            dc0 = (lb0 - b0) * 128
            ln = n - dc0
            first = True
            if ln > 0:
                for jc in range(4):
                    st, st_b = stage.next()
                    off = 2048 * jc + (lb0 - 1) * 128
                    pc, col = 1 + off // 4096, off % 4096
                    for cc in range(4):
                        ap, b = om_all[l][cc][pc]
                        P.dma("sp", st[:, cc:16:4, dc0:n], ap[:, col:col + ln].rearrange("(r p) t -> p r t", p=128),
                              reads=[b], writes=[st_b] if cc == 0 else (), pwrites=() if cc == 0 else [st_b])
                    if jc == 0:
                        P.ts("dve", o[:, :, dc0:n], st[:, :, dc0:n], selt[:, 0:1], ALU.mult, [st_b, sel_b], writes=[ob])
                    else:
                        P.stt(o[:, :, dc0:n], st[:, :, dc0:n], selt[:, jc:jc + 1], o[:, :, dc0:n], ALU.mult, ALU.add,
                              [st_b, sel_b, ob], pwrites=[ob])
                first = False
            if b0 == 0:
                for cc in range(4):
                    ap, b = om_all[l][cc][0]
                    P.dma("sp", o[:, cc:16:4, 0:128], ap.rearrange("(r p) t -> p r t", p=128), reads=[b],
                          writes=[ob] if first else (), pwrites=() if first else [ob])
                    first = False
        return om_load, om_extra

    def gather_cc(l, cc):
        for p in range(3):
            P.allgather(om_in[l][cc][p][0], om_all[l][cc][p][0], [om_in[l][cc][p][1]], [om_all[l][cc][p][1]])

    emit_F(P, C, banks, bankb, "pre", {"h_src": h_in, "g_next": gnext_d[0], "hn_store": make_hn_store(0)}, tag="p")
    for l in range(depth):
        W = {k: (v if k == "rope" else v[l]) for k, v in Wm.items()}
        hn_load, out_store = make_hn_load(l), make_out_store(l)
        with contextlib.ExitStack() as esl:
            save = P.es
            P.es = esl
            rot_a = Rot(P, "hna", [128, 8, 512], BF16, 2)
            rot_b = Rot(P, "hnb", [128, 8, 512], BF16, 2)
            DA, DB = Deferred(P), Deferred(P)
            emit_sb(DA, C, banks, bankb, hn_load, W["w_sb"], out_store, rot_a, scoped=False, bk=SB_BANKS_SHARED)
            emit_ssd(DB, C, banks, bankb, hn_load, W, out_store, rot_b, scoped=False, bank_list=(5, 6, 7))
            replay_interleaved(DA, DB, P, INTER_K)
            P.barrier()
            P.es = save
        for cc in (0, 1, 2):
            gather_cc(l, cc)
        with contextlib.ExitStack() as esl:
            save = P.es
            P.es = esl
            rot_c = Rot(P, "hnc", [128, 8, 512], BF16, 2)
            emit_mla(P, C, banks, bankb, hn_load, W, out_store, rot_c)
            P.barrier()
            P.es = save
        gather_cc(l, 3)
        last = l == depth - 1
        om_load, om_extra = make_om_load(l)
        io = {"h_src": h_in if l == 0 else h_scr, "h_src_b": () if l == 0 else [h_scr_b], "g_next": gnext_d[l + 1],
              "om_load": om_load, "om_extra": om_extra,
              "w_out": Wf["w_out"][l], "w_gate": Wf["w_gate"][l], "w_up": Wf["w_up"][l], "w_down": Wf["w_down"][l],
              "g_ssm": Wf["g_ssm"][l], "g_ffn": Wf["g_ffn"][l]}
        if last:
            io["y_d"], io["y_b"] = y_d, y_b
        else:
            io["h_dst"], io["h_dst_b"] = h_scr, h_scr_b
            io["hn_store"] = make_hn_store(l + 1)
        emit_F(P, C, banks, bankb, "last" if last else "mid", io, tag=str(l))
    P.finish()
    return nc


WOUT_PERM = np.concatenate([np.concatenate([np.arange(128 * r, 128 * r + 128), 512 + np.arange(256 * r, 256 * r + 256),
                                            1536 + np.arange(128 * r, 128 * r + 128)]) for r in range(4)])
_FUSED = []


def kernel(**inp):
    return _run(DEPTH, inp)


def _run(depth, inp):
    d = {k: np.asarray(v) for k, v in inp.items()}
    x = d["x"].astype(np.float32)
    meta = d["meta_tokens"].astype(np.float32)
    consts = host_consts()
    cores = list(range(8))
    blk0 = np.concatenate([np.zeros((112, D), np.float32), meta], 0)
    if not _FUSED:
        _FUSED.append(build_fused(depth))
    nc = _FUSED[0]
    shared = {
        "consts": consts,
        "g_next": np.ascontiguousarray(np.stack([_bc(d["mix_norm_g"][i]) for i in range(depth)] + [_bc(d["final_norm_g"])], 0)),
        "w_out": np.ascontiguousarray(d["w_out"][:depth][:, WOUT_PERM, :]),
        "w_gate": np.ascontiguousarray(d["w_gate"][:depth]), "w_up": np.ascontiguousarray(d["w_up"][:depth]),
        "w_down": np.ascontiguousarray(d["w_down"][:depth]),
        "g_ssm": np.ascontiguousarray(np.stack([d["ssm_norm_g"][i].reshape(8, 128).T for i in range(depth)], 0)),
        "g_ffn": np.ascontiguousarray(np.stack([_bc(d["ffn_norm_g"][i]) for i in range(depth)], 0)),
    }
    mw = {}
    for j in range(4):
        per = [m_inputs(d, i, j) for i in range(depth)]
        mw[j] = {k: (per[0][k] if k in ("rope",) else np.ascontiguousarray(np.stack([p[k] for p in per], 0)))
                 for k in per[0] if k != "consts"}
    in_maps = []
    for c in cores:
        b, j = divmod(c, 4)
        m = dict(shared)
        m.update(mw[j])
        sel = np.zeros((128, 4), np.float32)
        sel[:, j] = 1.0
        m["sel"] = sel
        m["h_in"] = np.ascontiguousarray(np.concatenate([blk0, x[b, 2048 * j:2048 * (j + 1)]], 0).reshape(TB, 128, D))
        in_maps.append(m)
    import os
    if os.environ.get("K_TRACE"):
        rr = run_bass_kernel_spmd(nc, in_maps, core_ids=cores, trace=True)
        print("EXEC_NS", rr.exec_time_ns)
        res = rr.results
    else:
        res = run_bass_kernel_spmd(nc, in_maps, core_ids=cores).results
    y = np.empty((2, 8192, D), np.float32)
    for c in cores:
        b, j = divmod(c, 4)
        y[b, 2048 * j:2048 * (j + 1)] = res[c]["y_out"].reshape(TT, D)[128:]
    return y


def _bc(v):
    return np.ascontiguousarray(np.broadcast_to(np.asarray(v, np.float32), (128, D)))
```
